# Optimizing a Trainium2 kernel written in Bass

```python
import math
import jax, jax.numpy as jnp
from jax import lax
import numpy as np

D_MODEL = 1024
BATCH = 2
SEQ = 16384
DEPTH = 2

HEAD_DIM = 64
GDN_HEADS = 6
GDN_CONV = 4
GDN_CHUNK = 64
GDN_WIDTH = GDN_HEADS * HEAD_DIM
NSA_HEADS = 6
NSA_KV_HEADS = 2
NSA_GROUP = NSA_HEADS // NSA_KV_HEADS
NSA_WIDTH = NSA_HEADS * HEAD_DIM
NSA_KV_WIDTH = NSA_KV_HEADS * HEAD_DIM
CMP_LEN = 32
CMP_STRIDE = 16
CMP_HIDDEN = 256
SEL_BLOCK = 64
SEL_TOPK = 16
WINDOW = 512
Q_BLOCK = 128
FORCED_SCORE = 1e4
POOL_WINDOWS = (2, 4, 8, 16)
POOL_GROUP_DIM = 64
POOL_WIDTH = 4 * POOL_GROUP_DIM
D_MIX = GDN_WIDTH + NSA_WIDTH + POOL_WIDTH
IN_SPLITS = (3 * GDN_WIDTH, GDN_WIDTH, GDN_HEADS, GDN_HEADS, NSA_WIDTH, 6 * NSA_KV_WIDTH, 3 * NSA_HEADS, POOL_WIDTH)
D_IN = 3 * GDN_WIDTH + GDN_WIDTH + GDN_HEADS + GDN_HEADS + NSA_WIDTH + 6 * NSA_KV_WIDTH + 3 * NSA_HEADS + POOL_WIDTH
D_FF = 4 * D_MODEL
EPS = 1e-6

kernel_name = "hymba_gdn_nsa_pool_hybrid"


def rms_norm(x, g):
    xf = x.astype(jnp.float32)
    return (xf * lax.rsqrt(jnp.mean(xf * xf, axis=-1, keepdims=True) + EPS) * g).astype(x.dtype)


def l2norm(x):
    xf = x.astype(jnp.float32)
    return xf * lax.rsqrt(jnp.sum(xf * xf, axis=-1, keepdims=True) + EPS)


def masked_softmax(s, mask):
    s = jnp.where(mask, s.astype(jnp.float32), -1e30)
    return jax.nn.softmax(s, axis=-1) * mask


def split_columns(proj):
    offs = []
    acc = 0
    for s in IN_SPLITS[:-1]:
        acc += s
        offs.append(acc)
    return jnp.split(proj, offs, axis=-1)


def causal_conv(x, w):
    T = x.shape[1]
    K = w.shape[0]
    xp = jnp.pad(x, ((0, 0), (K - 1, 0), (0, 0)))
    return sum(xp[:, k:k + T] * w[k] for k in range(K))


def gated_delta_rule(q, k, v, beta, g):
    B, T, H, Dk = q.shape
    Dv = v.shape[-1]
    C = GDN_CHUNK
    N = T // C
    f32 = jnp.float32

    def chunks(a):
        a = a.astype(f32).reshape((B, N, C, H) + a.shape[3:])
        return jnp.moveaxis(a, 3, 1)

    q, k, v, beta, g = map(chunks, (q, k, v, beta, g))
    q = q * Dk ** -0.5
    gc = jnp.cumsum(g, axis=-1)
    causal = jnp.tril(jnp.ones((C, C), bool))
    strict = jnp.tril(jnp.ones((C, C), bool), -1)
    decay = jnp.exp(jnp.where(causal, gc[..., :, None] - gc[..., None, :], -jnp.inf))
    k_beta = k * beta[..., None]
    L = jnp.where(strict, jnp.einsum('bhncd,bhnsd->bhncs', k_beta, k) * decay, 0.0)
    A = jnp.eye(C, dtype=f32) + L
    u = lax.linalg.triangular_solve(A, v * beta[..., None], left_side=True, lower=True, unit_diagonal=True)
    w = lax.linalg.triangular_solve(A, k_beta * jnp.exp(gc)[..., None], left_side=True, lower=True, unit_diagonal=True)
    qk = jnp.einsum('bhncd,bhnsd->bhncs', q, k) * decay
    q_dec = q * jnp.exp(gc)[..., None]
    k_dec = k * jnp.exp(gc[..., -1:] - gc)[..., None]
    g_last = jnp.exp(gc[..., -1])

    def step(S, xs):
        u_i, w_i, qk_i, qd_i, kd_i, gl_i = xs
        v_new = u_i - jnp.einsum('bhcd,bhde->bhce', w_i, S)
        o = jnp.einsum('bhcd,bhde->bhce', qd_i, S) + jnp.einsum('bhcs,bhse->bhce', qk_i, v_new)
        S = S * gl_i[..., None, None] + jnp.einsum('bhcd,bhce->bhde', kd_i, v_new)
        return S, o

    xs = tuple(jnp.moveaxis(a, 2, 0) for a in (u, w, qk, q_dec, k_dec, g_last))
    S0 = jnp.zeros((B, H, Dk, Dv), f32)
    _, o = lax.scan(step, S0, xs)
    return jnp.transpose(o, (1, 0, 3, 2, 4)).reshape(B, T, H, Dv)


def gdn_mixer(qkv, z, b, a, conv_w, a_log, dt_bias, norm_g):
    B, T, _ = qkv.shape
    qkv = jax.nn.silu(causal_conv(qkv, conv_w))
    q, k, v = jnp.split(qkv, 3, axis=-1)
    heads = lambda t: t.reshape(B, T, GDN_HEADS, HEAD_DIM)
    q, k, v = l2norm(heads(q)), l2norm(heads(k)), heads(v)
    beta = jax.nn.sigmoid(b.astype(jnp.float32))
    g = -jnp.exp(a_log.astype(jnp.float32)) * jax.nn.softplus(a.astype(jnp.float32) + dt_bias)
    o = gated_delta_rule(q, k, v, beta, g)
    o = rms_norm(o, norm_g) * jax.nn.silu(heads(z).astype(jnp.float32))
    return o.reshape(B, T, GDN_WIDTH).astype(qkv.dtype)


def compress(x, pos, w1, w2):
    B, T, H, D = x.shape
    R = CMP_LEN // CMP_STRIDE
    Nr = T // CMP_STRIDE
    r = x.reshape(B, Nr, CMP_STRIDE, H, D)
    blocks = jnp.concatenate([r[:, i:Nr - R + 1 + i] for i in range(R)], axis=2)
    blocks = blocks + pos[None, None, :, None, :]
    Nc = blocks.shape[1]
    flat = blocks.transpose(0, 1, 3, 2, 4).reshape(B, Nc, H, CMP_LEN * D)
    out = jax.nn.silu(flat @ w1) @ w2
    return out.transpose(0, 2, 1, 3)


def cmp_to_sel(p):
    R = SEL_BLOCK // CMP_STRIDE
    P = CMP_LEN // CMP_STRIDE - 1
    Ns = (p.shape[-1] + P) // R
    pp = jnp.pad(p, [(0, 0)] * (p.ndim - 1) + [(P, P)])
    return sum(lax.slice_in_dim(pp, o, o + R * (Ns - 1) + 1, stride=R, axis=-1) for o in range(R + P))


def nsa_mixer(q, kv, gate_logits, q_norm, k_norm, cmp_pos, cmp_w1, cmp_w2):
    B, T, _ = q.shape
    Hkv, G, Dh = NSA_KV_HEADS, NSA_GROUP, HEAD_DIM
    out_dtype = q.dtype
    scale = Dh ** -0.5
    q = rms_norm(q.reshape(B, T, NSA_HEADS, Dh), q_norm)
    q = q.reshape(B, T, Hkv, G, Dh).transpose(0, 2, 3, 1, 4)
    kc, vc, ks, vs, kw, vw = [t.reshape(B, T, Hkv, Dh) for t in jnp.split(kv, 6, axis=-1)]
    kc = rms_norm(compress(kc, cmp_pos[0], cmp_w1[0], cmp_w2[0]), k_norm[0])
    vc = compress(vc, cmp_pos[1], cmp_w1[1], cmp_w2[1])
    Nc = kc.shape[2]
    Ns = T // SEL_BLOCK
    K_SEL = min(SEL_TOPK, Ns)
    ks = rms_norm(ks, k_norm[1]).transpose(0, 2, 1, 3).reshape(B, Hkv, Ns, SEL_BLOCK, Dh)
    vs = vs.transpose(0, 2, 1, 3).reshape(B, Hkv, Ns, SEL_BLOCK, Dh)
    pad = ((0, 0), (0, 0), (WINDOW, 0), (0, 0))
    kw = jnp.pad(rms_norm(kw, k_norm[2]).transpose(0, 2, 1, 3), pad)
    vw = jnp.pad(vw.transpose(0, 2, 1, 3), pad)
    gates = jax.nn.sigmoid(gate_logits.astype(jnp.float32)).reshape(B, T, Hkv, G, 3).transpose(0, 2, 3, 1, 4)
    cmp_end = jnp.arange(Nc) * CMP_STRIDE + CMP_LEN - 1
    blk = jnp.arange(Ns)
    bidx = jnp.arange(B)[:, None, None, None]
    hidx = jnp.arange(Hkv)[None, :, None, None]

    def block(i):
        t0 = i * Q_BLOCK
        pos = t0 + jnp.arange(Q_BLOCK)
        qb = lax.dynamic_slice_in_dim(q, t0, Q_BLOCK, axis=3) * scale
        gb = lax.dynamic_slice_in_dim(gates, t0, Q_BLOCK, axis=3)
        s = jnp.einsum('bhgqd,bhcd->bhgqc', qb, kc)
        p_c = masked_softmax(s, cmp_end[None, :] <= pos[:, None])
        o_c = jnp.einsum('bhgqc,bhcd->bhgqd', p_c, vc)
        imp = cmp_to_sel(p_c.sum(axis=2))
        cur = pos // SEL_BLOCK
        forced = (blk[None] == 0) | (blk[None] == cur[:, None]) | (blk[None] == cur[:, None] - 1)
        valid = blk[None] * SEL_BLOCK <= pos[:, None]
        imp = jnp.where(forced, FORCED_SCORE, jnp.where(valid, imp, -1.0))
        _, idx = lax.top_k(imp, K_SEL)
        ksel = ks[bidx, hidx, idx]
        vsel = vs[bidx, hidx, idx].reshape(B, Hkv, Q_BLOCK, K_SEL * SEL_BLOCK, Dh)
        s = jnp.einsum('bhgqd,bhqksd->bhgqks', qb, ksel).reshape(B, Hkv, G, Q_BLOCK, K_SEL * SEL_BLOCK)
        key_pos = idx[..., None] * SEL_BLOCK + jnp.arange(SEL_BLOCK)
        mask_s = (key_pos <= pos[:, None, None]).reshape(B, Hkv, 1, Q_BLOCK, K_SEL * SEL_BLOCK)
        o_s = jnp.einsum('bhgqn,bhqnd->bhgqd', masked_softmax(s, mask_s), vsel)
        kwb = lax.dynamic_slice_in_dim(kw, t0, Q_BLOCK + WINDOW, axis=2)
        vwb = lax.dynamic_slice_in_dim(vw, t0, Q_BLOCK + WINDOW, axis=2)
        kpos = t0 - WINDOW + jnp.arange(Q_BLOCK + WINDOW)
        mask_w = (kpos[None] <= pos[:, None]) & (kpos[None] > pos[:, None] - WINDOW) & (kpos[None] >= 0)
        s = jnp.einsum('bhgqd,bhkd->bhgqk', qb, kwb)
        o_w = jnp.einsum('bhgqk,bhkd->bhgqd', masked_softmax(s, mask_w), vwb)
        return gb[..., 0:1] * o_c + gb[..., 1:2] * o_s + gb[..., 2:3] * o_w

    out = lax.map(block, jnp.arange(T // Q_BLOCK))
    return out.transpose(1, 0, 4, 2, 3, 5).reshape(B, T, NSA_WIDTH).astype(out_dtype)


def pool_mixer(u, pool_w, pool_scale):
    B, T, _ = u.shape
    uf = u.astype(jnp.float32)
    c = jnp.pad(jnp.cumsum(uf, axis=1), ((0, 0), (1, 0), (0, 0)))
    t1 = jnp.arange(1, T + 1, dtype=jnp.float32)
    outs = []
    for gi, w in enumerate(POOL_WINDOWS):
        sl = slice(gi * POOL_GROUP_DIM, (gi + 1) * POOL_GROUP_DIM)
        cg = c[..., sl]
        cg_lag = jnp.pad(cg, ((0, 0), (w - 1, 0), (0, 0)))[:, :T]
        mean = (cg[:, 1:] - cg_lag) / jnp.minimum(t1, float(w))[None, :, None]
        outs.append(jnp.einsum('btc,cd->btd', mean - uf[..., sl], pool_w[gi]))
    return (jnp.concatenate(outs, axis=-1) * pool_scale).astype(u.dtype)


def setup_inputs(seed: int = 0) -> dict:
    key = jax.random.key(seed)
    keys = jax.random.split(key, 20)
    nrm = lambda k, shape, s: jax.random.normal(k, shape, jnp.float32) * s
    L = DEPTH
    x = nrm(keys[0], (BATCH, SEQ, D_MODEL), 1.0)
    norm_mix = 1.0 + nrm(keys[1], (L, D_MODEL), 0.02)
    w_in = nrm(keys[2], (L, D_MODEL, D_IN), D_MODEL ** -0.5)
    conv_w = nrm(keys[3], (L, GDN_CONV, 3 * GDN_WIDTH), GDN_CONV ** -0.5)
    a_log = jnp.log(jax.random.uniform(keys[4], (L, GDN_HEADS), jnp.float32, 1.0, 16.0))
    dt = jnp.exp(jax.random.uniform(keys[5], (L, GDN_HEADS), jnp.float32, math.log(1e-3), math.log(1e-1)))
    dt_bias = dt + jnp.log(-jnp.expm1(-dt))
    gdn_norm = 1.0 + nrm(keys[6], (L, HEAD_DIM), 0.02)
    nsa_q_norm = 1.0 + nrm(keys[7], (L, HEAD_DIM), 0.02)
    nsa_k_norm = 1.0 + nrm(keys[8], (L, 3, HEAD_DIM), 0.02)
    cmp_pos = nrm(keys[9], (L, 2, CMP_LEN, HEAD_DIM), 0.1)
    cmp_w1 = nrm(keys[10], (L, 2, CMP_LEN * HEAD_DIM, CMP_HIDDEN), (CMP_LEN * HEAD_DIM) ** -0.5)
    cmp_w2 = nrm(keys[11], (L, 2, CMP_HIDDEN, HEAD_DIM), CMP_HIDDEN ** -0.5)
    pool_w = nrm(keys[12], (L, len(POOL_WINDOWS), POOL_GROUP_DIM, POOL_GROUP_DIM), POOL_GROUP_DIM ** -0.5)
    pool_scale = 1.0 + nrm(keys[13], (L, POOL_WIDTH), 0.1)
    w_out = nrm(keys[14], (L, D_MIX, D_MODEL), (2 * DEPTH * D_MIX) ** -0.5)
    norm_ffn = 1.0 + nrm(keys[15], (L, D_MODEL), 0.02)
    w_ffn1 = nrm(keys[16], (L, D_MODEL, D_FF), D_MODEL ** -0.5)
    w_ffn2 = nrm(keys[17], (L, D_FF, D_MODEL), (2 * DEPTH * D_FF) ** -0.5)
    return {"x": x, "norm_mix": norm_mix, "w_in": w_in, "conv_w": conv_w, "a_log": a_log, "dt_bias": dt_bias,
            "gdn_norm": gdn_norm, "nsa_q_norm": nsa_q_norm, "nsa_k_norm": nsa_k_norm, "cmp_pos": cmp_pos,
            "cmp_w1": cmp_w1, "cmp_w2": cmp_w2, "pool_w": pool_w, "pool_scale": pool_scale, "w_out": w_out,
            "norm_ffn": norm_ffn, "w_ffn1": w_ffn1, "w_ffn2": w_ffn2}


def reference(x, norm_mix, w_in, conv_w, a_log, dt_bias, gdn_norm, nsa_q_norm, nsa_k_norm, cmp_pos,
              cmp_w1, cmp_w2, pool_w, pool_scale, w_out, norm_ffn, w_ffn1, w_ffn2):
    for l in range(DEPTH):
        h = rms_norm(x, norm_mix[l])
        proj = h @ w_in[l]
        qkv_a, z_a, b_a, a_a, q_b, kv_b, gate_b, u_c = split_columns(proj)
        y_a = gdn_mixer(qkv_a, z_a, b_a, a_a, conv_w[l], a_log[l], dt_bias[l], gdn_norm[l])
        y_b = nsa_mixer(q_b, kv_b, gate_b, nsa_q_norm[l], nsa_k_norm[l], cmp_pos[l], cmp_w1[l], cmp_w2[l])
        y_c = pool_mixer(u_c, pool_w[l], pool_scale[l])
        y = jnp.concatenate([y_a.astype(x.dtype), y_b.astype(x.dtype), y_c.astype(x.dtype)], axis=-1)
        x = x + y @ w_out[l]
        h = rms_norm(x, norm_ffn[l])
        x = x + jnp.square(jax.nn.relu(h @ w_ffn1[l])) @ w_ffn2[l]
    return x
```

```python
import os
import numpy as np
import ml_dtypes
from contextlib import ExitStack
import concourse.bass as bass
import concourse.mybir as mybir
from concourse.bass_utils import run_bass_kernel_spmd

F32 = mybir.dt.float32
BF16 = mybir.dt.bfloat16
AF = mybir.ActivationFunctionType
ALU = mybir.AluOpType
AX = mybir.AxisListType

D = 1024
DIN = 2974
DFF = 4096
EPS = 1e-6
NEG = -30000.0

C_QKV, C_Z, C_B, C_A, C_QB, C_KV, C_GATE, C_U = 0, 1152, 1536, 1542, 1548, 1932, 2700, 2718
C_KC, C_VC, C_KS, C_VS, C_KW, C_VW = 1932, 2060, 2188, 2316, 2444, 2572


class FW:
    def __init__(self, nc, stack):
        self.nc = nc
        self.eng = {'pe': nc.tensor, 'act': nc.scalar, 'dve': nc.vector, 'pool': nc.gpsimd, 'sp': nc.sync}
        self.semh = {}
        self.cnt = {}
        for e in ['pe', 'act', 'dve', 'pool']:
            self.semh[e] = stack.enter_context(nc.semaphore('s_' + e))
            self.cnt[e] = 0
        self.NDS = 6
        self.dq = {}
        for q in ['sp', 'pool']:
            keys = []
            for i in range(self.NDS):
                k = 'd_%s%d' % (q, i)
                self.semh[k] = stack.enter_context(nc.semaphore(k))
                keys.append(k)
            self.dq[q] = {'keys': keys, 'n': 0}
        self.known = {e: {} for e in self.eng}
        self.lastw = {}
        self.readers = {}
        self.ninst = 0

    def _deps(self, R, W):
        deps = {}

        def add(k, v):
            if deps.get(k, 0) < v:
                deps[k] = v
        for r in R:
            t = self.lastw.get(r)
            if t is not None:
                add(*t)
        for w in W:
            t = self.lastw.get(w)
            if t is not None:
                add(*t)
            rd = self.readers.get(w)
            if rd:
                for k, v in rd.items():
                    add(k, v)
        return deps

    def _wait(self, e, deps):
        kn = self.known[e]
        for k, v in deps.items():
            if k == e and e == 'pe':
                continue
            if kn.get(k, 0) >= v:
                continue
            self.eng[e].wait_ge(self.semh[k], v)
            kn[k] = v
            self.ninst += 1

    def _commit(self, tok, R, W):
        k, v = tok
        for r in R:
            rd = self.readers.setdefault(r, {})
            if rd.get(k, 0) < v:
                rd[k] = v
        for w in W:
            self.lastw[w] = tok
            self.readers[w] = {}

    def op(self, e, R, W, fn):
        self._wait(e, self._deps(R, W))
        inst = fn(self.eng[e])
        self.cnt[e] += 1
        inst.then_inc(self.semh[e], 1)
        self._commit((e, self.cnt[e]), R, W)
        self.ninst += 1
        return inst

    def dma(self, q, out, in_, R, W):
        dq = self.dq[q]
        j = dq['n']
        s = j % self.NDS
        key = dq['keys'][s]
        deps = self._deps(R, W)
        if j >= self.NDS:
            v = 16 * (j // self.NDS)
            if deps.get(key, 0) < v:
                deps[key] = v
        self._wait(q, deps)
        inst = self.eng[q].dma_start(out=out, in_=in_)
        inst.then_inc(self.semh[key], 16)
        dq['n'] += 1
        self._commit((key, 16 * (j // self.NDS + 1)), R, W)
        self.ninst += 1

    def barrier(self):
        cur = {}
        for q, dq in self.dq.items():
            for si, key in enumerate(dq['keys']):
                n = (dq['n'] - si + self.NDS - 1) // self.NDS if dq['n'] > si else 0
                if n > 0:
                    cur[key] = 16 * n
        for e in ['pe', 'act', 'dve', 'pool']:
            if self.cnt[e] > 0:
                cur[e] = self.cnt[e]
        for e in self.eng:
            self._wait(e, {k: v for k, v in cur.items() if k != e})

    def finish(self):
        deps = {}
        for q, dq in self.dq.items():
            for s, key in enumerate(dq['keys']):
                n = (dq['n'] - s + self.NDS - 1) // self.NDS if dq['n'] > s else 0
                if n > 0:
                    deps[key] = 16 * n
        for e in ['pe', 'act', 'dve', 'pool']:
            if self.cnt[e] > 0:
                deps[e] = self.cnt[e]
        self._wait('sp', deps)


class Ctx:
    pass


def _consts():
    c = {}
    c['identb'] = np.eye(128).astype(ml_dtypes.bfloat16)
    c['identf'] = np.eye(128).astype(np.float32)
    ob = np.zeros((128, 128), np.float32)
    ob[:64, :64] = 1.0
    ob[64:, 64:] = 1.0
    c['onesblk'] = ob
    i = np.arange(64)
    c['triU'] = (i[:, None] <= i[None, :]).astype(np.float32)
    c['biasL'] = np.where(i[None, :] < i[:, None], 0.0, NEG).astype(np.float32)
    c['biasU'] = np.where(i[:, None] <= i[None, :], 0.0, NEG).astype(np.float32)
    t1 = np.arange(1, 513, dtype=np.float32)
    c['invcnt'] = np.stack([1.0 / np.minimum(t1, float(w)) for w in (2, 4, 8, 16)]).astype(np.float32)
    return c


def phase_p1(K, l):
    nc, fw, T = K.nc, K.fw, K.T
    NM = T // 512
    with ExitStack() as st:
        sb = lambda name, shape, dt: st.enter_context(nc.sbuf_tensor(name + "_L%d" % l, shape, dt))
        ps = lambda name, shape, dt: st.enter_context(nc.psum_tensor(name + "_L%d" % l, shape, dt))
        Wb = sb("p1_Wb", [128, 8, DIN], BF16)
        wst = [sb("p1_wst%d" % i, [128, DIN // 2], F32) for i in range(2)]
        gmix = sb("p1_gmix", [128, 8], F32)
        identb = sb("p1_identb", [128, 128], BF16)
        onesblk = sb("p1_onesblk", [128, 128], F32)
        cw = sb("p1_cw", [128, 9, 4], F32)
        qg = sb("p1_qg", [128, 4], F32)
        dtb = sb("p1_dtb", [128, 6], F32)
        nA = sb("p1_nA", [128, 6], F32)
        pscale = sb("p1_pscale", [128, 256], F32)
        poolw = sb("p1_poolw", [64, 4, 64], F32)
        poolwb = sb("p1_poolwb", [64, 4, 64], BF16)
        invc = sb("p1_invc", [64, 4, 512], F32)
        xt = [sb("p1_xt%d" % i, [128, 4, D], F32) for i in range(1)]
        junk = sb("p1_junk", [128, D], BF16)
        ss = sb("p1_ss", [128, 4], F32)
        rstd = sb("p1_rstd", [128, 4], F32)
        xb = sb("p1_xb", [128, 4, D], BF16)
        xT = [sb("p1_xT%d" % i, [128, 8, 512], BF16) for i in range(2)]
        cst = [sb("p1_cst%d" % i, [128, 515], F32) for i in range(9)]
        acc = [sb("p1_acc%d" % i, [128, 512], F32) for i in range(3)]
        sl = [sb("p1_sl%d" % i, [128, 512], F32) for i in range(3)]
        sq = [sb("p1_sq%d" % i, [128, 512], F32) for i in range(3)]
        rn = [sb("p1_rn%d" % i, [128, 512], F32) for i in range(3)]
        of = [sb("p1_of%d" % i, [128, 512], F32) for i in range(2)]
        ob = [sb("p1_ob%d" % i, [128, 512], BF16) for i in range(2)]
        ust = [sb("p1_ust%d" % i, [64, 527], F32) for i in range(4)]
        s2 = sb("p1_s2", [64, 526], F32)
        s4 = sb("p1_s4", [64, 524], F32)
        s8 = sb("p1_s8", [64, 520], F32)
        s16 = sb("p1_s16", [64, 512], F32)
        dfb = [sb("p1_dfb%d" % i, [64, 512], BF16) for i in range(4)]
        ycs = sb("p1_ycs", [128, 4, 256], F32)
        zst = sb("p1_zst", [128, 4, 384], F32)
        bgs = sb("p1_bgs", [128, 4, 12], F32)
        tmpa = sb("p1_tmpa", [128, 6], F32)
        gts = sb("p1_gts", [128, 4, 18], F32)
        v1s = [sb("p1_v1s%d" % i, [128, 4, 130], BF16) for i in range(2)]
        v1w = [sb("p1_v1w%d" % i, [128, 4, 130], BF16) for i in range(2)]
        pT = [ps("p1_pT%d" % i, [128, 512], BF16) for i in range(2)]
        pF = [ps("p1_pF%d" % i, [128, 512], F32) for i in range(3)]
        pS = [ps("p1_pS%d" % i, [128, 512], F32) for i in range(3)]

        fw.dma('sp', identb[:], K.c['identb'], [], ['identb'])
        fw.dma('sp', onesblk[:], K.c['onesblk'], [], ['onesblk'])
        fw.dma('sp', gmix[:], K.w['norm_mix'][l].rearrange("(k p) -> p k", p=128), [], ['gmix'])
        for k in range(4):
            fw.dma('sp', cw[:, :, k], K.w['conv_w'][l, k].rearrange("(c p) -> p c", p=128), [], ['cw'])
        for hh in range(2):
            fw.dma('sp', qg[hh * 64:(hh + 1) * 64, 0:1], K.w['nsa_q_norm'][l].rearrange("(p o) -> p o", o=1), [], ['qg'])
            for j in range(3):
                fw.dma('sp', qg[hh * 64:(hh + 1) * 64, 1 + j:2 + j], K.w['nsa_k_norm'][l, j].rearrange("(p o) -> p o", o=1), [], ['qg'])
        fw.op('dve', ['qg'], ['qg'], lambda e: e.tensor_scalar(out=qg[:, 0:1], in0=qg[:, 0:1], scalar1=0.125, scalar2=None, op0=ALU.mult))
        fw.dma('sp', dtb[:], K.w['dt_bias'][l:l + 1, :].partition_broadcast(128), [], ['dtb'])
        fw.dma('sp', nA[:], K.w['a_log'][l:l + 1, :].partition_broadcast(128), [], ['nA'])
        fw.op('act', ['nA'], ['nA'], lambda e: e.activation(out=nA[:], in_=nA[:], func=AF.Exp))
        fw.op('dve', ['nA'], ['nA'], lambda e: e.tensor_scalar(out=nA[:], in0=nA[:], scalar1=-1.0, scalar2=None, op0=ALU.mult))
        fw.dma('sp', pscale[:], K.w['pool_scale'][l:l + 1, :].partition_broadcast(128), [], ['pscale'])
        fw.dma('sp', poolw[:], K.w['pool_w'][l].rearrange("g c d -> c g d"), [], ['poolw'])
        fw.op('dve', ['poolw'], ['poolwb'], lambda e: e.tensor_copy(out=poolwb[:], in_=poolw[:]))
        fw.dma('sp', invc[:], K.c['invcnt'].partition_broadcast(64), [], ['invc'])
        HW = DIN // 2
        for kc in range(8):
            for hf in range(2):
                w_ = wst[hf]
                fw.dma('sp' if hf else 'pool', w_[:], K.w['w_in'][l, kc * 128:(kc + 1) * 128, hf * HW:(hf + 1) * HW], [], ['wst%d' % hf])
                fw.op('dve' if hf else 'pool', ['wst%d' % hf, 'gmix'], ['Wb'],
                      lambda e: e.tensor_scalar(out=Wb[:, kc, hf * HW:(hf + 1) * HW], in0=w_[:], scalar1=gmix[:, kc:kc + 1], scalar2=None, op0=ALU.mult))
        for i in range(9):
            fw.op('pool', [], ['cst%d' % i], lambda e: e.memset(cst[i][:, 0:3], 0.0))
        for i in range(4):
            fw.op('pool', [], ['ust%d' % i], lambda e: e.memset(ust[i][:, 0:15], 0.0))
        for i in range(2):
            fw.op('pool', [], ['v1s%d' % i], lambda e: e.memset(v1s[i][:], 1.0))
            fw.op('pool', [], ['v1w%d' % i], lambda e: e.memset(v1w[i][:], 1.0))

        xsrc = K.xin[l].rearrange("(m j p) d -> m p j d", p=128, j=4)
        cnt = [0]
        rc = {}
        freel = {}

        def rot(lst, key):
            rc[key] = rc.get(key, -1) + 1
            i = rc[key] % len(lst)
            return lst[i], '%s%d' % (key, i)

        for m in range(NM):
            par = m % 2
            x_, xk = xt[0], 'xt0'
            xT_, xTk = xT[par], 'xT%d' % par
            fw.dma('pool', x_[:], xsrc[m], [], [xk])
            fw.op('dve', [], ['ss'], lambda e: e.memset(ss[:], 0.0))
            for j in range(4):
                fw.op('act', [xk, 'ss'], ['junk', 'ss'], lambda e: e.activation(out=junk[:], in_=x_[:, j, :], func=AF.Square, accum_out=ss[:, j:j + 1]))
            fw.op('dve', ['ss'], ['rstd'], lambda e: e.tensor_scalar(out=rstd[:], in0=ss[:], scalar1=1.0 / D, scalar2=EPS, op0=ALU.mult, op1=ALU.add))
            fw.op('act', ['rstd'], ['rstd'], lambda e: e.activation(out=rstd[:], in_=rstd[:], func=AF.Sqrt))
            fw.op('dve', ['rstd'], ['rstd'], lambda e: e.reciprocal(out=rstd[:], in_=rstd[:]))
            for j in range(4):
                fw.op('dve' if j % 2 else 'pool', [xk, 'rstd'], ['xb%d' % j],
                      lambda e: e.tensor_scalar(out=xb[:, j, :], in0=x_[:, j, :], scalar1=rstd[:, j:j + 1], scalar2=None, op0=ALU.mult))
            for kc in range(8):
                p_, pk = pT[kc % 2], 'pT%d' % (kc % 2)
                for j in range(4):
                    fw.op('pe', ['xb%d' % j, 'identb'], [pk], lambda e: e.transpose(out=p_[:, j * 128:(j + 1) * 128], in_=xb[:, j, kc * 128:(kc + 1) * 128], identity=identb[:]))
                if kc % 2:
                    fw.op('act', [pk], [xTk], lambda e: e.copy(out=xT_[:, kc, :], in_=p_[:]))
                else:
                    fw.op('dve', [pk], [xTk], lambda e: e.tensor_copy(out=xT_[:, kc, :], in_=p_[:]))

            def acq(lst, key):
                fl = freel.setdefault(key, list(range(len(lst))))
                i = fl.pop(0)
                return lst[i], '%s%d' % (key, i)

            def rel(key, k_):
                freel[key].append(int(k_[len(key):]))

            def fm(col0, ncols):
                p_, pk = acq(pF, 'pF')
                for kc in range(8):
                    fw.op('pe', [xTk, 'Wb'], [pk], lambda e: e.matmul(p_[0:ncols, :], lhsT=Wb[:, kc, col0:col0 + ncols], rhs=xT_[:, kc, :], start=(kc == 0), stop=(kc == 7)))
                return p_, pk

            def rnorm(src_ap, srck, n, mean):
                q_, qk_ = acq(sq, 'sq')
                fw.op('pool' if srck.startswith('sl') else 'act', [srck], [qk_],
                      (lambda e: e.tensor_tensor(out=q_[:], in0=src_ap, in1=src_ap, op=ALU.mult)) if srck.startswith('sl')
                      else (lambda e: e.activation(out=q_[:], in_=src_ap, func=AF.Square)))
                yield
                s_, sk_ = acq(pS, 'pS')
                fw.op('pe', [qk_, 'onesblk'], [sk_], lambda e: e.matmul(s_[:], lhsT=onesblk[:], rhs=q_[:], start=True, stop=True))
                rel('sq', qk_)
                yield
                r_, rk_ = rot(rn, 'rn')
                fw.op('dve', [sk_], [rk_], lambda e: e.tensor_scalar(out=r_[:], in0=s_[:], scalar1=(1.0 / 64 if mean else 1.0), scalar2=EPS, op0=ALU.mult, op1=ALU.add))
                rel('pS', sk_)
                fw.op('act', [rk_], [rk_], lambda e: e.activation(out=r_[:], in_=r_[:], func=AF.Sqrt))
                fw.op('dve', [rk_], [rk_], lambda e: e.reciprocal(out=r_[:], in_=r_[:]))
                return r_, rk_

            def gdn_chunk(ch):
                p_, pk = fm(C_QKV + ch * 128, 128)
                yield
                c_, ck = cst[ch], 'cst%d' % ch
                fw.op('act', [pk], [ck], lambda e: e.copy(out=c_[:, 3:515], in_=p_[:]))
                a_, ak = rot(acc, 'acc')
                fw.op('dve', [ck, 'cw'], [ak], lambda e: e.tensor_scalar(out=a_[:], in0=c_[:, 0:512], scalar1=cw[:, ch, 0:1], scalar2=None, op0=ALU.mult))
                for k in range(1, 4):
                    fw.op('dve', [ck, 'cw', ak], [ak],
                          lambda e: e.scalar_tensor_tensor(out=a_[:], in0=c_[:, k:k + 512], scalar=cw[:, ch, k:k + 1], in1=a_[:], op0=ALU.mult, op1=ALU.add))
                fw.op('pool', [ck], [ck], lambda e: e.tensor_copy(out=c_[:, 0:3], in_=c_[:, 512:515]))
                s_, sk = acq(sl, 'sl')
                fw.op('act', [ak], [sk], lambda e: e.activation(out=s_[:], in_=a_[:], func=AF.Silu))
                rel('pF', pk)
                if ch < 6:
                    r_, rk = yield from rnorm(s_[:], sk, 128, False)
                    o_, ok = rot(of, 'of')
                    fw.op('dve', [sk, rk], [ok], lambda e: e.scalar_tensor_tensor(out=o_[:], in0=s_[:], scalar=(0.125 if ch < 3 else 1.0), in1=r_[:], op0=ALU.mult, op1=ALU.mult))
                    fw.dma('sp', K.gqk[ch * 128:(ch + 1) * 128, m * 512:(m + 1) * 512], o_[:], [ok], ['gqk'])
                else:
                    fw.dma('sp', K.gqk[ch * 128:(ch + 1) * 128, m * 512:(m + 1) * 512], s_[:], [sk], ['gqk'])
                rel('sl', sk)

            def nsa_chunk(col0, ch, gi_, dst):
                p_, pk = fm(col0 + ch * 128, 128)
                yield
                r_, rk = yield from rnorm(p_[:], pk, 128, True)
                o_, ok = rot(ob, 'ob')
                fw.op('dve', [pk, rk, 'qg'], [ok], lambda e: e.scalar_tensor_tensor(out=o_[:], in0=p_[:], scalar=qg[:, gi_:gi_ + 1], in1=r_[:], op0=ALU.mult, op1=ALU.mult))
                fw.dma('sp', dst[ch * 128:(ch + 1) * 128, m * 512:(m + 1) * 512], o_[:], [ok], ['nsa_dst'])
                rel('pF', pk)

            def raw_chunk(col0, dst):
                p_, pk = fm(col0, 128)
                yield
                o_, ok = rot(ob, 'ob')
                fw.op('act', [pk], [ok], lambda e: e.copy(out=o_[:], in_=p_[:]))
                fw.dma('sp', dst[:, m * 512:(m + 1) * 512], o_[:], [ok], ['nsa_dst'])
                rel('pF', pk)

            def pool_chunk(gi, wlen):
                p_, pk = fm(C_U + gi * 64, 64)
                yield
                u_, uk = ust[gi], 'ust%d' % gi
                fw.op('act', [pk], [uk], lambda e: e.copy(out=u_[:, 15:527], in_=p_[0:64, :]))
                rel('pF', pk)
                fw.op('dve', [uk], ['s2'], lambda e: e.tensor_tensor(out=s2[:], in0=u_[:, 1:527], in1=u_[:, 0:526], op=ALU.add))
                sw = s2
                if wlen >= 4:
                    fw.op('dve', ['s2'], ['s4'], lambda e: e.tensor_tensor(out=s4[:], in0=s2[:, 2:526], in1=s2[:, 0:524], op=ALU.add))
                    sw = s4
                if wlen >= 8:
                    fw.op('dve', ['s4'], ['s8'], lambda e: e.tensor_tensor(out=s8[:], in0=s4[:, 4:524], in1=s4[:, 0:520], op=ALU.add))
                    sw = s8
                if wlen >= 16:
                    fw.op('dve', ['s8'], ['s16'], lambda e: e.tensor_tensor(out=s16[:], in0=s8[:, 8:520], in1=s8[:, 0:512], op=ALU.add))
                    sw = s16
                nsw = {2: 526, 4: 524, 8: 520, 16: 512}[wlen]
                swk = 's%d' % wlen
                d_, dk = dfb[gi], 'dfb%d' % gi
                if m == 0:
                    fw.op('dve', [swk, 'invc'], [swk], lambda e: e.tensor_tensor(out=sw[:, nsw - 512:nsw], in0=sw[:, nsw - 512:nsw], in1=invc[:, gi, :], op=ALU.mult))
                    fw.op('dve', [swk, uk], [dk], lambda e: e.tensor_tensor(out=d_[:], in0=sw[:, nsw - 512:nsw], in1=u_[:, 15:527], op=ALU.subtract))
                else:
                    fw.op('dve', [swk, uk], [dk], lambda e: e.scalar_tensor_tensor(out=d_[:], in0=sw[:, nsw - 512:nsw], scalar=1.0 / wlen, in1=u_[:, 15:527], op0=ALU.mult, op1=ALU.subtract))
                fw.op('pool', [uk], [uk], lambda e: e.tensor_copy(out=u_[:, 0:15], in_=u_[:, 512:527]))

            def pool_out(j):
                p_, pk = acq(pS, 'pS')
                for gi in range(4):
                    fw.op('pe', ['dfb%d' % gi, 'poolwb'], [pk], lambda e: e.matmul(p_[:, gi * 64:(gi + 1) * 64], lhsT=dfb[gi][:, j * 128:(j + 1) * 128], rhs=poolwb[:, gi, :], start=True, stop=True))
                yield
                fw.op('dve', [pk, 'pscale'], ['ycs'], lambda e: e.tensor_tensor(out=ycs[:, j, :], in0=p_[:, 0:256], in1=pscale[:], op=ALU.mult))
                rel('pS', pk)
                if j == 3:
                    fw.dma('sp', K.y.rearrange("(m j p) d -> m p j d", p=128, j=4)[m][:, :, 768:1024], ycs[:], ['ycs'], ['y_c'])

            v_, vk = v1s[par], 'v1s%d' % par
            vw_, vwk = v1w[par], 'v1w%d' % par

            def tok_a(j):
                p_, pk = acq(pF, 'pF')
                for kc in range(8):
                    fw.op('pe', [xTk, 'Wb'], [pk], lambda e: e.matmul(p_[:, 0:396], lhsT=xT_[:, kc, j * 128:(j + 1) * 128], rhs=Wb[:, kc, C_Z:C_Z + 396], start=(kc == 0), stop=(kc == 7)))
                yield
                fw.op('act', [pk], ['zst'], lambda e: e.activation(out=zst[:, j, :], in_=p_[:, 0:384], func=AF.Silu))
                fw.op('act', [pk], ['bgs'], lambda e: e.activation(out=bgs[:, j, 0:6], in_=p_[:, 384:390], func=AF.Sigmoid))
                fw.op('dve', [pk, 'dtb', 'zst', 'bgs'], ['tmpa'], lambda e: e.tensor_tensor(out=tmpa[:], in0=p_[:, 390:396], in1=dtb[:], op=ALU.add))
                fw.op('act', ['tmpa'], ['tmpa'], lambda e: e.activation(out=tmpa[:], in_=tmpa[:], func=AF.Exp))
                fw.op('act', ['tmpa'], ['tmpa'], lambda e: e.activation(out=tmpa[:], in_=tmpa[:], func=AF.Ln, bias=1.0))
                fw.op('dve', ['tmpa', 'nA'], ['bgs'], lambda e: e.tensor_tensor(out=bgs[:, j, 6:12], in0=tmpa[:], in1=nA[:], op=ALU.mult))
                rel('pF', pk)

            def tok_b(j):
                p_, pk = acq(pF, 'pF')
                for (c0, o0) in ((C_VS, 0), (C_VW, 128), (C_GATE, 256)):
                    nn = 18 if c0 == C_GATE else 128
                    for kc in range(8):
                        fw.op('pe', [xTk, 'Wb'], [pk], lambda e: e.matmul(p_[:, o0:o0 + nn], lhsT=xT_[:, kc, j * 128:(j + 1) * 128], rhs=Wb[:, kc, c0:c0 + nn], start=(kc == 0), stop=(kc == 7)))
                yield
                fw.op('act', [pk], ['gts'], lambda e: e.activation(out=gts[:, j, :], in_=p_[:, 256:274], func=AF.Sigmoid))
                fw.op('dve', [pk, 'gts'], [vk], lambda e: e.tensor_copy(out=v_[:, j, :].rearrange("p (h c) -> p h c", h=2)[:, :, 0:64],
                                                                     in_=p_[:, 0:128].rearrange("p (h c) -> p h c", h=2)))
                fw.op('dve', [pk, 'gts'], [vwk], lambda e: e.tensor_copy(out=vw_[:, j, :].rearrange("p (h c) -> p h c", h=2)[:, :, 0:64],
                                                                     in_=p_[:, 128:256].rearrange("p (h c) -> p h c", h=2)))
                rel('pF', pk)

            tasks = [gdn_chunk(ch) for ch in range(9)]
            tasks += [nsa_chunk(C_QB, ch, 0, K.qn) for ch in range(3)] + [nsa_chunk(C_KS, 0, 2, K.ksT), nsa_chunk(C_KW, 0, 3, K.kwT)]
            tasks += [raw_chunk(C_KC, K.kcT), raw_chunk(C_VC, K.vcT)]
            tasks += [pool_chunk(gi, wlen) for gi, wlen in enumerate((2, 4, 8, 16))]
            tasks += [None]
            tasks += [pool_out(j) for j in range(4)]
            for j in range(4):
                tasks += [tok_a(j), tok_b(j)]
            active = []
            ti = 0
            while ti < len(tasks) or active:
                while ti < len(tasks) and len(active) < 3:
                    if tasks[ti] is None:
                        if active:
                            break
                        ti += 1
                        continue
                    active.append(tasks[ti])
                    ti += 1
                for g_ in list(active):
                    try:
                        next(g_)
                    except StopIteration:
                        active.remove(g_)
            fw.dma('sp', K.zs.rearrange("(m j p) d -> m p j d", p=128, j=4)[m], zst[:], ['zst'], ['zs'])
            fw.dma('sp', K.bg.rearrange("(m j p) d -> m p j d", p=128, j=4)[m], bgs[:], ['bgs'], ['bg'])
            fw.dma('sp', K.gt.rearrange("(m j p) d -> m p j d", p=128, j=4)[m], gts[:], ['gts'], ['gt'])
            fw.dma('sp', K.vs1.rearrange("(m j p) d -> m p j d", p=128, j=4)[m], v_[:], [vk], ['vs1'])
            fw.dma('sp', K.vw1.rearrange("(m j p) d -> m p j d", p=128, j=4)[m], vw_[:], [vwk], ['vw1'])
        fw.barrier()


WNAMES = ["norm_mix", "w_in", "conv_w", "a_log", "dt_bias", "gdn_norm", "nsa_q_norm", "nsa_k_norm", "cmp_pos",
          "cmp_w1", "cmp_w2", "pool_w", "pool_scale", "w_out", "norm_ffn", "w_ffn1", "w_ffn2"]
WSHAPES = {"norm_mix": [2, 1024], "w_in": [2, 1024, 2974], "conv_w": [2, 4, 1152], "a_log": [2, 6], "dt_bias": [2, 6],
           "gdn_norm": [2, 64], "nsa_q_norm": [2, 64], "nsa_k_norm": [2, 3, 64], "cmp_pos": [2, 2, 32, 64],
           "cmp_w1": [2, 2, 2048, 256], "cmp_w2": [2, 2, 256, 64], "pool_w": [2, 4, 64, 64], "pool_scale": [2, 256],
           "w_out": [2, 1024, 1024], "norm_ffn": [2, 1024], "w_ffn1": [2, 1024, 4096], "w_ffn2": [2, 4096, 1024]}


def build(T, L=2, phases=("p1", "gdn", "cmp", "nsa", "p3"), dbg=False):
    nc = bass.Bass("TRN2", target_bir_lowering=False)
    K = Ctx()
    K.nc, K.T, K.L = nc, T, L
    K.cut = int(os.environ.get('GDN_CUT', '99'))
    K.f32r = os.environ.get('GDN_F32R', '0') == '1'
    dk = "ExternalOutput" if dbg else "Internal"
    K.w = {n: nc.dram_tensor(n, WSHAPES[n], F32, kind="ExternalInput").ap() for n in WNAMES}
    cs = _consts()
    K.c = {n: nc.dram_tensor("c_" + n, list(v.shape), BF16 if v.dtype != np.float32 else F32, kind="ExternalInput").ap() for n, v in cs.items()}
    x = nc.dram_tensor("x", [T, D], F32, kind="ExternalInput").ap()
    out = nc.dram_tensor("out", [T, D], F32, kind="ExternalOutput").ap()
    xs1 = nc.dram_tensor("xs1", [T, D], F32, kind=dk).ap()
    K.xin = [x, xs1]
    K.xout = [xs1, out] if L == 2 else [out]
    K.gqk = nc.dram_tensor("gqk", [1152, T], F32, kind=dk).ap()
    K.zs = nc.dram_tensor("zs", [T, 384], F32, kind=dk).ap()
    K.bg = nc.dram_tensor("bg", [T, 12], F32, kind=dk).ap()
    K.gt = nc.dram_tensor("gt", [T, 18], F32, kind=dk).ap()
    K.qn = nc.dram_tensor("qn", [384, T], BF16, kind=dk).ap()
    K.ksT = nc.dram_tensor("ksT", [128, T], BF16, kind=dk).ap()
    K.kwT = nc.dram_tensor("kwT", [128, T], BF16, kind=dk).ap()
    K.kcT = nc.dram_tensor("kcT", [128, T], BF16, kind=dk).ap()
    K.vcT = nc.dram_tensor("vcT", [128, T], BF16, kind=dk).ap()
    K.vs1 = nc.dram_tensor("vs1", [T, 130], BF16, kind=dk).ap()
    K.vw1 = nc.dram_tensor("vw1", [T, 130], BF16, kind=dk).ap()
    K.y = nc.dram_tensor("y", [T, D], F32, kind=dk).ap()
    Ns_, NcP_ = T // 64, T // 16
    K.KcT = nc.dram_tensor("KcT", [2, 64, NcP_], BF16, kind=dk).ap()
    K.Vc1 = nc.dram_tensor("Vc1", [2, NcP_, 64 + Ns_ + 1], BF16, kind=dk).ap()
    csn = _nsa_consts(T)
    K.cn = {n: nc.dram_tensor("cn_" + n, list(v.shape), BF16, kind="ExternalInput").ap() for n, v in csn.items()}
    K.consts_n = csn
    with ExitStack() as st:
        st.enter_context(nc.allow_non_contiguous_dma(reason="small parameter loads"))
        K.fw = FW(nc, st)
        for l in range(L):
            if "p1" in phases:
                phase_p1(K, l)
            if "gdn" in phases:
                phase_gdn(K, l)
            if "cmp" in phases:
                phase_cmp(K, l)
            if "nsa" in phases:
                phase_nsa(K, l)
            if "p3" in phases:
                phase_p3(K, l)
        K.fw.finish()
    K.consts = cs
    return nc, K


def phase_gdn(K, l):
    nc, fw, T = K.nc, K.fw, K.T
    NCH = T // 64
    F32R = mybir.dt.float32r
    R_ = (lambda ap: ap.bitcast(F32R)) if K.f32r else (lambda ap: ap)
    with ExitStack() as st:
        sb = lambda name, shape, dt=F32: st.enter_context(nc.sbuf_tensor(name + "_L%d" % l, shape, dt))
        identf = sb("g_identf", [64, 64])
        ones64 = sb("g_ones", [64, 64])
        triU = sb("g_triU", [64, 64])
        biasL = sb("g_biasL", [64, 64])
        biasU = sb("g_biasU", [64, 64])
        gnorm = sb("g_gnorm", [64, 64])
        Fq = [sb("g_F%d" % i, [64, 18, 512]) for i in range(2)]
        Zs = [sb("g_Z%d" % i, [64, 8, 384]) for i in range(1)]
        BG = [sb("g_BG%d" % i, [64, 8, 12]) for i in range(2)]
        Yst = [sb("g_Y%d" % i, [64, 8, 384]) for i in range(1)]
        S = [sb("g_S%d" % i, [64, 6, 64]) for i in range(2)]
        Sb = [sb("g_Sb%d" % i, [64, 6, 64], BF16) for i in range(2)]
        names = ["KVt", "rhsG", "gcol", "D1", "D2", "E1", "E2", "egrow", "egcol", "kdsc", "nbeta", "bege", "N", "qkT", "vb", "kbg", "kd", "qdT",
                 "Mp0", "Mp1", "Np0", "Np1", "Tt0", "Tt1", "U", "WT", "vnew", "tmpS", "O", "sqo", "ssum", "Yt", "Gs", "Nb", "Fb"]
        BFN = ("Mp0", "Mp1", "Np0", "Np1", "Tt0", "Tt1", "vb", "kbg", "kd", "qdT", "qkT", "WT", "vnew", "Nb", "Fb")
        W2 = {}
        for nm in names:
            shape = [64, 12, 64] if nm in ("KVt", "Fb") else [64, 390] if nm == "Gs" else ([64, 6] if nm in ("gcol", "egcol", "kdsc", "nbeta", "bege", "ssum") else [64, 6, 64])
            W2[nm] = [sb("g_%s%d" % (nm, i), shape, BF16 if nm in BFN else F32) for i in range(2)]
        banks = [st.enter_context(nc.psum_tensor("g_ps%d_L%d" % (i, l), [64, 512], F32)) for i in range(8)]
        bc = [0]

        def bank():
            bc[0] += 1
            i = bc[0] % 8
            return banks[i], 'gps%d' % i

        fw.dma('sp', identf[:], K.c['identf'][0:64, 0:64], [], ['identf'])
        fw.dma('sp', triU[:], K.c['triU'], [], ['triU'])
        fw.dma('sp', biasL[:], K.c['biasL'], [], ['biasL'])
        fw.dma('sp', biasU[:], K.c['biasU'], [], ['biasU'])
        fw.dma('sp', gnorm[:], K.w['gdn_norm'][l:l + 1, :].partition_broadcast(64), [], ['gnorm'])
        fw.op('pool', [], ['ones64'], lambda e: e.memset(ones64[:], 1.0))
        fw.op('pool', [], ['S0'], lambda e: e.memset(S[0][:], 0.0))
        fw.op('pool', [], ['Sb0'], lambda e: e.memset(Sb[0][:], 0.0))

        gq = K.gqk.rearrange("(s p) t -> p s t", p=64)
        zsr = K.zs.rearrange("(n c) f -> c n f", c=64)
        bgr = K.bg.rearrange("(n c) f -> c n f", c=64)
        yr = K.y.rearrange("(n c) f -> c n f", c=64)

        def b3(ap2, n=64):
            return ap2.unsqueeze(2).to_broadcast([64, 6, n])

        def m3(ap2):
            return ap2.unsqueeze(1).to_broadcast([64, 6, 64])

        state = {}

        def load_group(gi):
            par = gi % 2
            fw.dma('sp', Fq[par][:], gq[:, :, gi * 512:(gi + 1) * 512], [], ['F%d' % par])
            fw.dma('sp', BG[par][:], bgr[:, gi * 8:(gi + 1) * 8, :], [], ['BG%d' % par])

        def prep(n):
            gi, j = n // 8, n % 8
            gp = gi % 2
            p = n % 2
            F_, Fk = Fq[gp], 'F%d' % gp
            BG_, BGk = BG[gp], 'BG%d' % gp
            t = {nm: W2[nm][p] for nm in names}
            k_ = lambda nm: '%s%d' % (nm, p)
            cs = slice(j * 64, (j + 1) * 64)
            qT = F_[:, 0:6, cs]
            kT = F_[:, 6:12, cs]
            beta = BG_[:, j, 0:6]
            g = BG_[:, j, 6:12]
            fw.op('pool', [Fk], [k_('Fb')], lambda e: e.tensor_copy(out=t['Fb'][:], in_=F_[:, 0:12, cs]))
            pk1, pk1k = bank()
            pv1, pv1k = bank()
            for h in range(6):
                fw.op('pe', [Fk, 'identf'], [pk1k], lambda e: e.transpose(out=pk1[:, h * 64:(h + 1) * 64], in_=F_[:, 6 + h, cs], identity=identf[:]))
            for h in range(6):
                fw.op('pe', [Fk, 'identf'], [pv1k], lambda e: e.transpose(out=pv1[:, h * 64:(h + 1) * 64], in_=F_[:, 12 + h, cs], identity=identf[:]))
            fw.op('act', [pk1k], [k_('KVt')], lambda e: e.copy(out=t['KVt'][:, 0:6, :], in_=pk1[:, 0:384].rearrange("p (h c) -> p h c", h=6)))
            fw.op('act', [pv1k], [k_('KVt')], lambda e: e.copy(out=t['KVt'][:, 6:12, :], in_=pv1[:, 0:384].rearrange("p (h c) -> p h c", h=6)))
            yield
            pKK, pKKk = bank()
            pQK, pQKk = bank()
            for h in range(6):
                fw.op('pe', [k_('Fb')], [pKKk], lambda e: e.matmul(pKK[:, h * 64:(h + 1) * 64], lhsT=t['Fb'][:, 6 + h, :], rhs=t['Fb'][:, 6 + h, :], start=True, stop=True))
            for h in range(6):
                fw.op('pe', [k_('Fb')], [pQKk], lambda e: e.matmul(pQK[:, h * 64:(h + 1) * 64], lhsT=t['Fb'][:, 6 + h, :], rhs=t['Fb'][:, h, :], start=True, stop=True))
            yield
            fw.op('dve', [BGk, 'triU'], [k_('rhsG')], lambda e: e.tensor_tensor(out=t['rhsG'][:], in0=b3(g), in1=m3(triU[:]), op=ALU.mult))
            pG, pGk = bank()
            fw.op('pe', [k_('rhsG'), 'ones64'], [pGk], lambda e: e.matmul(pG[:, 0:384], lhsT=R_(ones64[:]), rhs=R_(t['rhsG'][:].rearrange("p h c -> p (h c)")), start=True, stop=True))
            fw.op('pe', [BGk, 'triU'], [pGk], lambda e: e.matmul(pG[:, 384:390], lhsT=R_(triU[:]), rhs=R_(g), start=True, stop=True))
            fw.op('act', [pGk], [k_('Gs')], lambda e: e.copy(out=t['Gs'][:], in_=pG[:, 0:390]))
            pGk = k_('Gs')
            pG3 = t['Gs'][:, 0:384].rearrange("p (h c) -> p h c", h=6)
            fw.op('pool', [pGk], [k_('gcol')], lambda e: e.tensor_copy(out=t['gcol'][:], in_=t['Gs'][:, 384:390]))
            fw.op('dve', [pGk, k_('gcol')], [k_('D1')], lambda e: e.tensor_tensor(out=t['D1'][:], in0=b3(t['gcol'][:]), in1=pG3, op=ALU.subtract))
            fw.op('pool', [k_('D1'), 'biasU'], [k_('D2')], lambda e: e.tensor_tensor(out=t['D2'][:], in0=m3(biasU[:]), in1=t['D1'][:], op=ALU.subtract))
            fw.op('dve', [k_('D1'), 'biasL'], [k_('D1')], lambda e: e.tensor_tensor(out=t['D1'][:], in0=t['D1'][:], in1=m3(biasL[:]), op=ALU.add))
            fw.op('act', [k_('D1')], [k_('E1')], lambda e: e.activation(out=t['E1'][:], in_=t['D1'][:], func=AF.Exp))
            fw.op('act', [k_('D2')], [k_('E2')], lambda e: e.activation(out=t['E2'][:], in_=t['D2'][:], func=AF.Exp))
            fw.op('act', [pGk], [k_('egrow')], lambda e: e.activation(out=t['egrow'][:], in_=pG3, func=AF.Exp))
            fw.op('act', [k_('gcol')], [k_('egcol')], lambda e: e.activation(out=t['egcol'][:], in_=t['gcol'][:], func=AF.Exp))
            fw.op('dve', [pGk, k_('gcol')], [k_('kdsc')], lambda e: e.tensor_tensor(out=t['kdsc'][:], in0=pG3[:, :, 63], in1=t['gcol'][:], op=ALU.subtract))
            fw.op('act', [k_('kdsc')], [k_('kdsc')], lambda e: e.activation(out=t['kdsc'][:], in_=t['kdsc'][:], func=AF.Exp))
            fw.op('pool', [BGk], [k_('nbeta')], lambda e: e.tensor_scalar(out=t['nbeta'][:], in0=beta, scalar1=-1.0, scalar2=None, op0=ALU.mult))
            fw.op('pool', [BGk, k_('egcol')], [k_('bege')], lambda e: e.tensor_tensor(out=t['bege'][:], in0=beta, in1=t['egcol'][:], op=ALU.mult))
            yield
            fw.op('dve', [pKKk, k_('E1')], [k_('N')], lambda e: e.tensor_tensor(out=t['N'][:], in0=pKK[:, 0:384].rearrange("p (h c) -> p h c", h=6), in1=t['E1'][:], op=ALU.mult))
            fw.op('dve', [k_('N'), k_('nbeta')], [k_('N')], lambda e: e.tensor_tensor(out=t['N'][:], in0=t['N'][:], in1=b3(t['nbeta'][:]), op=ALU.mult))
            fw.op('pool', [k_('N')], [k_('Nb')], lambda e: e.tensor_copy(out=t['Nb'][:], in_=t['N'][:]))
            fw.op('dve', [pQKk, k_('E2')], [k_('qkT')], lambda e: e.tensor_tensor(out=t['qkT'][:], in0=pQK[:, 0:384].rearrange("p (h c) -> p h c", h=6), in1=t['E2'][:], op=ALU.mult))
            fw.op('pool', [k_('KVt'), BGk], [k_('vb')], lambda e: e.tensor_tensor(out=t['vb'][:], in0=t['KVt'][:, 6:12, :], in1=b3(beta), op=ALU.mult))
            fw.op('pool', [k_('KVt'), k_('bege')], [k_('kbg')], lambda e: e.tensor_tensor(out=t['kbg'][:], in0=t['KVt'][:, 0:6, :], in1=b3(t['bege'][:]), op=ALU.mult))
            fw.op('pool', [k_('KVt'), k_('kdsc')], [k_('kd')], lambda e: e.tensor_tensor(out=t['kd'][:], in0=t['KVt'][:, 0:6, :], in1=b3(t['kdsc'][:]), op=ALU.mult))
            fw.op('pool', [Fk, k_('egrow')], [k_('qdT')], lambda e: e.tensor_tensor(out=t['qdT'][:], in0=qT, in1=t['egrow'][:], op=ALU.mult))
            yield
            pM, pMk = bank()
            for h in range(6):
                fw.op('pe', [k_('N'), 'identf'], [pMk], lambda e: e.transpose(out=pM[:, h * 64:(h + 1) * 64], in_=t['N'][:, h, :], identity=identf[:]))
            pM3 = pM[:, 0:384].rearrange("p (h c) -> p h c", h=6)
            fw.op('act', [pMk], [k_('Mp0')], lambda e: e.copy(out=t['Mp0'][:], in_=pM3))
            fw.op('pool', [k_('Mp0'), 'identf'], [k_('Tt0')], lambda e: e.tensor_tensor(out=t['Tt0'][:], in0=t['Mp0'][:], in1=m3(identf[:]), op=ALU.add))
            yield
            Np, Npk = t['Nb'], k_('Nb')
            Mp, Mpk = t['Mp0'], k_('Mp0')
            Tt, Ttk = t['Tt0'], k_('Tt0')
            for s_ in range(5):
                Nn, Nnk = t['Np%d' % (s_ % 2)], k_('Np%d' % (s_ % 2))
                Mn, Mnk = t['Mp%d' % ((s_ + 1) % 2)], k_('Mp%d' % ((s_ + 1) % 2))
                Tn, Tnk = t['Tt%d' % ((s_ + 1) % 2)], k_('Tt%d' % ((s_ + 1) % 2))
                pN2, pN2k = bank()
                for h in range(6):
                    fw.op('pe', [Mpk, Npk], [pN2k], lambda e: e.matmul(pN2[:, h * 64:(h + 1) * 64], lhsT=R_(Mp[:, h, :]), rhs=R_(Np[:, h, :]), start=True, stop=True))
                if s_ < 4:
                    pM2, pM2k = bank()
                    for h in range(6):
                        fw.op('pe', [Mpk, Npk], [pM2k], lambda e: e.matmul(pM2[:, h * 64:(h + 1) * 64], lhsT=R_(Np[:, h, :]), rhs=R_(Mp[:, h, :]), start=True, stop=True))
                yield
                fw.op('act', [pN2k], [Nnk], lambda e: e.copy(out=Nn[:], in_=pN2[:, 0:384].rearrange("p (h c) -> p h c", h=6)))
                if s_ < 4:
                    fw.op('dve', [pM2k], [Mnk], lambda e: e.tensor_copy(out=Mn[:], in_=pM2[:, 0:384].rearrange("p (h c) -> p h c", h=6)))
                pT_, pTk = bank()
                for h in range(6):
                    fw.op('pe', [Nnk, Ttk], [pTk], lambda e: e.matmul(pT_[:, h * 64:(h + 1) * 64], lhsT=R_(Nn[:, h, :]), rhs=R_(Tt[:, h, :]), start=True, stop=True))
                yield
                fw.op('dve', [pTk, Ttk], [Tnk], lambda e: e.tensor_tensor(out=Tn[:], in0=pT_[:, 0:384].rearrange("p (h c) -> p h c", h=6), in1=Tt[:], op=ALU.add))
                Np, Npk, Mp, Mpk, Tt, Ttk = Nn, Nnk, Mn, Mnk, Tn, Tnk
            yield
            pU, pUk = bank()
            pW, pWk = bank()
            for h in range(6):
                fw.op('pe', [Ttk, k_('vb')], [pUk], lambda e: e.matmul(pU[:, h * 64:(h + 1) * 64], lhsT=R_(Tt[:, h, :]), rhs=R_(t['vb'][:, h, :]), start=True, stop=True))
            for h in range(6):
                fw.op('pe', [Ttk, k_('kbg')], [pWk], lambda e: e.matmul(pW[:, h * 64:(h + 1) * 64], lhsT=R_(t['kbg'][:, h, :]), rhs=R_(Tt[:, h, :]), start=True, stop=True))
            yield
            fw.op('act', [pUk], [k_('U')], lambda e: e.copy(out=t['U'][:], in_=pU[:, 0:384].rearrange("p (h c) -> p h c", h=6)))
            fw.op('dve', [pWk], [k_('WT')], lambda e: e.tensor_copy(out=t['WT'][:], in_=pW[:, 0:384].rearrange("p (h c) -> p h c", h=6)))

        def scan(n):
            gi, j = n // 8, n % 8
            gp = gi % 2
            p = n % 2
            t = {nm: W2[nm][p] for nm in names}
            k_ = lambda nm: '%s%d' % (nm, p)
            if j == 0:
                fw.dma('pool', Zs[0][:], zsr[:, gi * 8:(gi + 1) * 8, :], [], ['Z0'])
            So, Sok = S[n % 2], 'S%d' % (n % 2)
            Sn, Snk = S[(n + 1) % 2], 'S%d' % ((n + 1) % 2)
            Sob, Sobk = Sb[n % 2], 'Sb%d' % (n % 2)
            Snb, Snbk = Sb[(n + 1) % 2], 'Sb%d' % ((n + 1) % 2)
            pWS, pWSk = bank()
            for h in range(6):
                fw.op('pe', [k_('WT'), Sobk], [pWSk], lambda e: e.matmul(pWS[:, h * 64:(h + 1) * 64], lhsT=t['WT'][:, h, :], rhs=Sob[:, h, :], start=True, stop=True))
            yield
            fw.op('dve', [pWSk, k_('U')], [k_('vnew')], lambda e: e.tensor_tensor(out=t['vnew'][:], in0=t['U'][:], in1=pWS[:, 0:384].rearrange("p (h c) -> p h c", h=6), op=ALU.subtract))
            pdS, pdSk = bank()
            for h in range(6):
                fw.op('pe', [k_('kd'), k_('vnew')], [pdSk], lambda e: e.matmul(pdS[:, h * 64:(h + 1) * 64], lhsT=R_(t['kd'][:, h, :]), rhs=R_(t['vnew'][:, h, :]), start=True, stop=True))
            pO, pOk = bank()
            for h in range(6):
                fw.op('pe', [k_('qdT'), Sobk], [pOk], lambda e: e.matmul(pO[:, h * 64:(h + 1) * 64], lhsT=t['qdT'][:, h, :], rhs=Sob[:, h, :], start=True, stop=False))
                fw.op('pe', [k_('qkT'), k_('vnew')], [pOk], lambda e: e.matmul(pO[:, h * 64:(h + 1) * 64], lhsT=R_(t['qkT'][:, h, :]), rhs=R_(t['vnew'][:, h, :]), start=False, stop=True))
            yield
            fw.op('pool', [Sok, k_('egrow')], [k_('tmpS')], lambda e: e.tensor_tensor(out=t['tmpS'][:], in0=So[:], in1=t['egrow'][:, :, 63:64].to_broadcast([64, 6, 64]), op=ALU.mult))
            fw.op('dve', [k_('tmpS'), pdSk], [Snk], lambda e: e.tensor_tensor(out=Sn[:], in0=t['tmpS'][:], in1=pdS[:, 0:384].rearrange("p (h c) -> p h c", h=6), op=ALU.add))
            fw.op('pool', [Snk], [Snbk], lambda e: e.tensor_copy(out=Snb[:], in_=Sn[:]))
            yield
            fw.op('act', [pOk], [k_('O')], lambda e: e.copy(out=t['O'][:], in_=pO[:, 0:384].rearrange("p (h c) -> p h c", h=6)))
            fw.op('pool', [k_('O')], [k_('sqo')], lambda e: e.tensor_tensor(out=t['sqo'][:], in0=t['O'][:], in1=t['O'][:], op=ALU.mult))
            fw.op('dve', [k_('sqo')], [k_('ssum')], lambda e: e.tensor_reduce(out=t['ssum'][:], in_=t['sqo'][:], axis=AX.X, op=ALU.add))
            fw.op('dve', [k_('ssum')], [k_('ssum')], lambda e: e.tensor_scalar(out=t['ssum'][:], in0=t['ssum'][:], scalar1=1.0 / 64, scalar2=EPS, op0=ALU.mult, op1=ALU.add))
            fw.op('act', [k_('ssum')], [k_('ssum')], lambda e: e.activation(out=t['ssum'][:], in_=t['ssum'][:], func=AF.Sqrt))
            fw.op('dve', [k_('ssum')], [k_('ssum')], lambda e: e.reciprocal(out=t['ssum'][:], in_=t['ssum'][:]))
            fw.op('pool', [k_('O'), k_('ssum')], [k_('Yt')], lambda e: e.tensor_tensor(out=t['Yt'][:], in0=t['O'][:], in1=b3(t['ssum'][:]), op=ALU.mult))
            fw.op('pool', [k_('Yt'), 'gnorm'], [k_('Yt')], lambda e: e.tensor_tensor(out=t['Yt'][:], in0=t['Yt'][:], in1=m3(gnorm[:]), op=ALU.mult))
            fw.op('dve', [k_('Yt'), 'Z0'], ['Y0'], lambda e: e.tensor_tensor(out=Yst[0][:, j, :].rearrange("p (h c) -> p h c", h=6), in0=t['Yt'][:], in1=Zs[0][:, j, :].rearrange("p (h c) -> p h c", h=6), op=ALU.mult))
            if j == 7:
                fw.dma('pool', yr[:, gi * 8:(gi + 1) * 8, 0:384], Yst[0][:], ['Y0'], ['y_a'])

        def run_il(gens):
            gens = list(gens)
            while gens:
                for g_ in list(gens):
                    try:
                        next(g_)
                    except StopIteration:
                        gens.remove(g_)

        load_group(0)
        run_il([prep(0)])
        for n in range(NCH):
            tasks = [scan(n)]
            if n + 1 < NCH:
                if (n + 1) % 8 == 0:
                    load_group((n + 1) // 8)
                tasks.append(prep(n + 1))
            run_il(tasks)
        fw.barrier()


def _nsa_consts(T):
    c = {}
    Ns = T // 64
    NcP = T // 16
    Nc = NcP - 1
    WV = 64 + Ns + 1
    cc = np.arange(NcP)
    nn = np.arange(Ns)
    mat = ((cc[:, None] >= 4 * nn[None, :] - 1) & (cc[:, None] <= 4 * nn[None, :] + 3) & (cc[:, None] < Nc)).astype(np.float32)
    v1 = np.zeros((NcP, WV), np.float32)
    v1[:, 64:64 + Ns] = mat
    v1[:, WV - 1] = 1.0
    c['vc1init'] = v1.astype(ml_dtypes.bfloat16)
    cl = np.arange(128)
    q = np.arange(128)
    cm = np.zeros((128, 17, 3, 128), np.float32)
    for idx in range(17):
        ok = (16 * cl[:, None] + 31) <= (128 * idx + q[None, :])
        cm[:, idx, :, :] = np.where(ok, 0.0, NEG)[:, None, :]
    c['cmask'] = cm.reshape(128, 17, 384).astype(ml_dtypes.bfloat16)
    c['causb'] = np.tile(np.where(cl[:, None] <= q[None, :], 0.0, NEG), (1, 3)).astype(ml_dtypes.bfloat16)
    c['bandb'] = np.tile(np.where(cl[:, None] > q[None, :], 0.0, NEG), (1, 3)).astype(ml_dtypes.bfloat16)
    TE = min(T, 4096)
    t = np.arange(TE)
    c['exT'] = (((t[None, :] // 64) % 64) == np.arange(64)[:, None]).astype(np.float32).astype(ml_dtypes.bfloat16)
    return c


def phase_cmp(K, l):
    nc, fw, T = K.nc, K.fw, K.T
    NcP = T // 16
    Ns = T // 64
    WV = 64 + Ns + 1
    with ExitStack() as st:
        sb = lambda name, shape, dt=F32: st.enter_context(nc.sbuf_tensor(name + "_L%d" % l, shape, dt))
        ps = lambda name, shape, dt=F32: st.enter_context(nc.psum_tensor(name + "_L%d" % l, shape, dt))
        w1f = sb("c_w1f", [64, 32, 256])
        w1b = sb("c_w1b", [64, 32, 256], BF16)
        w2f = sb("c_w2f", [128, 2, 64])
        w2b = sb("c_w2b", [128, 2, 64], BF16)
        posf = sb("c_posf", [64, 32])
        posb = sb("c_posb", [64, 32], BF16)
        bias1 = sb("c_bias1", [128, 2])
        XT = sb("c_XT", [64, T + 16], BF16)
        h1T = sb("c_h1T", [128, 2, 512], BF16)
        ones64 = sb("c_ones", [64, 64])
        kg = sb("c_kg", [64, 1])
        sq = sb("c_sq", [64, 512])
        rn = sb("c_rn", [64, 512])
        ko = sb("c_ko", [64, 512], BF16)
        vst = sb("c_vst", [128, WV], BF16)
        pH = [ps("c_pH%d" % i, [128, 512]) for i in range(2)]
        pB = ps("c_pB", [128, 512])
        pK = ps("c_pK", [128, 512])
        pS = ps("c_pS", [128, 512])
        fw.op('pool', [], ['ones64'], lambda e: e.memset(ones64[:], 1.0))
        fw.dma('sp', kg[:], K.w['nsa_k_norm'][l, 0].rearrange("(p o) -> p o", o=1), [], ['kg'])
        for kv in range(2):
            fw.dma('sp', w1f[:], K.w['cmp_w1'][l, kv].rearrange("(l d) j -> d l j", d=64), [], ['w1f'])
            fw.op('dve', ['w1f'], ['w1b'], lambda e: e.tensor_copy(out=w1b[:], in_=w1f[:]))
            fw.dma('sp', w2f[:], K.w['cmp_w2'][l, kv].rearrange("(c p) d -> p c d", p=128), [], ['w2f'])
            fw.op('dve', ['w2f'], ['w2b'], lambda e: e.tensor_copy(out=w2b[:], in_=w2f[:]))
            fw.dma('sp', posf[:], K.w['cmp_pos'][l, kv].rearrange("l d -> d l"), [], ['posf'])
            fw.op('dve', ['posf'], ['posb'], lambda e: e.tensor_copy(out=posb[:], in_=posf[:]))
            for jc in range(2):
                for ll in range(32):
                    fw.op('pe', ['w1b', 'posb'], ['pB'], lambda e: e.matmul(pB[:, jc:jc + 1], lhsT=w1b[:, ll, jc * 128:(jc + 1) * 128], rhs=posb[:, ll:ll + 1], start=(ll == 0), stop=(ll == 31)))
            fw.op('act', ['pB'], ['bias1'], lambda e: e.copy(out=bias1[:], in_=pB[:, 0:2]))
            src = K.kcT if kv == 0 else K.vcT
            for hk in range(2):
                fw.dma('sp', XT[:, 0:T], src[hk * 64:(hk + 1) * 64, :], [], ['XT'])
                fw.op('pool', [], ['XT'], lambda e: e.memset(XT[:, T:T + 16], 0.0))
                for n0 in range(0, NcP, 512):
                    nn = min(512, NcP - n0)
                    for jc in range(2):
                        p_, pk = pH[jc], 'pH%d' % jc
                        for ll in range(32):
                            fw.op('pe', ['w1b', 'XT'], [pk], lambda e: e.matmul(p_[:, 0:nn], lhsT=w1b[:, ll, jc * 128:(jc + 1) * 128],
                                                                                rhs=XT[:, ll + 16 * n0:ll + 16 * (n0 + nn - 1) + 1:16], start=(ll == 0), stop=(ll == 31)))
                        fw.op('act', [pk, 'bias1'], ['h1T'], lambda e: e.activation(out=h1T[:, jc, 0:nn], in_=p_[:, 0:nn], func=AF.Silu, bias=bias1[:, jc:jc + 1]))
                    if kv == 0:
                        for jc in range(2):
                            fw.op('pe', ['h1T', 'w2b'], ['pK'], lambda e: e.matmul(pK[0:64, 0:nn], lhsT=w2b[:, jc, :], rhs=h1T[:, jc, 0:nn], start=(jc == 0), stop=(jc == 1)))
                        fw.op('act', ['pK'], ['sq'], lambda e: e.activation(out=sq[:, 0:nn], in_=pK[0:64, 0:nn], func=AF.Square))
                        fw.op('pe', ['sq', 'ones64'], ['pS'], lambda e: e.matmul(pS[0:64, 0:nn], lhsT=ones64[:], rhs=sq[:, 0:nn], start=True, stop=True))
                        fw.op('dve', ['pS'], ['rn'], lambda e: e.tensor_scalar(out=rn[:, 0:nn], in0=pS[0:64, 0:nn], scalar1=1.0 / 64, scalar2=EPS, op0=ALU.mult, op1=ALU.add))
                        fw.op('act', ['rn'], ['rn'], lambda e: e.activation(out=rn[:, 0:nn], in_=rn[:, 0:nn], func=AF.Sqrt))
                        fw.op('dve', ['rn'], ['rn'], lambda e: e.reciprocal(out=rn[:, 0:nn], in_=rn[:, 0:nn]))
                        fw.op('dve', ['pK', 'rn', 'kg'], ['ko'], lambda e: e.scalar_tensor_tensor(out=ko[:, 0:nn], in0=pK[0:64, 0:nn], scalar=kg[:, 0:1], in1=rn[:, 0:nn], op0=ALU.mult, op1=ALU.mult))
                        fw.dma('pool', K.KcT[hk, :, n0:n0 + nn], ko[:, 0:nn], ['ko'], ['KcT'])
                    else:
                        for c0 in range(0, nn, 128):
                            cn = min(128, nn - c0)
                            for jc in range(2):
                                fw.op('pe', ['h1T', 'w2b'], ['pK'], lambda e: e.matmul(pK[0:cn, 0:64], lhsT=h1T[:, jc, c0:c0 + cn], rhs=w2b[:, jc, :], start=(jc == 0), stop=(jc == 1)))
                            fw.dma('sp', vst[0:cn, :], K.cn['vc1init'][n0 + c0:n0 + c0 + cn, :], [], ['vst'])
                            fw.op('act', ['pK'], ['vst'], lambda e: e.copy(out=vst[0:cn, 0:64], in_=pK[0:cn, 0:64]))
                            fw.dma('pool', K.Vc1[hk, n0 + c0:n0 + c0 + cn, :], vst[0:cn, :], ['vst'], ['Vc1'])
        fw.barrier()


def phase_nsa(K, l):
    nc, fw, T = K.nc, K.fw, K.T
    NcP = T // 16
    Ns = T // 64
    WV = 64 + Ns + 1
    NQ = T // 128
    NCC = (NcP + 127) // 128
    NG = (Ns + 63) // 64
    SW = max(64 + Ns, 128 * 1 if Ns < 64 else 64 + Ns)
    TE = min(T, 4096)
    with ExitStack() as st:
        sb = lambda name, shape, dt=F32: st.enter_context(nc.sbuf_tensor(name + "_L%d" % l, shape, dt))
        ps = lambda name, shape, dt=F32: st.enter_context(nc.psum_tensor(name + "_L%d" % l, shape, dt))
        identb = sb("n_identb", [128, 128], BF16)
        identf = sb("n_identf", [128, 128])
        cmask = sb("n_cmask", [128, 17, 384], BF16)
        causb = sb("n_causb", [128, 384], BF16)
        bandb = sb("n_bandb", [128, 384], BF16)
        KX = sb("n_KX", [128, T], BF16)
        KwT = sb("n_KwT", [64, T], BF16)
        Vs1 = sb("n_Vs1", [128, NQ, 65], BF16)
        Vw1 = sb("n_Vw1", [128, NQ, 65], BF16)
        KcT = sb("n_KcT", [64, NCC * 128], BF16)
        Vc1 = sb("n_Vc1", [128, NCC, WV], BF16)
        QX = [sb("n_QX%d" % i, [128, NG, 3, 128], BF16) for i in range(2)]
        gts = [sb("n_gt%d" % i, [128, 18]) for i in range(2)]
        E = [sb("n_E%d" % i, [128, 384], BF16) for i in range(3)]
        rZ = sb("n_rZ", [128, 3])
        coef = sb("n_coef", [128, 3])
        imp = sb("n_imp", [128, Ns])
        imp2 = sb("n_imp2", [128, Ns])
        m1 = sb("n_m1", [128, 8])
        m2 = sb("n_m2", [128, 8])
        sel = sb("n_sel", [128, Ns])
        nsel = sb("n_nsel", [128, SW], BF16)
        nselT = [sb("n_nselT%d" % i, [128, SW], BF16) for i in range(2)]
        OT = [sb("n_OT%d" % i, [65, 384]) for i in range(2)]
        rd = sb("n_rd", [128, 1])
        Yb = [sb("n_Yb%d" % i, [128, 3, 64]) for i in range(2)]
        pS = [ps("n_pS%d" % i, [128, 512]) for i in range(2)]
        pOC = [ps("n_pOC%d" % i, [128, 512]) for i in range(3)]
        pOs = ps("n_pOs", [128, 512])
        pOw = ps("n_pOw", [128, 512])
        pTn = ps("n_pTn", [128, 128], BF16)

        fw.dma('sp', identb[:], K.c['identb'], [], ['identb'])
        fw.dma('sp', identf[:], K.c['identf'], [], ['identf'])
        fw.dma('sp', cmask[:], K.cn['cmask'], [], ['cmask'])
        fw.dma('sp', causb[:], K.cn['causb'], [], ['causb'])
        fw.dma('sp', bandb[:], K.cn['bandb'], [], ['bandb'])
        fw.op('pool', [], ['nsel'], lambda e: e.memset(nsel[:], 0.0))
        fw.op('pool', [], ['KcT'], lambda e: e.memset(KcT[:], 0.0))
        cnt = [0]
        rc = {}

        def rot(lst, key):
            rc[key] = rc.get(key, -1) + 1
            i = rc[key] % len(lst)
            return lst[i], '%s%d' % (key, i)

        yq = K.y.rearrange("(i p) d -> i p d", p=128)
        gtq = K.gt.rearrange("(i p) d -> i p d", p=128)
        for hk in range(2):
            fw.dma('sp', KX[0:64, :], K.ksT[hk * 64:(hk + 1) * 64, :], [], ['KX'])
            for t0 in range(0, T, TE):
                fw.dma('sp', KX[64:128, t0:t0 + TE], K.cn['exT'], [], ['KX'])
            fw.dma('sp', KwT[:], K.kwT[hk * 64:(hk + 1) * 64, :], [], ['KwT'])
            fw.dma('sp', Vs1[:], K.vs1.rearrange("(c p) w -> p c w", p=128)[:, :, hk * 65:(hk + 1) * 65], [], ['Vs1'])
            fw.dma('sp', Vw1[:], K.vw1.rearrange("(c p) w -> p c w", p=128)[:, :, hk * 65:(hk + 1) * 65], [], ['Vw1'])
            fw.dma('sp', KcT[:, 0:NcP], K.KcT[hk], [], ['KcT'])
            if NcP >= 128:
                fw.dma('sp', Vc1[:], K.Vc1[hk].rearrange("(c p) w -> p c w", p=128), [], ['Vc1'])
            else:
                fw.op('pool', [], ['Vc1'], lambda e: e.memset(Vc1[:], 0.0))
                fw.dma('sp', Vc1[0:NcP, 0, :], K.Vc1[hk], [], ['Vc1'])
            def run_branch(tiles, score, pv):
                nxt = score(tiles[0])
                for ti, tk in enumerate(tiles):
                    p_, pk = nxt
                    if ti + 1 < len(tiles):
                        nxt = score(tiles[ti + 1])
                    e_, ek = rot(E, 'E')
                    fw.op('act', [pk], [ek], lambda e: e.activation(out=e_[:], in_=p_[:, 0:384], func=AF.Exp))
                    pv(tk, e_, ek)

            def partA(i):
                par = i % 2
                Q_, Qk = QX[par], 'QX%d' % par
                g_, gk = gts[par], 'gt%d' % par
                Y_, Yk = Yb[par], 'Yb%d' % par
                ngrp = (2 * i + 1) // 64 + 1
                for grp in range(ngrp):
                    fw.dma('sp', Q_[0:64, grp, :, :], K.qn[hk * 192:(hk + 1) * 192, i * 128:(i + 1) * 128].rearrange("(g d) q -> d g q", d=64), [], [Qk])
                fw.dma('sp', g_[:], gtq[i], [], [gk])
                q0 = Q_[0:64, 0, :, :].rearrange("p g q -> p (g q)")
                jmax = (8 * i + 6) // 128

                def cmp_score(jc):
                    p_, pk = rot(pS, 'pS')
                    full = (16 * (128 * jc + 127) + 31) <= 128 * i
                    fw.op('pe', ['KcT', Qk], [pk], lambda e: e.matmul(p_[:, 0:384], lhsT=KcT[:, jc * 128:(jc + 1) * 128], rhs=q0, start=True, stop=full))
                    if not full:
                        idx = (128 * i - 2048 * jc) // 128
                        fw.op('pe', ['identb', 'cmask'], [pk], lambda e: e.matmul(p_[:, 0:384], lhsT=identb[:], rhs=cmask[:, idx, :], start=False, stop=True))
                    return p_, pk

                def cmp_pv(jc, e_, ek):
                    for g in range(3):
                        fw.op('pe', [ek, 'Vc1'], ['pOC%d' % g], lambda e: e.matmul(pOC[g][:, 0:WV], lhsT=e_[:, g * 128:(g + 1) * 128], rhs=Vc1[:, jc, :], start=(jc == 0), stop=(jc == jmax)))

                run_branch(list(range(jmax + 1)), cmp_score, cmp_pv)
                for g in range(3):
                    fw.op('dve', ['pOC%d' % g], ['rZ'], lambda e: e.tensor_scalar(out=rZ[:, g:g + 1], in0=pOC[g][:, WV - 1:WV], scalar1=1e-30, scalar2=None, op0=ALU.max))
                fw.op('dve', ['rZ'], ['rZ'], lambda e: e.reciprocal(out=rZ[:], in_=rZ[:]))
                gv = g_[:, hk * 9:(hk + 1) * 9].rearrange("p (g b) -> p g b", b=3)
                fw.op('dve', ['rZ', gk], ['coef'], lambda e: e.tensor_tensor(out=coef[:], in0=rZ[:], in1=gv[:, :, 0], op=ALU.mult))
                for g in range(3):
                    fw.op('dve', ['pOC%d' % g, 'coef'], [Yk], lambda e: e.tensor_scalar(out=Y_[:, g, :], in0=pOC[g][:, 0:64], scalar1=coef[:, g:g + 1], scalar2=None, op0=ALU.mult))
                    if g == 0:
                        fw.op('dve', ['pOC0', 'rZ'], ['imp'], lambda e: e.tensor_scalar(out=imp[:], in0=pOC[0][:, 64:64 + Ns], scalar1=rZ[:, 0:1], scalar2=None, op0=ALU.mult))
                    else:
                        fw.op('dve', ['pOC%d' % g, 'rZ', 'imp'], ['imp'], lambda e: e.scalar_tensor_tensor(out=imp[:], in0=pOC[g][:, 64:64 + Ns], scalar=rZ[:, g:g + 1], in1=imp[:], op0=ALU.mult, op1=ALU.add))
                if 2 * i + 2 < Ns:
                    fw.op('pool', ['imp'], ['imp'], lambda e: e.memset(imp[:, 2 * i + 2:Ns], -1.0))
                fw.op('pool', ['imp'], ['imp'], lambda e: e.memset(imp[0:64, 2 * i + 1:2 * i + 2], -1.0))
                fw.op('pool', ['imp'], ['imp'], lambda e: e.memset(imp[64:128, 2 * i + 1:2 * i + 2], 1e4))
                fw.op('pool', ['imp'], ['imp'], lambda e: e.memset(imp[:, 2 * i:2 * i + 1], 1e4))
                if i >= 1:
                    fw.op('pool', ['imp'], ['imp'], lambda e: e.memset(imp[0:64, 2 * i - 1:2 * i], 1e4))
                fw.op('pool', ['imp'], ['imp'], lambda e: e.memset(imp[:, 0:1], 1e4))
                fw.op('dve', ['imp'], ['m1'], lambda e: e.max(out=m1[:], in_=imp[:]))
                fw.op('dve', ['imp', 'm1'], ['imp2'], lambda e: e.match_replace(out=imp2[:], in_to_replace=m1[:], in_values=imp[:], imm_value=-2.0))
                fw.op('dve', ['imp2'], ['m2'], lambda e: e.max(out=m2[:], in_=imp2[:]))
                fw.op('dve', ['imp', 'm2'], ['sel'], lambda e: e.tensor_scalar(out=sel[:], in0=imp[:], scalar1=m2[:, 7:8], scalar2=None, op0=ALU.is_ge))
                fw.op('dve', ['sel'], ['nsel'], lambda e: e.tensor_scalar(out=nsel[:, 64:64 + Ns], in0=sel[:], scalar1=-NEG, scalar2=NEG, op0=ALU.mult, op1=ALU.add))
                fw.op('dve', ['nsel'], ['nselT%d' % par], lambda e: e.tensor_copy(out=nselT[par][:], in_=nsel[:]))

            def partA2(i):
                par = i % 2
                Q_, Qk = QX[par], 'QX%d' % par
                ngrp = (2 * i + 1) // 64 + 1
                for grp in range(ngrp):
                    fw.op('pe', ['nselT%d' % par, 'identb'], ['pTn'], lambda e: e.transpose(out=pTn[:, :], in_=nselT[par][:, grp * 64:grp * 64 + 128], identity=identb[:]))
                    fw.op('dve', ['pTn'], [Qk], lambda e: e.tensor_copy(out=Q_[64:128, grp, :, :], in_=pTn[64:128, :].unsqueeze(1).to_broadcast([64, 3, 128])))

            def partB(i):
                par = i % 2
                Q_, Qk = QX[par], 'QX%d' % par
                g_, gk = gts[par], 'gt%d' % par
                Y_, Yk = Yb[par], 'Yb%d' % par
                q0 = Q_[0:64, 0, :, :].rearrange("p g q -> p (g q)")
                gv = g_[:, hk * 9:(hk + 1) * 9].rearrange("p (g b) -> p g b", b=3)
                k0 = max(0, i - 4)

                def win_score(kc):
                    p_, pk = rot(pS, 'pS')
                    msk = causb if kc == i else (bandb if kc == i - 4 else None)
                    fw.op('pe', ['KwT', Qk], [pk], lambda e: e.matmul(p_[:, 0:384], lhsT=KwT[:, kc * 128:(kc + 1) * 128], rhs=q0, start=True, stop=(msk is None)))
                    if msk is not None:
                        fw.op('pe', ['identb', 'causb', 'bandb'], [pk], lambda e: e.matmul(p_[:, 0:384], lhsT=identb[:], rhs=msk[:], start=False, stop=True))
                    return p_, pk

                def win_pv(kc, e_, ek):
                    fw.op('pe', [ek, 'Vw1'], ['pOw'], lambda e: e.matmul(pOw[0:65, 0:384], lhsT=Vw1[:, kc, :], rhs=e_[:], start=(kc == k0), stop=(kc == i)))

                def sel_score(kc):
                    grp = kc // 32
                    p_, pk = rot(pS, 'pS')
                    fw.op('pe', ['KX', Qk], [pk], lambda e: e.matmul(p_[:, 0:384], lhsT=KX[:, kc * 128:(kc + 1) * 128], rhs=Q_[:, grp, :, :].rearrange("p g q -> p (g q)"), start=True, stop=(kc < i)))
                    if kc == i:
                        fw.op('pe', ['identb', 'causb'], [pk], lambda e: e.matmul(p_[:, 0:384], lhsT=identb[:], rhs=causb[:], start=False, stop=True))
                    return p_, pk

                def sel_pv(kc, e_, ek):
                    fw.op('pe', [ek, 'Vs1'], ['pOs'], lambda e: e.matmul(pOs[0:65, 0:384], lhsT=Vs1[:, kc, :], rhs=e_[:], start=(kc == 0), stop=(kc == i)))

                run_branch(list(range(k0, i + 1)), win_score, win_pv)
                run_branch(list(range(i + 1)), sel_score, sel_pv)
                if i + 1 < NQ:
                    partA2(i + 1)
                for bi, (pO_, pOk) in enumerate(((pOw, 'pOw'), (pOs, 'pOs'))):
                    o_, ok = OT[bi], 'OT%d' % bi
                    fw.op('act', [pOk], [ok], lambda e: e.copy(out=o_[:], in_=pO_[0:65, 0:384]))
                    for g in range(3):
                        pf, pfk = pOC[g], 'pOC%d' % g
                        fw.op('pe', [ok, 'identf'], [pfk], lambda e: e.transpose(out=pf[:, 0:65], in_=o_[0:65, g * 128:(g + 1) * 128], identity=identf[0:65, 0:65]))
                        fw.op('dve', [pfk], ['rd'], lambda e: e.reciprocal(out=rd[:], in_=pf[:, 64:65]))
                        fw.op('dve', ['rd', gk], ['rd'], lambda e: e.tensor_tensor(out=rd[:], in0=rd[:], in1=gv[:, g, 2 - bi:3 - bi], op=ALU.mult))
                        fw.op('dve', [pfk, 'rd', Yk], [Yk], lambda e: e.scalar_tensor_tensor(out=Y_[:, g, :], in0=pf[:, 0:64], scalar=rd[:, 0:1], in1=Y_[:, g, :], op0=ALU.mult, op1=ALU.add))
                fw.dma('pool', yq[i][:, 384 + hk * 192:384 + (hk + 1) * 192], Y_[:].rearrange("p g d -> p (g d)"), [Yk], ['y_b'])

            partA(0)
            partA2(0)
            for i in range(NQ):
                if i + 1 < NQ:
                    partA(i + 1)
                partB(i)
        fw.barrier()


def phase_p3(K, l):
    nc, fw, T = K.nc, K.fw, K.T
    NM = T // 256
    with ExitStack() as st:
        sb = lambda name, shape, dt=F32: st.enter_context(nc.sbuf_tensor(name + "_L%d" % l, shape, dt))
        ps = lambda name, shape, dt=F32: st.enter_context(nc.psum_tensor(name + "_L%d" % l, shape, dt))
        Wo = sb("f_Wo", [128, 8, 1024], BF16)
        W1 = sb("f_W1", [128, 8, 4096], BF16)
        W2 = sb("f_W2", [128, 32, 1024], BF16)
        wst = [sb("f_wst%d" % i, [128, 1024]) for i in range(2)]
        gf = sb("f_gf", [128, 8])
        identb = sb("f_identb", [128, 128], BF16)
        yt = sb("f_yt", [128, 2, 1024])
        yb = sb("f_yb", [128, 2, 1024], BF16)
        yT = sb("f_yT", [128, 8, 256], BF16)
        xr = [sb("f_xr%d" % i, [128, 2, 1024]) for i in range(1)]
        xb = sb("f_xb", [128, 2, 1024], BF16)
        xnT = sb("f_xnT", [128, 8, 256], BF16)
        hT = sb("f_hT", [128, 32, 256], BF16)
        rl = [sb("f_rl%d" % i, [128, 256]) for i in range(2)]
        junk = sb("f_junk", [128, 1024], BF16)
        ss = sb("f_ss", [128, 2])
        rstd = sb("f_rstd", [128, 2])
        pT = [ps("f_pT%d" % i, [128, 256], BF16) for i in range(2)]
        pA = [ps("f_pA%d" % i, [128, 512]) for i in range(4)]
        pH = [ps("f_pH%d" % i, [128, 512]) for i in range(2)]
        fw.dma('sp', identb[:], K.c['identb'], [], ['identb'])
        fw.dma('sp', gf[:], K.w['norm_ffn'][l].rearrange("(k p) -> p k", p=128), [], ['gf'])
        wi = [0]

        def loadw(dst_ap, src_ap, scal=None):
            i = wi[0] % 2
            wi[0] += 1
            fw.dma('sp' if i else 'pool', wst[i][:], src_ap, [], ['wst%d' % i])
            if scal is None:
                fw.op('dve' if i else 'pool', ['wst%d' % i], ['W'], lambda e: e.tensor_copy(out=dst_ap, in_=wst[i][:]))
            else:
                fw.op('dve' if i else 'pool', ['wst%d' % i, 'gf'], ['W'], lambda e: e.tensor_scalar(out=dst_ap, in0=wst[i][:], scalar1=scal, scalar2=None, op0=ALU.mult))
        for kc in range(8):
            loadw(Wo[:, kc, :], K.w['w_out'][l, kc * 128:(kc + 1) * 128, :])
            for q4 in range(4):
                loadw(W1[:, kc, q4 * 1024:(q4 + 1) * 1024], K.w['w_ffn1'][l, kc * 128:(kc + 1) * 128, q4 * 1024:(q4 + 1) * 1024], gf[:, kc:kc + 1])
        for fc in range(32):
            loadw(W2[:, fc, :], K.w['w_ffn2'][l, fc * 128:(fc + 1) * 128, :])
        ysrc = K.y.rearrange("(m j p) d -> m p j d", p=128, j=2)
        xsrc = K.xin[l].rearrange("(m j p) d -> m p j d", p=128, j=2)
        xdst = K.xout[l].rearrange("(m j p) d -> m p j d", p=128, j=2)
        cnt = [0]
        rc = {}

        def rot(lst, key):
            rc[key] = rc.get(key, -1) + 1
            i = rc[key] % len(lst)
            return lst[i], '%s%d' % (key, i)

        def transposes(src, srck, dst, dstk):
            for kc in range(8):
                p_, pk = pT[kc % 2], 'pT%d' % (kc % 2)
                for j in range(2):
                    fw.op('pe', [srck, 'identb'], [pk], lambda e: e.transpose(out=p_[:, j * 128:(j + 1) * 128], in_=src[:, j, kc * 128:(kc + 1) * 128], identity=identb[:]))
                if kc % 2:
                    fw.op('act', [pk], [dstk], lambda e: e.copy(out=dst[:, kc, :], in_=p_[:]))
                else:
                    fw.op('dve', [pk], [dstk], lambda e: e.tensor_copy(out=dst[:, kc, :], in_=p_[:]))

        for m in range(NM):
            x_, xk = xr[0], 'xr0'
            fw.dma('sp', yt[:], ysrc[m], [], ['yt'])
            fw.dma('sp', x_[:], xsrc[m], [], [xk])
            fw.op('pool', ['yt'], ['yb'], lambda e: e.tensor_copy(out=yb[:], in_=yt[:]))
            transposes(yb, 'yb', yT, 'yT')
            for j in range(2):
                for nh in range(2):
                    p_, pk = rot(pA, 'pA')
                    for kc in range(8):
                        fw.op('pe', ['yT', 'W'], [pk], lambda e: e.matmul(p_[:], lhsT=yT[:, kc, j * 128:(j + 1) * 128], rhs=Wo[:, kc, nh * 512:(nh + 1) * 512], start=(kc == 0), stop=(kc == 7)))
                    fw.op('dve', [pk, xk], [xk], lambda e: e.tensor_tensor(out=x_[:, j, nh * 512:(nh + 1) * 512], in0=x_[:, j, nh * 512:(nh + 1) * 512], in1=p_[:], op=ALU.add))
            fw.op('dve', [], ['ss'], lambda e: e.memset(ss[:], 0.0))
            for j in range(2):
                fw.op('act', [xk, 'ss'], ['junk', 'ss'], lambda e: e.activation(out=junk[:], in_=x_[:, j, :], func=AF.Square, accum_out=ss[:, j:j + 1]))
            fw.op('dve', ['ss'], ['rstd'], lambda e: e.tensor_scalar(out=rstd[:], in0=ss[:], scalar1=1.0 / D, scalar2=EPS, op0=ALU.mult, op1=ALU.add))
            fw.op('act', ['rstd'], ['rstd'], lambda e: e.activation(out=rstd[:], in_=rstd[:], func=AF.Sqrt))
            fw.op('dve', ['rstd'], ['rstd'], lambda e: e.reciprocal(out=rstd[:], in_=rstd[:]))
            for j in range(2):
                fw.op('pool', [xk, 'rstd'], ['xb'], lambda e: e.tensor_scalar(out=xb[:, j, :], in0=x_[:, j, :], scalar1=rstd[:, j:j + 1], scalar2=None, op0=ALU.mult))
            transposes(xb, 'xb', xnT, 'xnT')
            for fc in range(32):
                p_, pk = rot(pH, 'pH')
                for kc in range(8):
                    fw.op('pe', ['xnT', 'W'], [pk], lambda e: e.matmul(p_[:, 0:256], lhsT=W1[:, kc, fc * 128:(fc + 1) * 128], rhs=xnT[:, kc, :], start=(kc == 0), stop=(kc == 7)))
                r_, rk = rot(rl, 'rl')
                fw.op('act', [pk], [rk], lambda e: e.activation(out=r_[:], in_=p_[:, 0:256], func=AF.Relu))
                fw.op('pool' if fc % 2 else 'dve', [rk], ['hT%d' % fc], lambda e: e.tensor_tensor(out=hT[:, fc, :], in0=r_[:], in1=r_[:], op=ALU.mult))
            for j in range(2):
                for nh in range(2):
                    p_, pk = rot(pA, 'pA')
                    for fc in range(32):
                        fw.op('pe', ['hT%d' % fc, 'W'], [pk], lambda e: e.matmul(p_[:], lhsT=hT[:, fc, j * 128:(j + 1) * 128], rhs=W2[:, fc, nh * 512:(nh + 1) * 512], start=(fc == 0), stop=(fc == 31)))
                    fw.op('dve', [pk, xk], [xk], lambda e: e.tensor_tensor(out=x_[:, j, nh * 512:(nh + 1) * 512], in0=x_[:, j, nh * 512:(nh + 1) * 512], in1=p_[:], op=ALU.add))
            fw.dma('pool', xdst[m], x_[:], [xk], ['xout'])
        fw.barrier()


def kernel(**inputs):
    x = np.asarray(inputs["x"], dtype=np.float32)
    B, T, _ = x.shape
    nc, K = build(T, L=2)
    base = {n: np.ascontiguousarray(np.asarray(inputs[n], dtype=np.float32)) for n in WNAMES}
    for n, v in K.consts.items():
        base['c_' + n] = v
    for n, v in K.consts_n.items():
        base['cn_' + n] = v
    in_maps = []
    for b in range(B):
        m = dict(base)
        m['x'] = np.ascontiguousarray(x[b])
        in_maps.append(m)
    res = run_bass_kernel_spmd(nc, in_maps, core_ids=list(range(B)))
    return np.stack([np.asarray(r["out"], dtype=np.float32) for r in res.results], axis=0)
```

```python
import os
import numpy as np
import ml_dtypes
from contextlib import ExitStack
import concourse.bass as bass
import concourse.mybir as mybir
from concourse.bass_utils import run_bass_kernel_spmd

F32 = mybir.dt.float32
BF16 = mybir.dt.bfloat16
AF = mybir.ActivationFunctionType
ALU = mybir.AluOpType
AX = mybir.AxisListType

D = 1024
DIN = 2974
DFF = 4096
EPS = 1e-6
NEG = -30000.0

C_QKV, C_Z, C_B, C_A, C_QB, C_KV, C_GATE, C_U = 0, 1152, 1536, 1542, 1548, 1932, 2700, 2718
C_KC, C_VC, C_KS, C_VS, C_KW, C_VW = 1932, 2060, 2188, 2316, 2444, 2572


class FW:
    def __init__(self, nc, stack):
        self.nc = nc
        self.eng = {'pe': nc.tensor, 'act': nc.scalar, 'dve': nc.vector, 'pool': nc.gpsimd, 'sp': nc.sync}
        self.semh = {}
        self.cnt = {}
        for e in ['pe', 'act', 'dve', 'pool']:
            self.semh[e] = stack.enter_context(nc.semaphore('s_' + e))
            self.cnt[e] = 0
        self.NDS = 6
        self.dq = {}
        for q in ['sp', 'pool']:
            keys = []
            for i in range(self.NDS):
                k = 'd_%s%d' % (q, i)
                self.semh[k] = stack.enter_context(nc.semaphore(k))
                keys.append(k)
            self.dq[q] = {'keys': keys, 'n': 0}
        self.known = {e: {} for e in self.eng}
        self.lastw = {}
        self.readers = {}
        self.ninst = 0

    def _deps(self, R, W):
        deps = {}

        def add(k, v):
            if deps.get(k, 0) < v:
                deps[k] = v
        for r in R:
            t = self.lastw.get(r)
            if t is not None:
                add(*t)
        for w in W:
            t = self.lastw.get(w)
            if t is not None:
                add(*t)
            rd = self.readers.get(w)
            if rd:
                for k, v in rd.items():
                    add(k, v)
        return deps

    def _wait(self, e, deps):
        kn = self.known[e]
        for k, v in deps.items():
            if k == e and e == 'pe':
                continue
            if kn.get(k, 0) >= v:
                continue
            self.eng[e].wait_ge(self.semh[k], v)
            kn[k] = v
            self.ninst += 1

    def _commit(self, tok, R, W):
        k, v = tok
        for r in R:
            rd = self.readers.setdefault(r, {})
            if rd.get(k, 0) < v:
                rd[k] = v
        for w in W:
            self.lastw[w] = tok
            self.readers[w] = {}

    def op(self, e, R, W, fn):
        self._wait(e, self._deps(R, W))
        inst = fn(self.eng[e])
        self.cnt[e] += 1
        inst.then_inc(self.semh[e], 1)
        self._commit((e, self.cnt[e]), R, W)
        self.ninst += 1
        return inst

    def dma(self, q, out, in_, R, W):
        dq = self.dq[q]
        j = dq['n']
        s = j % self.NDS
        key = dq['keys'][s]
        deps = self._deps(R, W)
        if j >= self.NDS:
            v = 16 * (j // self.NDS)
            if deps.get(key, 0) < v:
                deps[key] = v
        self._wait(q, deps)
        inst = self.eng[q].dma_start(out=out, in_=in_)
        inst.then_inc(self.semh[key], 16)
        dq['n'] += 1
        self._commit((key, 16 * (j // self.NDS + 1)), R, W)
        self.ninst += 1

    def barrier(self):
        cur = {}
        for q, dq in self.dq.items():
            for si, key in enumerate(dq['keys']):
                n = (dq['n'] - si + self.NDS - 1) // self.NDS if dq['n'] > si else 0
                if n > 0:
                    cur[key] = 16 * n
        for e in ['pe', 'act', 'dve', 'pool']:
            if self.cnt[e] > 0:
                cur[e] = self.cnt[e]
        for e in self.eng:
            self._wait(e, {k: v for k, v in cur.items() if k != e})

    def finish(self):
        deps = {}
        for q, dq in self.dq.items():
            for s, key in enumerate(dq['keys']):
                n = (dq['n'] - s + self.NDS - 1) // self.NDS if dq['n'] > s else 0
                if n > 0:
                    deps[key] = 16 * n
        for e in ['pe', 'act', 'dve', 'pool']:
            if self.cnt[e] > 0:
                deps[e] = self.cnt[e]
        self._wait('sp', deps)


class Ctx:
    pass


def _consts():
    c = {}
    c['identb'] = np.eye(128).astype(ml_dtypes.bfloat16)
    c['identf'] = np.eye(128).astype(np.float32)
    ob = np.zeros((128, 128), np.float32)
    ob[:64, :64] = 1.0
    ob[64:, 64:] = 1.0
    c['onesblk'] = ob
    i = np.arange(64)
    c['triU'] = (i[:, None] <= i[None, :]).astype(np.float32)
    c['biasL'] = np.where(i[None, :] < i[:, None], 0.0, NEG).astype(np.float32)
    c['biasU'] = np.where(i[:, None] <= i[None, :], 0.0, NEG).astype(np.float32)
    t1 = np.arange(1, 513, dtype=np.float32)
    c['invcnt'] = np.stack([1.0 / np.minimum(t1, float(w)) for w in (2, 4, 8, 16)]).astype(np.float32)
    return c


def phase_p1(K, l):
    nc, fw, T = K.nc, K.fw, K.T
    NM = T // 512
    with ExitStack() as st:
        sb = lambda name, shape, dt: st.enter_context(nc.sbuf_tensor(name + "_L%d" % l, shape, dt))
        ps = lambda name, shape, dt: st.enter_context(nc.psum_tensor(name + "_L%d" % l, shape, dt))
        Wb = sb("p1_Wb", [128, 8, DIN], BF16)
        wst = [sb("p1_wst%d" % i, [128, DIN // 2], F32) for i in range(2)]
        gmix = sb("p1_gmix", [128, 8], F32)
        identb = sb("p1_identb", [128, 128], BF16)
        onesblk = sb("p1_onesblk", [128, 128], F32)
        cw = sb("p1_cw", [128, 9, 4], F32)
        qg = sb("p1_qg", [128, 4], F32)
        dtb = sb("p1_dtb", [128, 6], F32)
        nA = sb("p1_nA", [128, 6], F32)
        pscale = sb("p1_pscale", [128, 256], F32)
        poolw = sb("p1_poolw", [64, 4, 64], F32)
        poolwb = sb("p1_poolwb", [64, 4, 64], BF16)
        invc = sb("p1_invc", [64, 4, 512], F32)
        xt = [sb("p1_xt%d" % i, [128, 4, D], F32) for i in range(1)]
        junk = sb("p1_junk", [128, D], BF16)
        ss = sb("p1_ss", [128, 4], F32)
        rstd = sb("p1_rstd", [128, 4], F32)
        xb = sb("p1_xb", [128, 4, D], BF16)
        xT = [sb("p1_xT%d" % i, [128, 8, 512], BF16) for i in range(2)]
        cst = [sb("p1_cst%d" % i, [128, 515], F32) for i in range(9)]
        acc = [sb("p1_acc%d" % i, [128, 512], F32) for i in range(3)]
        sl = [sb("p1_sl%d" % i, [128, 512], F32) for i in range(3)]
        sq = [sb("p1_sq%d" % i, [128, 512], F32) for i in range(3)]
        rn = [sb("p1_rn%d" % i, [128, 512], F32) for i in range(3)]
        of = [sb("p1_of%d" % i, [128, 512], F32) for i in range(2)]
        ob = [sb("p1_ob%d" % i, [128, 512], BF16) for i in range(2)]
        ust = [sb("p1_ust%d" % i, [64, 527], F32) for i in range(4)]
        s2 = sb("p1_s2", [64, 526], F32)
        s4 = sb("p1_s4", [64, 524], F32)
        s8 = sb("p1_s8", [64, 520], F32)
        s16 = sb("p1_s16", [64, 512], F32)
        dfb = [sb("p1_dfb%d" % i, [64, 512], BF16) for i in range(4)]
        ycs = sb("p1_ycs", [128, 4, 256], F32)
        zst = sb("p1_zst", [128, 4, 384], F32)
        bgs = sb("p1_bgs", [128, 4, 12], F32)
        tmpa = sb("p1_tmpa", [128, 6], F32)
        gts = sb("p1_gts", [128, 4, 18], F32)
        v1s = [sb("p1_v1s%d" % i, [128, 4, 130], BF16) for i in range(2)]
        v1w = [sb("p1_v1w%d" % i, [128, 4, 130], BF16) for i in range(2)]
        pT = [ps("p1_pT%d" % i, [128, 512], BF16) for i in range(2)]
        pF = [ps("p1_pF%d" % i, [128, 512], F32) for i in range(3)]
        pS = [ps("p1_pS%d" % i, [128, 512], F32) for i in range(3)]

        fw.dma('sp', identb[:], K.c['identb'], [], ['identb'])
        fw.dma('sp', onesblk[:], K.c['onesblk'], [], ['onesblk'])
        fw.dma('sp', gmix[:], K.w['norm_mix'][l].rearrange("(k p) -> p k", p=128), [], ['gmix'])
        for k in range(4):
            fw.dma('sp', cw[:, :, k], K.w['conv_w'][l, k].rearrange("(c p) -> p c", p=128), [], ['cw'])
        for hh in range(2):
            fw.dma('sp', qg[hh * 64:(hh + 1) * 64, 0:1], K.w['nsa_q_norm'][l].rearrange("(p o) -> p o", o=1), [], ['qg'])
            for j in range(3):
                fw.dma('sp', qg[hh * 64:(hh + 1) * 64, 1 + j:2 + j], K.w['nsa_k_norm'][l, j].rearrange("(p o) -> p o", o=1), [], ['qg'])
        fw.op('dve', ['qg'], ['qg'], lambda e: e.tensor_scalar(out=qg[:, 0:1], in0=qg[:, 0:1], scalar1=0.125, scalar2=None, op0=ALU.mult))
        fw.dma('sp', dtb[:], K.w['dt_bias'][l:l + 1, :].partition_broadcast(128), [], ['dtb'])
        fw.dma('sp', nA[:], K.w['a_log'][l:l + 1, :].partition_broadcast(128), [], ['nA'])
        fw.op('act', ['nA'], ['nA'], lambda e: e.activation(out=nA[:], in_=nA[:], func=AF.Exp))
        fw.op('dve', ['nA'], ['nA'], lambda e: e.tensor_scalar(out=nA[:], in0=nA[:], scalar1=-1.0, scalar2=None, op0=ALU.mult))
        fw.dma('sp', pscale[:], K.w['pool_scale'][l:l + 1, :].partition_broadcast(128), [], ['pscale'])
        fw.dma('sp', poolw[:], K.w['pool_w'][l].rearrange("g c d -> c g d"), [], ['poolw'])
        fw.op('dve', ['poolw'], ['poolwb'], lambda e: e.tensor_copy(out=poolwb[:], in_=poolw[:]))
        fw.dma('sp', invc[:], K.c['invcnt'].partition_broadcast(64), [], ['invc'])
        HW = DIN // 2
        for kc in range(8):
            for hf in range(2):
                w_ = wst[hf]
                fw.dma('sp' if hf else 'pool', w_[:], K.w['w_in'][l, kc * 128:(kc + 1) * 128, hf * HW:(hf + 1) * HW], [], ['wst%d' % hf])
                fw.op('dve' if hf else 'pool', ['wst%d' % hf, 'gmix'], ['Wb'],
                      lambda e: e.tensor_scalar(out=Wb[:, kc, hf * HW:(hf + 1) * HW], in0=w_[:], scalar1=gmix[:, kc:kc + 1], scalar2=None, op0=ALU.mult))
        for i in range(9):
            fw.op('pool', [], ['cst%d' % i], lambda e: e.memset(cst[i][:, 0:3], 0.0))
        for i in range(4):
            fw.op('pool', [], ['ust%d' % i], lambda e: e.memset(ust[i][:, 0:15], 0.0))
        for i in range(2):
            fw.op('pool', [], ['v1s%d' % i], lambda e: e.memset(v1s[i][:], 1.0))
            fw.op('pool', [], ['v1w%d' % i], lambda e: e.memset(v1w[i][:], 1.0))

        xsrc = K.xin[l].rearrange("(m j p) d -> m p j d", p=128, j=4)
        cnt = [0]
        rc = {}
        freel = {}

        def rot(lst, key):
            rc[key] = rc.get(key, -1) + 1
            i = rc[key] % len(lst)
            return lst[i], '%s%d' % (key, i)

        for m in range(NM):
            par = m % 2
            x_, xk = xt[0], 'xt0'
            xT_, xTk = xT[par], 'xT%d' % par
            fw.dma('pool', x_[:], xsrc[m], [], [xk])
            fw.op('dve', [], ['ss'], lambda e: e.memset(ss[:], 0.0))
            for j in range(4):
                fw.op('act', [xk, 'ss'], ['junk', 'ss'], lambda e: e.activation(out=junk[:], in_=x_[:, j, :], func=AF.Square, accum_out=ss[:, j:j + 1]))
            fw.op('dve', ['ss'], ['rstd'], lambda e: e.tensor_scalar(out=rstd[:], in0=ss[:], scalar1=1.0 / D, scalar2=EPS, op0=ALU.mult, op1=ALU.add))
            fw.op('act', ['rstd'], ['rstd'], lambda e: e.activation(out=rstd[:], in_=rstd[:], func=AF.Sqrt))
            fw.op('dve', ['rstd'], ['rstd'], lambda e: e.reciprocal(out=rstd[:], in_=rstd[:]))
            for j in range(4):
                fw.op('dve' if j % 2 else 'pool', [xk, 'rstd'], ['xb%d' % j],
                      lambda e: e.tensor_scalar(out=xb[:, j, :], in0=x_[:, j, :], scalar1=rstd[:, j:j + 1], scalar2=None, op0=ALU.mult))
            for kc in range(8):
                p_, pk = pT[kc % 2], 'pT%d' % (kc % 2)
                for j in range(4):
                    fw.op('pe', ['xb%d' % j, 'identb'], [pk], lambda e: e.transpose(out=p_[:, j * 128:(j + 1) * 128], in_=xb[:, j, kc * 128:(kc + 1) * 128], identity=identb[:]))
                if kc % 2:
                    fw.op('act', [pk], [xTk], lambda e: e.copy(out=xT_[:, kc, :], in_=p_[:]))
                else:
                    fw.op('dve', [pk], [xTk], lambda e: e.tensor_copy(out=xT_[:, kc, :], in_=p_[:]))

            def acq(lst, key):
                fl = freel.setdefault(key, list(range(len(lst))))
                i = fl.pop(0)
                return lst[i], '%s%d' % (key, i)

            def rel(key, k_):
                freel[key].append(int(k_[len(key):]))

            def fm(col0, ncols):
                p_, pk = acq(pF, 'pF')
                for kc in range(8):
                    fw.op('pe', [xTk, 'Wb'], [pk], lambda e: e.matmul(p_[0:ncols, :], lhsT=Wb[:, kc, col0:col0 + ncols], rhs=xT_[:, kc, :], start=(kc == 0), stop=(kc == 7)))
                return p_, pk

            def rnorm(src_ap, srck, n, mean):
                q_, qk_ = acq(sq, 'sq')
                fw.op('pool' if srck.startswith('sl') else 'act', [srck], [qk_],
                      (lambda e: e.tensor_tensor(out=q_[:], in0=src_ap, in1=src_ap, op=ALU.mult)) if srck.startswith('sl')
                      else (lambda e: e.activation(out=q_[:], in_=src_ap, func=AF.Square)))
                yield
                s_, sk_ = acq(pS, 'pS')
                fw.op('pe', [qk_, 'onesblk'], [sk_], lambda e: e.matmul(s_[:], lhsT=onesblk[:], rhs=q_[:], start=True, stop=True))
                rel('sq', qk_)
                yield
                r_, rk_ = rot(rn, 'rn')
                fw.op('dve', [sk_], [rk_], lambda e: e.tensor_scalar(out=r_[:], in0=s_[:], scalar1=(1.0 / 64 if mean else 1.0), scalar2=EPS, op0=ALU.mult, op1=ALU.add))
                rel('pS', sk_)
                fw.op('act', [rk_], [rk_], lambda e: e.activation(out=r_[:], in_=r_[:], func=AF.Sqrt))
                fw.op('dve', [rk_], [rk_], lambda e: e.reciprocal(out=r_[:], in_=r_[:]))
                return r_, rk_

            def gdn_chunk(ch):
                p_, pk = fm(C_QKV + ch * 128, 128)
                yield
                c_, ck = cst[ch], 'cst%d' % ch
                fw.op('act', [pk], [ck], lambda e: e.copy(out=c_[:, 3:515], in_=p_[:]))
                a_, ak = rot(acc, 'acc')
                fw.op('dve', [ck, 'cw'], [ak], lambda e: e.tensor_scalar(out=a_[:], in0=c_[:, 0:512], scalar1=cw[:, ch, 0:1], scalar2=None, op0=ALU.mult))
                for k in range(1, 4):
                    fw.op('dve', [ck, 'cw', ak], [ak],
                          lambda e: e.scalar_tensor_tensor(out=a_[:], in0=c_[:, k:k + 512], scalar=cw[:, ch, k:k + 1], in1=a_[:], op0=ALU.mult, op1=ALU.add))
                fw.op('pool', [ck], [ck], lambda e: e.tensor_copy(out=c_[:, 0:3], in_=c_[:, 512:515]))
                s_, sk = acq(sl, 'sl')
                fw.op('act', [ak], [sk], lambda e: e.activation(out=s_[:], in_=a_[:], func=AF.Silu))
                rel('pF', pk)
                if ch < 6:
                    r_, rk = yield from rnorm(s_[:], sk, 128, False)
                    o_, ok = rot(of, 'of')
                    fw.op('dve', [sk, rk], [ok], lambda e: e.scalar_tensor_tensor(out=o_[:], in0=s_[:], scalar=(0.125 if ch < 3 else 1.0), in1=r_[:], op0=ALU.mult, op1=ALU.mult))
                    fw.dma('sp', K.gqk[ch * 128:(ch + 1) * 128, m * 512:(m + 1) * 512], o_[:], [ok], ['gqk'])
                else:
                    fw.dma('sp', K.gqk[ch * 128:(ch + 1) * 128, m * 512:(m + 1) * 512], s_[:], [sk], ['gqk'])
                rel('sl', sk)

            def nsa_chunk(col0, ch, gi_, dst):
                p_, pk = fm(col0 + ch * 128, 128)
                yield
                r_, rk = yield from rnorm(p_[:], pk, 128, True)
                o_, ok = rot(ob, 'ob')
                fw.op('dve', [pk, rk, 'qg'], [ok], lambda e: e.scalar_tensor_tensor(out=o_[:], in0=p_[:], scalar=qg[:, gi_:gi_ + 1], in1=r_[:], op0=ALU.mult, op1=ALU.mult))
                fw.dma('sp', dst[ch * 128:(ch + 1) * 128, m * 512:(m + 1) * 512], o_[:], [ok], ['nsa_dst'])
                rel('pF', pk)

            def raw_chunk(col0, dst):
                p_, pk = fm(col0, 128)
                yield
                o_, ok = rot(ob, 'ob')
                fw.op('act', [pk], [ok], lambda e: e.copy(out=o_[:], in_=p_[:]))
                fw.dma('sp', dst[:, m * 512:(m + 1) * 512], o_[:], [ok], ['nsa_dst'])
                rel('pF', pk)

            def pool_chunk(gi, wlen):
                p_, pk = fm(C_U + gi * 64, 64)
                yield
                u_, uk = ust[gi], 'ust%d' % gi
                fw.op('act', [pk], [uk], lambda e: e.copy(out=u_[:, 15:527], in_=p_[0:64, :]))
                rel('pF', pk)
                fw.op('dve', [uk], ['s2'], lambda e: e.tensor_tensor(out=s2[:], in0=u_[:, 1:527], in1=u_[:, 0:526], op=ALU.add))
                sw = s2
                if wlen >= 4:
                    fw.op('dve', ['s2'], ['s4'], lambda e: e.tensor_tensor(out=s4[:], in0=s2[:, 2:526], in1=s2[:, 0:524], op=ALU.add))
                    sw = s4
                if wlen >= 8:
                    fw.op('dve', ['s4'], ['s8'], lambda e: e.tensor_tensor(out=s8[:], in0=s4[:, 4:524], in1=s4[:, 0:520], op=ALU.add))
                    sw = s8
                if wlen >= 16:
                    fw.op('dve', ['s8'], ['s16'], lambda e: e.tensor_tensor(out=s16[:], in0=s8[:, 8:520], in1=s8[:, 0:512], op=ALU.add))
                    sw = s16
                nsw = {2: 526, 4: 524, 8: 520, 16: 512}[wlen]
                swk = 's%d' % wlen
                d_, dk = dfb[gi], 'dfb%d' % gi
                if m == 0:
                    fw.op('dve', [swk, 'invc'], [swk], lambda e: e.tensor_tensor(out=sw[:, nsw - 512:nsw], in0=sw[:, nsw - 512:nsw], in1=invc[:, gi, :], op=ALU.mult))
                    fw.op('dve', [swk, uk], [dk], lambda e: e.tensor_tensor(out=d_[:], in0=sw[:, nsw - 512:nsw], in1=u_[:, 15:527], op=ALU.subtract))
                else:
                    fw.op('dve', [swk, uk], [dk], lambda e: e.scalar_tensor_tensor(out=d_[:], in0=sw[:, nsw - 512:nsw], scalar=1.0 / wlen, in1=u_[:, 15:527], op0=ALU.mult, op1=ALU.subtract))
                fw.op('pool', [uk], [uk], lambda e: e.tensor_copy(out=u_[:, 0:15], in_=u_[:, 512:527]))

            def pool_out(j):
                p_, pk = acq(pS, 'pS')
                for gi in range(4):
                    fw.op('pe', ['dfb%d' % gi, 'poolwb'], [pk], lambda e: e.matmul(p_[:, gi * 64:(gi + 1) * 64], lhsT=dfb[gi][:, j * 128:(j + 1) * 128], rhs=poolwb[:, gi, :], start=True, stop=True))
                yield
                fw.op('dve', [pk, 'pscale'], ['ycs'], lambda e: e.tensor_tensor(out=ycs[:, j, :], in0=p_[:, 0:256], in1=pscale[:], op=ALU.mult))
                rel('pS', pk)
                if j == 3:
                    fw.dma('sp', K.y.rearrange("(m j p) d -> m p j d", p=128, j=4)[m][:, :, 768:1024], ycs[:], ['ycs'], ['y_c'])

            v_, vk = v1s[par], 'v1s%d' % par
            vw_, vwk = v1w[par], 'v1w%d' % par

            def tok_a(j):
                p_, pk = acq(pF, 'pF')
                for kc in range(8):
                    fw.op('pe', [xTk, 'Wb'], [pk], lambda e: e.matmul(p_[:, 0:396], lhsT=xT_[:, kc, j * 128:(j + 1) * 128], rhs=Wb[:, kc, C_Z:C_Z + 396], start=(kc == 0), stop=(kc == 7)))
                yield
                fw.op('act', [pk], ['zst'], lambda e: e.activation(out=zst[:, j, :], in_=p_[:, 0:384], func=AF.Silu))
                fw.op('act', [pk], ['bgs'], lambda e: e.activation(out=bgs[:, j, 0:6], in_=p_[:, 384:390], func=AF.Sigmoid))
                fw.op('dve', [pk, 'dtb', 'zst', 'bgs'], ['tmpa'], lambda e: e.tensor_tensor(out=tmpa[:], in0=p_[:, 390:396], in1=dtb[:], op=ALU.add))
                fw.op('act', ['tmpa'], ['tmpa'], lambda e: e.activation(out=tmpa[:], in_=tmpa[:], func=AF.Exp))
                fw.op('act', ['tmpa'], ['tmpa'], lambda e: e.activation(out=tmpa[:], in_=tmpa[:], func=AF.Ln, bias=1.0))
                fw.op('dve', ['tmpa', 'nA'], ['bgs'], lambda e: e.tensor_tensor(out=bgs[:, j, 6:12], in0=tmpa[:], in1=nA[:], op=ALU.mult))
                rel('pF', pk)

            def tok_b(j):
                p_, pk = acq(pF, 'pF')
                for (c0, o0) in ((C_VS, 0), (C_VW, 128), (C_GATE, 256)):
                    nn = 18 if c0 == C_GATE else 128
                    for kc in range(8):
                        fw.op('pe', [xTk, 'Wb'], [pk], lambda e: e.matmul(p_[:, o0:o0 + nn], lhsT=xT_[:, kc, j * 128:(j + 1) * 128], rhs=Wb[:, kc, c0:c0 + nn], start=(kc == 0), stop=(kc == 7)))
                yield
                fw.op('act', [pk], ['gts'], lambda e: e.activation(out=gts[:, j, :], in_=p_[:, 256:274], func=AF.Sigmoid))
                fw.op('dve', [pk, 'gts'], [vk], lambda e: e.tensor_copy(out=v_[:, j, :].rearrange("p (h c) -> p h c", h=2)[:, :, 0:64],
                                                                     in_=p_[:, 0:128].rearrange("p (h c) -> p h c", h=2)))
                fw.op('dve', [pk, 'gts'], [vwk], lambda e: e.tensor_copy(out=vw_[:, j, :].rearrange("p (h c) -> p h c", h=2)[:, :, 0:64],
                                                                     in_=p_[:, 128:256].rearrange("p (h c) -> p h c", h=2)))
                rel('pF', pk)

            tasks = [gdn_chunk(ch) for ch in range(9)]
            tasks += [nsa_chunk(C_QB, ch, 0, K.qn) for ch in range(3)] + [nsa_chunk(C_KS, 0, 2, K.ksT), nsa_chunk(C_KW, 0, 3, K.kwT)]
            tasks += [raw_chunk(C_KC, K.kcT), raw_chunk(C_VC, K.vcT)]
            tasks += [pool_chunk(gi, wlen) for gi, wlen in enumerate((2, 4, 8, 16))]
            tasks += [None]
            tasks += [pool_out(j) for j in range(4)]
            for j in range(4):
                tasks += [tok_a(j), tok_b(j)]
            active = []
            ti = 0
            while ti < len(tasks) or active:
                while ti < len(tasks) and len(active) < 3:
                    if tasks[ti] is None:
                        if active:
                            break
                        ti += 1
                        continue
                    active.append(tasks[ti])
                    ti += 1
                for g_ in list(active):
                    try:
                        next(g_)
                    except StopIteration:
                        active.remove(g_)
            fw.dma('sp', K.zs.rearrange("(m j p) d -> m p j d", p=128, j=4)[m], zst[:], ['zst'], ['zs'])
            fw.dma('sp', K.bg.rearrange("(m j p) d -> m p j d", p=128, j=4)[m], bgs[:], ['bgs'], ['bg'])
            fw.dma('sp', K.gt.rearrange("(m j p) d -> m p j d", p=128, j=4)[m], gts[:], ['gts'], ['gt'])
            fw.dma('sp', K.vs1.rearrange("(m j p) d -> m p j d", p=128, j=4)[m], v_[:], [vk], ['vs1'])
            fw.dma('sp', K.vw1.rearrange("(m j p) d -> m p j d", p=128, j=4)[m], vw_[:], [vwk], ['vw1'])
        fw.barrier()


WNAMES = ["norm_mix", "w_in", "conv_w", "a_log", "dt_bias", "gdn_norm", "nsa_q_norm", "nsa_k_norm", "cmp_pos",
          "cmp_w1", "cmp_w2", "pool_w", "pool_scale", "w_out", "norm_ffn", "w_ffn1", "w_ffn2"]
WSHAPES = {"norm_mix": [2, 1024], "w_in": [2, 1024, 2974], "conv_w": [2, 4, 1152], "a_log": [2, 6], "dt_bias": [2, 6],
           "gdn_norm": [2, 64], "nsa_q_norm": [2, 64], "nsa_k_norm": [2, 3, 64], "cmp_pos": [2, 2, 32, 64],
           "cmp_w1": [2, 2, 2048, 256], "cmp_w2": [2, 2, 256, 64], "pool_w": [2, 4, 64, 64], "pool_scale": [2, 256],
           "w_out": [2, 1024, 1024], "norm_ffn": [2, 1024], "w_ffn1": [2, 1024, 4096], "w_ffn2": [2, 4096, 1024]}


def build(T, L=2, phases=("p1", "gdn", "cmp", "nsa", "p3"), dbg=False):
    nc = bass.Bass("TRN2", target_bir_lowering=False)
    K = Ctx()
    K.nc, K.T, K.L = nc, T, L
    K.cut = int(os.environ.get('GDN_CUT', '99'))
    K.f32r = os.environ.get('GDN_F32R', '0') == '1'
    dk = "ExternalOutput" if dbg else "Internal"
    K.w = {n: nc.dram_tensor(n, WSHAPES[n], F32, kind="ExternalInput").ap() for n in WNAMES}
    cs = _consts()
    K.c = {n: nc.dram_tensor("c_" + n, list(v.shape), BF16 if v.dtype != np.float32 else F32, kind="ExternalInput").ap() for n, v in cs.items()}
    x = nc.dram_tensor("x", [T, D], F32, kind="ExternalInput").ap()
    out = nc.dram_tensor("out", [T, D], F32, kind="ExternalOutput").ap()
    xs1 = nc.dram_tensor("xs1", [T, D], F32, kind=dk).ap()
    K.xin = [x, xs1]
    K.xout = [xs1, out] if L == 2 else [out]
    K.gqk = nc.dram_tensor("gqk", [1152, T], F32, kind=dk).ap()
    K.zs = nc.dram_tensor("zs", [T, 384], F32, kind=dk).ap()
    K.bg = nc.dram_tensor("bg", [T, 12], F32, kind=dk).ap()
    K.gt = nc.dram_tensor("gt", [T, 18], F32, kind=dk).ap()
    K.qn = nc.dram_tensor("qn", [384, T], BF16, kind=dk).ap()
    K.ksT = nc.dram_tensor("ksT", [128, T], BF16, kind=dk).ap()
    K.kwT = nc.dram_tensor("kwT", [128, T], BF16, kind=dk).ap()
    K.kcT = nc.dram_tensor("kcT", [128, T], BF16, kind=dk).ap()
    K.vcT = nc.dram_tensor("vcT", [128, T], BF16, kind=dk).ap()
    K.vs1 = nc.dram_tensor("vs1", [T, 130], BF16, kind=dk).ap()
    K.vw1 = nc.dram_tensor("vw1", [T, 130], BF16, kind=dk).ap()
    K.y = nc.dram_tensor("y", [T, D], F32, kind=dk).ap()
    Ns_, NcP_ = T // 64, T // 16
    K.KcT = nc.dram_tensor("KcT", [2, 64, NcP_], BF16, kind=dk).ap()
    K.Vc1 = nc.dram_tensor("Vc1", [2, NcP_, 64 + Ns_ + 1], BF16, kind=dk).ap()
    csn = _nsa_consts(T)
    K.cn = {n: nc.dram_tensor("cn_" + n, list(v.shape), BF16, kind="ExternalInput").ap() for n, v in csn.items()}
    K.consts_n = csn
    with ExitStack() as st:
        st.enter_context(nc.allow_non_contiguous_dma(reason="small parameter loads"))
        K.fw = FW(nc, st)
        for l in range(L):
            if "p1" in phases:
                phase_p1(K, l)
            if "gdn" in phases:
                phase_gdn(K, l)
            if "cmp" in phases:
                phase_cmp(K, l)
            if "nsa" in phases:
                phase_nsa(K, l)
            if "p3" in phases:
                phase_p3(K, l)
        K.fw.finish()
    K.consts = cs
    return nc, K


def phase_gdn(K, l):
    nc, fw, T = K.nc, K.fw, K.T
    NCH = T // 64
    F32R = mybir.dt.float32r
    R_ = (lambda ap: ap.bitcast(F32R)) if K.f32r else (lambda ap: ap)
    with ExitStack() as st:
        sb = lambda name, shape, dt=F32: st.enter_context(nc.sbuf_tensor(name + "_L%d" % l, shape, dt))
        identf = sb("g_identf", [64, 64])
        ones64 = sb("g_ones", [64, 64])
        triU = sb("g_triU", [64, 64])
        biasL = sb("g_biasL", [64, 64])
        biasU = sb("g_biasU", [64, 64])
        gnorm = sb("g_gnorm", [64, 64])
        Fq = [sb("g_F%d" % i, [64, 18, 512]) for i in range(2)]
        Zs = [sb("g_Z%d" % i, [64, 8, 384]) for i in range(1)]
        BG = [sb("g_BG%d" % i, [64, 8, 12]) for i in range(2)]
        Yst = [sb("g_Y%d" % i, [64, 8, 384]) for i in range(1)]
        S = [sb("g_S%d" % i, [64, 6, 64]) for i in range(2)]
        Sb = [sb("g_Sb%d" % i, [64, 6, 64], BF16) for i in range(2)]
        names = ["KVt", "rhsG", "gcol", "D1", "D2", "E1", "E2", "egrow", "egcol", "kdsc", "nbeta", "bege", "N", "qkT", "vb", "kbg", "kd", "qdT",
                 "Mp0", "Mp1", "Np0", "Np1", "Tt0", "Tt1", "U", "WT", "vnew", "tmpS", "O", "sqo", "ssum", "Yt", "Gs", "Nb", "Fb"]
        BFN = ("Mp0", "Mp1", "Np0", "Np1", "Tt0", "Tt1", "vb", "kbg", "kd", "qdT", "qkT", "WT", "vnew", "Nb", "Fb")
        W2 = {}
        for nm in names:
            shape = [64, 12, 64] if nm in ("KVt", "Fb") else [64, 390] if nm == "Gs" else ([64, 6] if nm in ("gcol", "egcol", "kdsc", "nbeta", "bege", "ssum") else [64, 6, 64])
            W2[nm] = [sb("g_%s%d" % (nm, i), shape, BF16 if nm in BFN else F32) for i in range(2)]
        banks = [st.enter_context(nc.psum_tensor("g_ps%d_L%d" % (i, l), [64, 512], F32)) for i in range(8)]
        bc = [0]

        def bank():
            bc[0] += 1
            i = bc[0] % 8
            return banks[i], 'gps%d' % i

        fw.dma('sp', identf[:], K.c['identf'][0:64, 0:64], [], ['identf'])
        fw.dma('sp', triU[:], K.c['triU'], [], ['triU'])
        fw.dma('sp', biasL[:], K.c['biasL'], [], ['biasL'])
        fw.dma('sp', biasU[:], K.c['biasU'], [], ['biasU'])
        fw.dma('sp', gnorm[:], K.w['gdn_norm'][l:l + 1, :].partition_broadcast(64), [], ['gnorm'])
        fw.op('pool', [], ['ones64'], lambda e: e.memset(ones64[:], 1.0))
        fw.op('pool', [], ['S0'], lambda e: e.memset(S[0][:], 0.0))
        fw.op('pool', [], ['Sb0'], lambda e: e.memset(Sb[0][:], 0.0))

        gq = K.gqk.rearrange("(s p) t -> p s t", p=64)
        zsr = K.zs.rearrange("(n c) f -> c n f", c=64)
        bgr = K.bg.rearrange("(n c) f -> c n f", c=64)
        yr = K.y.rearrange("(n c) f -> c n f", c=64)

        def b3(ap2, n=64):
            return ap2.unsqueeze(2).to_broadcast([64, 6, n])

        def m3(ap2):
            return ap2.unsqueeze(1).to_broadcast([64, 6, 64])

        state = {}

        def load_group(gi):
            par = gi % 2
            fw.dma('sp', Fq[par][:], gq[:, :, gi * 512:(gi + 1) * 512], [], ['F%d' % par])
            fw.dma('sp', BG[par][:], bgr[:, gi * 8:(gi + 1) * 8, :], [], ['BG%d' % par])

        def prep(n):
            gi, j = n // 8, n % 8
            gp = gi % 2
            p = n % 2
            F_, Fk = Fq[gp], 'F%d' % gp
            BG_, BGk = BG[gp], 'BG%d' % gp
            t = {nm: W2[nm][p] for nm in names}
            k_ = lambda nm: '%s%d' % (nm, p)
            cs = slice(j * 64, (j + 1) * 64)
            qT = F_[:, 0:6, cs]
            kT = F_[:, 6:12, cs]
            beta = BG_[:, j, 0:6]
            g = BG_[:, j, 6:12]
            fw.op('pool', [Fk], [k_('Fb')], lambda e: e.tensor_copy(out=t['Fb'][:], in_=F_[:, 0:12, cs]))
            pk1, pk1k = bank()
            pv1, pv1k = bank()
            for h in range(6):
                fw.op('pe', [Fk, 'identf'], [pk1k], lambda e: e.transpose(out=pk1[:, h * 64:(h + 1) * 64], in_=F_[:, 6 + h, cs], identity=identf[:]))
            for h in range(6):
                fw.op('pe', [Fk, 'identf'], [pv1k], lambda e: e.transpose(out=pv1[:, h * 64:(h + 1) * 64], in_=F_[:, 12 + h, cs], identity=identf[:]))
            fw.op('act', [pk1k], [k_('KVt')], lambda e: e.copy(out=t['KVt'][:, 0:6, :], in_=pk1[:, 0:384].rearrange("p (h c) -> p h c", h=6)))
            fw.op('act', [pv1k], [k_('KVt')], lambda e: e.copy(out=t['KVt'][:, 6:12, :], in_=pv1[:, 0:384].rearrange("p (h c) -> p h c", h=6)))
            yield
            pKK, pKKk = bank()
            pQK, pQKk = bank()
            for h in range(6):
                fw.op('pe', [k_('Fb')], [pKKk], lambda e: e.matmul(pKK[:, h * 64:(h + 1) * 64], lhsT=t['Fb'][:, 6 + h, :], rhs=t['Fb'][:, 6 + h, :], start=True, stop=True))
            for h in range(6):
                fw.op('pe', [k_('Fb')], [pQKk], lambda e: e.matmul(pQK[:, h * 64:(h + 1) * 64], lhsT=t['Fb'][:, 6 + h, :], rhs=t['Fb'][:, h, :], start=True, stop=True))
            yield
            fw.op('dve', [BGk, 'triU'], [k_('rhsG')], lambda e: e.tensor_tensor(out=t['rhsG'][:], in0=b3(g), in1=m3(triU[:]), op=ALU.mult))
            pG, pGk = bank()
            fw.op('pe', [k_('rhsG'), 'ones64'], [pGk], lambda e: e.matmul(pG[:, 0:384], lhsT=R_(ones64[:]), rhs=R_(t['rhsG'][:].rearrange("p h c -> p (h c)")), start=True, stop=True))
            fw.op('pe', [BGk, 'triU'], [pGk], lambda e: e.matmul(pG[:, 384:390], lhsT=R_(triU[:]), rhs=R_(g), start=True, stop=True))
            fw.op('act', [pGk], [k_('Gs')], lambda e: e.copy(out=t['Gs'][:], in_=pG[:, 0:390]))
            pGk = k_('Gs')
            pG3 = t['Gs'][:, 0:384].rearrange("p (h c) -> p h c", h=6)
            fw.op('pool', [pGk], [k_('gcol')], lambda e: e.tensor_copy(out=t['gcol'][:], in_=t['Gs'][:, 384:390]))
            fw.op('dve', [pGk, k_('gcol')], [k_('D1')], lambda e: e.tensor_tensor(out=t['D1'][:], in0=b3(t['gcol'][:]), in1=pG3, op=ALU.subtract))
            fw.op('pool', [k_('D1'), 'biasU'], [k_('D2')], lambda e: e.tensor_tensor(out=t['D2'][:], in0=m3(biasU[:]), in1=t['D1'][:], op=ALU.subtract))
            fw.op('dve', [k_('D1'), 'biasL'], [k_('D1')], lambda e: e.tensor_tensor(out=t['D1'][:], in0=t['D1'][:], in1=m3(biasL[:]), op=ALU.add))
            fw.op('act', [k_('D1')], [k_('E1')], lambda e: e.activation(out=t['E1'][:], in_=t['D1'][:], func=AF.Exp))
            fw.op('act', [k_('D2')], [k_('E2')], lambda e: e.activation(out=t['E2'][:], in_=t['D2'][:], func=AF.Exp))
            fw.op('act', [pGk], [k_('egrow')], lambda e: e.activation(out=t['egrow'][:], in_=pG3, func=AF.Exp))
            fw.op('act', [k_('gcol')], [k_('egcol')], lambda e: e.activation(out=t['egcol'][:], in_=t['gcol'][:], func=AF.Exp))
            fw.op('dve', [pGk, k_('gcol')], [k_('kdsc')], lambda e: e.tensor_tensor(out=t['kdsc'][:], in0=pG3[:, :, 63], in1=t['gcol'][:], op=ALU.subtract))
            fw.op('act', [k_('kdsc')], [k_('kdsc')], lambda e: e.activation(out=t['kdsc'][:], in_=t['kdsc'][:], func=AF.Exp))
            fw.op('pool', [BGk], [k_('nbeta')], lambda e: e.tensor_scalar(out=t['nbeta'][:], in0=beta, scalar1=-1.0, scalar2=None, op0=ALU.mult))
            fw.op('pool', [BGk, k_('egcol')], [k_('bege')], lambda e: e.tensor_tensor(out=t['bege'][:], in0=beta, in1=t['egcol'][:], op=ALU.mult))
            yield
            fw.op('dve', [pKKk, k_('E1')], [k_('N')], lambda e: e.tensor_tensor(out=t['N'][:], in0=pKK[:, 0:384].rearrange("p (h c) -> p h c", h=6), in1=t['E1'][:], op=ALU.mult))
            fw.op('dve', [k_('N'), k_('nbeta')], [k_('N')], lambda e: e.tensor_tensor(out=t['N'][:], in0=t['N'][:], in1=b3(t['nbeta'][:]), op=ALU.mult))
            fw.op('pool', [k_('N')], [k_('Nb')], lambda e: e.tensor_copy(out=t['Nb'][:], in_=t['N'][:]))
            fw.op('dve', [pQKk, k_('E2')], [k_('qkT')], lambda e: e.tensor_tensor(out=t['qkT'][:], in0=pQK[:, 0:384].rearrange("p (h c) -> p h c", h=6), in1=t['E2'][:], op=ALU.mult))
            fw.op('pool', [k_('KVt'), BGk], [k_('vb')], lambda e: e.tensor_tensor(out=t['vb'][:], in0=t['KVt'][:, 6:12, :], in1=b3(beta), op=ALU.mult))
            fw.op('pool', [k_('KVt'), k_('bege')], [k_('kbg')], lambda e: e.tensor_tensor(out=t['kbg'][:], in0=t['KVt'][:, 0:6, :], in1=b3(t['bege'][:]), op=ALU.mult))
            fw.op('pool', [k_('KVt'), k_('kdsc')], [k_('kd')], lambda e: e.tensor_tensor(out=t['kd'][:], in0=t['KVt'][:, 0:6, :], in1=b3(t['kdsc'][:]), op=ALU.mult))
            fw.op('pool', [Fk, k_('egrow')], [k_('qdT')], lambda e: e.tensor_tensor(out=t['qdT'][:], in0=qT, in1=t['egrow'][:], op=ALU.mult))
            yield
            pM, pMk = bank()
            for h in range(6):
                fw.op('pe', [k_('N'), 'identf'], [pMk], lambda e: e.transpose(out=pM[:, h * 64:(h + 1) * 64], in_=t['N'][:, h, :], identity=identf[:]))
            pM3 = pM[:, 0:384].rearrange("p (h c) -> p h c", h=6)
            fw.op('act', [pMk], [k_('Mp0')], lambda e: e.copy(out=t['Mp0'][:], in_=pM3))
            fw.op('pool', [k_('Mp0'), 'identf'], [k_('Tt0')], lambda e: e.tensor_tensor(out=t['Tt0'][:], in0=t['Mp0'][:], in1=m3(identf[:]), op=ALU.add))
            yield
            Np, Npk = t['Nb'], k_('Nb')
            Mp, Mpk = t['Mp0'], k_('Mp0')
            Tt, Ttk = t['Tt0'], k_('Tt0')
            for s_ in range(5):
                Nn, Nnk = t['Np%d' % (s_ % 2)], k_('Np%d' % (s_ % 2))
                Mn, Mnk = t['Mp%d' % ((s_ + 1) % 2)], k_('Mp%d' % ((s_ + 1) % 2))
                Tn, Tnk = t['Tt%d' % ((s_ + 1) % 2)], k_('Tt%d' % ((s_ + 1) % 2))
                pN2, pN2k = bank()
                for h in range(6):
                    fw.op('pe', [Mpk, Npk], [pN2k], lambda e: e.matmul(pN2[:, h * 64:(h + 1) * 64], lhsT=R_(Mp[:, h, :]), rhs=R_(Np[:, h, :]), start=True, stop=True))
                if s_ < 4:
                    pM2, pM2k = bank()
                    for h in range(6):
                        fw.op('pe', [Mpk, Npk], [pM2k], lambda e: e.matmul(pM2[:, h * 64:(h + 1) * 64], lhsT=R_(Np[:, h, :]), rhs=R_(Mp[:, h, :]), start=True, stop=True))
                yield
                fw.op('act', [pN2k], [Nnk], lambda e: e.copy(out=Nn[:], in_=pN2[:, 0:384].rearrange("p (h c) -> p h c", h=6)))
                if s_ < 4:
                    fw.op('dve', [pM2k], [Mnk], lambda e: e.tensor_copy(out=Mn[:], in_=pM2[:, 0:384].rearrange("p (h c) -> p h c", h=6)))
                pT_, pTk = bank()
                for h in range(6):
                    fw.op('pe', [Nnk, Ttk], [pTk], lambda e: e.matmul(pT_[:, h * 64:(h + 1) * 64], lhsT=R_(Nn[:, h, :]), rhs=R_(Tt[:, h, :]), start=True, stop=True))
                yield
                fw.op('dve', [pTk, Ttk], [Tnk], lambda e: e.tensor_tensor(out=Tn[:], in0=pT_[:, 0:384].rearrange("p (h c) -> p h c", h=6), in1=Tt[:], op=ALU.add))
                Np, Npk, Mp, Mpk, Tt, Ttk = Nn, Nnk, Mn, Mnk, Tn, Tnk
            yield
            pU, pUk = bank()
            pW, pWk = bank()
            for h in range(6):
                fw.op('pe', [Ttk, k_('vb')], [pUk], lambda e: e.matmul(pU[:, h * 64:(h + 1) * 64], lhsT=R_(Tt[:, h, :]), rhs=R_(t['vb'][:, h, :]), start=True, stop=True))
            for h in range(6):
                fw.op('pe', [Ttk, k_('kbg')], [pWk], lambda e: e.matmul(pW[:, h * 64:(h + 1) * 64], lhsT=R_(t['kbg'][:, h, :]), rhs=R_(Tt[:, h, :]), start=True, stop=True))
            yield
            fw.op('act', [pUk], [k_('U')], lambda e: e.copy(out=t['U'][:], in_=pU[:, 0:384].rearrange("p (h c) -> p h c", h=6)))
            fw.op('dve', [pWk], [k_('WT')], lambda e: e.tensor_copy(out=t['WT'][:], in_=pW[:, 0:384].rearrange("p (h c) -> p h c", h=6)))

        def scan(n):
            gi, j = n // 8, n % 8
            gp = gi % 2
            p = n % 2
            t = {nm: W2[nm][p] for nm in names}
            k_ = lambda nm: '%s%d' % (nm, p)
            if j == 0:
                fw.dma('pool', Zs[0][:], zsr[:, gi * 8:(gi + 1) * 8, :], [], ['Z0'])
            So, Sok = S[n % 2], 'S%d' % (n % 2)
            Sn, Snk = S[(n + 1) % 2], 'S%d' % ((n + 1) % 2)
            Sob, Sobk = Sb[n % 2], 'Sb%d' % (n % 2)
            Snb, Snbk = Sb[(n + 1) % 2], 'Sb%d' % ((n + 1) % 2)
            pWS, pWSk = bank()
            for h in range(6):
                fw.op('pe', [k_('WT'), Sobk], [pWSk], lambda e: e.matmul(pWS[:, h * 64:(h + 1) * 64], lhsT=t['WT'][:, h, :], rhs=Sob[:, h, :], start=True, stop=True))
            yield
            fw.op('dve', [pWSk, k_('U')], [k_('vnew')], lambda e: e.tensor_tensor(out=t['vnew'][:], in0=t['U'][:], in1=pWS[:, 0:384].rearrange("p (h c) -> p h c", h=6), op=ALU.subtract))
            pdS, pdSk = bank()
            for h in range(6):
                fw.op('pe', [k_('kd'), k_('vnew')], [pdSk], lambda e: e.matmul(pdS[:, h * 64:(h + 1) * 64], lhsT=R_(t['kd'][:, h, :]), rhs=R_(t['vnew'][:, h, :]), start=True, stop=True))
            pO, pOk = bank()
            for h in range(6):
                fw.op('pe', [k_('qdT'), Sobk], [pOk], lambda e: e.matmul(pO[:, h * 64:(h + 1) * 64], lhsT=t['qdT'][:, h, :], rhs=Sob[:, h, :], start=True, stop=False))
                fw.op('pe', [k_('qkT'), k_('vnew')], [pOk], lambda e: e.matmul(pO[:, h * 64:(h + 1) * 64], lhsT=R_(t['qkT'][:, h, :]), rhs=R_(t['vnew'][:, h, :]), start=False, stop=True))
            yield
            fw.op('pool', [Sok, k_('egrow')], [k_('tmpS')], lambda e: e.tensor_tensor(out=t['tmpS'][:], in0=So[:], in1=t['egrow'][:, :, 63:64].to_broadcast([64, 6, 64]), op=ALU.mult))
            fw.op('dve', [k_('tmpS'), pdSk], [Snk], lambda e: e.tensor_tensor(out=Sn[:], in0=t['tmpS'][:], in1=pdS[:, 0:384].rearrange("p (h c) -> p h c", h=6), op=ALU.add))
            fw.op('pool', [Snk], [Snbk], lambda e: e.tensor_copy(out=Snb[:], in_=Sn[:]))
            yield
            fw.op('act', [pOk], [k_('O')], lambda e: e.copy(out=t['O'][:], in_=pO[:, 0:384].rearrange("p (h c) -> p h c", h=6)))
            fw.op('pool', [k_('O')], [k_('sqo')], lambda e: e.tensor_tensor(out=t['sqo'][:], in0=t['O'][:], in1=t['O'][:], op=ALU.mult))
            fw.op('dve', [k_('sqo')], [k_('ssum')], lambda e: e.tensor_reduce(out=t['ssum'][:], in_=t['sqo'][:], axis=AX.X, op=ALU.add))
            fw.op('dve', [k_('ssum')], [k_('ssum')], lambda e: e.tensor_scalar(out=t['ssum'][:], in0=t['ssum'][:], scalar1=1.0 / 64, scalar2=EPS, op0=ALU.mult, op1=ALU.add))
            fw.op('act', [k_('ssum')], [k_('ssum')], lambda e: e.activation(out=t['ssum'][:], in_=t['ssum'][:], func=AF.Sqrt))
            fw.op('dve', [k_('ssum')], [k_('ssum')], lambda e: e.reciprocal(out=t['ssum'][:], in_=t['ssum'][:]))
            fw.op('pool', [k_('O'), k_('ssum')], [k_('Yt')], lambda e: e.tensor_tensor(out=t['Yt'][:], in0=t['O'][:], in1=b3(t['ssum'][:]), op=ALU.mult))
            fw.op('pool', [k_('Yt'), 'gnorm'], [k_('Yt')], lambda e: e.tensor_tensor(out=t['Yt'][:], in0=t['Yt'][:], in1=m3(gnorm[:]), op=ALU.mult))
            fw.op('dve', [k_('Yt'), 'Z0'], ['Y0'], lambda e: e.tensor_tensor(out=Yst[0][:, j, :].rearrange("p (h c) -> p h c", h=6), in0=t['Yt'][:], in1=Zs[0][:, j, :].rearrange("p (h c) -> p h c", h=6), op=ALU.mult))
            if j == 7:
                fw.dma('pool', yr[:, gi * 8:(gi + 1) * 8, 0:384], Yst[0][:], ['Y0'], ['y_a'])

        def run_il(gens):
            gens = list(gens)
            while gens:
                for g_ in list(gens):
                    try:
                        next(g_)
                    except StopIteration:
                        gens.remove(g_)

        load_group(0)
        run_il([prep(0)])
        for n in range(NCH):
            tasks = [scan(n)]
            if n + 1 < NCH:
                if (n + 1) % 8 == 0:
                    load_group((n + 1) // 8)
                tasks.append(prep(n + 1))
            run_il(tasks)
        fw.barrier()


def _nsa_consts(T):
    c = {}
    Ns = T // 64
    NcP = T // 16
    Nc = NcP - 1
    WV = 64 + Ns + 1
    cc = np.arange(NcP)
    nn = np.arange(Ns)
    mat = ((cc[:, None] >= 4 * nn[None, :] - 1) & (cc[:, None] <= 4 * nn[None, :] + 3) & (cc[:, None] < Nc)).astype(np.float32)
    v1 = np.zeros((NcP, WV), np.float32)
    v1[:, 64:64 + Ns] = mat
    v1[:, WV - 1] = 1.0
    c['vc1init'] = v1.astype(ml_dtypes.bfloat16)
    cl = np.arange(128)
    q = np.arange(128)
    cm = np.zeros((128, 17, 3, 128), np.float32)
    for idx in range(17):
        ok = (16 * cl[:, None] + 31) <= (128 * idx + q[None, :])
        cm[:, idx, :, :] = np.where(ok, 0.0, NEG)[:, None, :]
    c['cmask'] = cm.reshape(128, 17, 384).astype(ml_dtypes.bfloat16)
    c['causb'] = np.tile(np.where(cl[:, None] <= q[None, :], 0.0, NEG), (1, 3)).astype(ml_dtypes.bfloat16)
    c['bandb'] = np.tile(np.where(cl[:, None] > q[None, :], 0.0, NEG), (1, 3)).astype(ml_dtypes.bfloat16)
    TE = min(T, 4096)
    t = np.arange(TE)
    c['exT'] = (((t[None, :] // 64) % 64) == np.arange(64)[:, None]).astype(np.float32).astype(ml_dtypes.bfloat16)
    return c


def phase_cmp(K, l):
    nc, fw, T = K.nc, K.fw, K.T
    NcP = T // 16
    Ns = T // 64
    WV = 64 + Ns + 1
    with ExitStack() as st:
        sb = lambda name, shape, dt=F32: st.enter_context(nc.sbuf_tensor(name + "_L%d" % l, shape, dt))
        ps = lambda name, shape, dt=F32: st.enter_context(nc.psum_tensor(name + "_L%d" % l, shape, dt))
        w1f = sb("c_w1f", [64, 32, 256])
        w1b = sb("c_w1b", [64, 32, 256], BF16)
        w2f = sb("c_w2f", [128, 2, 64])
        w2b = sb("c_w2b", [128, 2, 64], BF16)
        posf = sb("c_posf", [64, 32])
        posb = sb("c_posb", [64, 32], BF16)
        bias1 = sb("c_bias1", [128, 2])
        XT = sb("c_XT", [64, T + 16], BF16)
        h1T = sb("c_h1T", [128, 2, 512], BF16)
        ones64 = sb("c_ones", [64, 64])
        kg = sb("c_kg", [64, 1])
        sq = sb("c_sq", [64, 512])
        rn = sb("c_rn", [64, 512])
        ko = sb("c_ko", [64, 512], BF16)
        vst = sb("c_vst", [128, WV], BF16)
        pH = [ps("c_pH%d" % i, [128, 512]) for i in range(2)]
        pB = ps("c_pB", [128, 512])
        pK = ps("c_pK", [128, 512])
        pS = ps("c_pS", [128, 512])
        fw.op('pool', [], ['ones64'], lambda e: e.memset(ones64[:], 1.0))
        fw.dma('sp', kg[:], K.w['nsa_k_norm'][l, 0].rearrange("(p o) -> p o", o=1), [], ['kg'])
        for kv in range(2):
            fw.dma('sp', w1f[:], K.w['cmp_w1'][l, kv].rearrange("(l d) j -> d l j", d=64), [], ['w1f'])
            fw.op('dve', ['w1f'], ['w1b'], lambda e: e.tensor_copy(out=w1b[:], in_=w1f[:]))
            fw.dma('sp', w2f[:], K.w['cmp_w2'][l, kv].rearrange("(c p) d -> p c d", p=128), [], ['w2f'])
            fw.op('dve', ['w2f'], ['w2b'], lambda e: e.tensor_copy(out=w2b[:], in_=w2f[:]))
            fw.dma('sp', posf[:], K.w['cmp_pos'][l, kv].rearrange("l d -> d l"), [], ['posf'])
            fw.op('dve', ['posf'], ['posb'], lambda e: e.tensor_copy(out=posb[:], in_=posf[:]))
            for jc in range(2):
                for ll in range(32):
                    fw.op('pe', ['w1b', 'posb'], ['pB'], lambda e: e.matmul(pB[:, jc:jc + 1], lhsT=w1b[:, ll, jc * 128:(jc + 1) * 128], rhs=posb[:, ll:ll + 1], start=(ll == 0), stop=(ll == 31)))
            fw.op('act', ['pB'], ['bias1'], lambda e: e.copy(out=bias1[:], in_=pB[:, 0:2]))
            src = K.kcT if kv == 0 else K.vcT
            for hk in range(2):
                fw.dma('sp', XT[:, 0:T], src[hk * 64:(hk + 1) * 64, :], [], ['XT'])
                fw.op('pool', [], ['XT'], lambda e: e.memset(XT[:, T:T + 16], 0.0))
                for n0 in range(0, NcP, 512):
                    nn = min(512, NcP - n0)
                    for jc in range(2):
                        p_, pk = pH[jc], 'pH%d' % jc
                        for ll in range(32):
                            fw.op('pe', ['w1b', 'XT'], [pk], lambda e: e.matmul(p_[:, 0:nn], lhsT=w1b[:, ll, jc * 128:(jc + 1) * 128],
                                                                                rhs=XT[:, ll + 16 * n0:ll + 16 * (n0 + nn - 1) + 1:16], start=(ll == 0), stop=(ll == 31)))
                        fw.op('act', [pk, 'bias1'], ['h1T'], lambda e: e.activation(out=h1T[:, jc, 0:nn], in_=p_[:, 0:nn], func=AF.Silu, bias=bias1[:, jc:jc + 1]))
                    if kv == 0:
                        for jc in range(2):
                            fw.op('pe', ['h1T', 'w2b'], ['pK'], lambda e: e.matmul(pK[0:64, 0:nn], lhsT=w2b[:, jc, :], rhs=h1T[:, jc, 0:nn], start=(jc == 0), stop=(jc == 1)))
                        fw.op('act', ['pK'], ['sq'], lambda e: e.activation(out=sq[:, 0:nn], in_=pK[0:64, 0:nn], func=AF.Square))
                        fw.op('pe', ['sq', 'ones64'], ['pS'], lambda e: e.matmul(pS[0:64, 0:nn], lhsT=ones64[:], rhs=sq[:, 0:nn], start=True, stop=True))
                        fw.op('dve', ['pS'], ['rn'], lambda e: e.tensor_scalar(out=rn[:, 0:nn], in0=pS[0:64, 0:nn], scalar1=1.0 / 64, scalar2=EPS, op0=ALU.mult, op1=ALU.add))
                        fw.op('act', ['rn'], ['rn'], lambda e: e.activation(out=rn[:, 0:nn], in_=rn[:, 0:nn], func=AF.Sqrt))
                        fw.op('dve', ['rn'], ['rn'], lambda e: e.reciprocal(out=rn[:, 0:nn], in_=rn[:, 0:nn]))
                        fw.op('dve', ['pK', 'rn', 'kg'], ['ko'], lambda e: e.scalar_tensor_tensor(out=ko[:, 0:nn], in0=pK[0:64, 0:nn], scalar=kg[:, 0:1], in1=rn[:, 0:nn], op0=ALU.mult, op1=ALU.mult))
                        fw.dma('pool', K.KcT[hk, :, n0:n0 + nn], ko[:, 0:nn], ['ko'], ['KcT'])
                    else:
                        for c0 in range(0, nn, 128):
                            cn = min(128, nn - c0)
                            for jc in range(2):
                                fw.op('pe', ['h1T', 'w2b'], ['pK'], lambda e: e.matmul(pK[0:cn, 0:64], lhsT=h1T[:, jc, c0:c0 + cn], rhs=w2b[:, jc, :], start=(jc == 0), stop=(jc == 1)))
                            fw.dma('sp', vst[0:cn, :], K.cn['vc1init'][n0 + c0:n0 + c0 + cn, :], [], ['vst'])
                            fw.op('act', ['pK'], ['vst'], lambda e: e.copy(out=vst[0:cn, 0:64], in_=pK[0:cn, 0:64]))
                            fw.dma('pool', K.Vc1[hk, n0 + c0:n0 + c0 + cn, :], vst[0:cn, :], ['vst'], ['Vc1'])
        fw.barrier()


def phase_nsa(K, l):
    nc, fw, T = K.nc, K.fw, K.T
    NcP = T // 16
    Ns = T // 64
    WV = 64 + Ns + 1
    NQ = T // 128
    NCC = (NcP + 127) // 128
    NG = (Ns + 63) // 64
    SW = max(64 + Ns, 128 * 1 if Ns < 64 else 64 + Ns)
    TE = min(T, 4096)
    with ExitStack() as st:
        sb = lambda name, shape, dt=F32: st.enter_context(nc.sbuf_tensor(name + "_L%d" % l, shape, dt))
        ps = lambda name, shape, dt=F32: st.enter_context(nc.psum_tensor(name + "_L%d" % l, shape, dt))
        identb = sb("n_identb", [128, 128], BF16)
        identf = sb("n_identf", [128, 128])
        cmask = sb("n_cmask", [128, 17, 384], BF16)
        causb = sb("n_causb", [128, 384], BF16)
        bandb = sb("n_bandb", [128, 384], BF16)
        KX = sb("n_KX", [128, T], BF16)
        KwT = sb("n_KwT", [64, T], BF16)
        Vs1 = sb("n_Vs1", [128, NQ, 65], BF16)
        Vw1 = sb("n_Vw1", [128, NQ, 65], BF16)
        KcT = sb("n_KcT", [64, NCC * 128], BF16)
        Vc1 = sb("n_Vc1", [128, NCC, WV], BF16)
        QX = [sb("n_QX%d" % i, [128, NG, 3, 128], BF16) for i in range(2)]
        gts = [sb("n_gt%d" % i, [128, 18]) for i in range(2)]
        E = [sb("n_E%d" % i, [128, 384], BF16) for i in range(4)]
        rZ = sb("n_rZ", [128, 3])
        coef = sb("n_coef", [128, 3])
        imp = sb("n_imp", [128, Ns])
        imp2 = sb("n_imp2", [128, Ns])
        m1 = sb("n_m1", [128, 8])
        m2 = sb("n_m2", [128, 8])
        sel = sb("n_sel", [128, Ns])
        nsel = sb("n_nsel", [128, SW], BF16)
        nselT = [sb("n_nselT%d" % i, [128, SW], BF16) for i in range(2)]
        OT = [sb("n_OT%d" % i, [65, 384]) for i in range(2)]
        rd = sb("n_rd", [128, 1])
        Yb = [sb("n_Yb%d" % i, [128, 3, 64]) for i in range(2)]
        pS = [ps("n_pS%d" % i, [128, 512]) for i in range(3)]
        pOC = [ps("n_pOC%d" % i, [128, 512]) for i in range(3)]
        pOs = ps("n_pOs", [128, 512])
        pOw = pOs
        pTn = ps("n_pTn", [128, 128], BF16)

        fw.dma('sp', identb[:], K.c['identb'], [], ['identb'])
        fw.dma('sp', identf[:], K.c['identf'], [], ['identf'])
        fw.dma('sp', cmask[:], K.cn['cmask'], [], ['cmask'])
        fw.dma('sp', causb[:], K.cn['causb'], [], ['causb'])
        fw.dma('sp', bandb[:], K.cn['bandb'], [], ['bandb'])
        fw.op('pool', [], ['nsel'], lambda e: e.memset(nsel[:], 0.0))
        fw.op('pool', [], ['KcT'], lambda e: e.memset(KcT[:], 0.0))
        cnt = [0]
        rc = {}

        def rot(lst, key):
            rc[key] = rc.get(key, -1) + 1
            i = rc[key] % len(lst)
            return lst[i], '%s%d' % (key, i)

        yq = K.y.rearrange("(i p) d -> i p d", p=128)
        gtq = K.gt.rearrange("(i p) d -> i p d", p=128)
        for hk in range(2):
            fw.dma('sp', KX[0:64, :], K.ksT[hk * 64:(hk + 1) * 64, :], [], ['KX'])
            for t0 in range(0, T, TE):
                fw.dma('sp', KX[64:128, t0:t0 + TE], K.cn['exT'], [], ['KX'])
            fw.dma('sp', KwT[:], K.kwT[hk * 64:(hk + 1) * 64, :], [], ['KwT'])
            fw.dma('sp', Vs1[:], K.vs1.rearrange("(c p) w -> p c w", p=128)[:, :, hk * 65:(hk + 1) * 65], [], ['Vs1'])
            fw.dma('sp', Vw1[:], K.vw1.rearrange("(c p) w -> p c w", p=128)[:, :, hk * 65:(hk + 1) * 65], [], ['Vw1'])
            fw.dma('sp', KcT[:, 0:NcP], K.KcT[hk], [], ['KcT'])
            if NcP >= 128:
                fw.dma('sp', Vc1[:], K.Vc1[hk].rearrange("(c p) w -> p c w", p=128), [], ['Vc1'])
            else:
                fw.op('pool', [], ['Vc1'], lambda e: e.memset(Vc1[:], 0.0))
                fw.dma('sp', Vc1[0:NcP, 0, :], K.Vc1[hk], [], ['Vc1'])
            def run_branch(tiles, score, pv, look=2):
                pend = [score(tk) for tk in tiles[:look]]
                for ti, tk in enumerate(tiles):
                    p_, pk = pend.pop(0)
                    if ti + look < len(tiles):
                        pend.append(score(tiles[ti + look]))
                    e_, ek = rot(E, 'E')
                    fw.op('act', [pk], [ek], lambda e: e.activation(out=e_[:], in_=p_[:, 0:384], func=AF.Exp))
                    pv(tk, e_, ek)

            def partA(i):
                par = i % 2
                Q_, Qk = QX[par], 'QX%d' % par
                g_, gk = gts[par], 'gt%d' % par
                Y_, Yk = Yb[par], 'Yb%d' % par
                ngrp = (2 * i + 1) // 64 + 1
                for grp in range(ngrp):
                    fw.dma('sp', Q_[0:64, grp, :, :], K.qn[hk * 192:(hk + 1) * 192, i * 128:(i + 1) * 128].rearrange("(g d) q -> d g q", d=64), [], [Qk])
                fw.dma('sp', g_[:], gtq[i], [], [gk])
                q0 = Q_[0:64, 0, :, :].rearrange("p g q -> p (g q)")
                jmax = (8 * i + 6) // 128

                def cmp_score(jc):
                    p_, pk = rot(pS, 'pS')
                    full = (16 * (128 * jc + 127) + 31) <= 128 * i
                    fw.op('pe', ['KcT', Qk], [pk], lambda e: e.matmul(p_[:, 0:384], lhsT=KcT[:, jc * 128:(jc + 1) * 128], rhs=q0, start=True, stop=full))
                    if not full:
                        idx = (128 * i - 2048 * jc) // 128
                        fw.op('pe', ['identb', 'cmask'], [pk], lambda e: e.matmul(p_[:, 0:384], lhsT=identb[:], rhs=cmask[:, idx, :], start=False, stop=True))
                    return p_, pk

                def cmp_pv(jc, e_, ek):
                    for g in range(3):
                        fw.op('pe', [ek, 'Vc1'], ['pOC%d' % g], lambda e: e.matmul(pOC[g][:, 0:WV], lhsT=e_[:, g * 128:(g + 1) * 128], rhs=Vc1[:, jc, :], start=(jc == 0), stop=(jc == jmax)))

                run_branch(list(range(jmax + 1)), cmp_score, cmp_pv)
                for g in range(3):
                    fw.op('dve', ['pOC%d' % g], ['rZ'], lambda e: e.tensor_scalar(out=rZ[:, g:g + 1], in0=pOC[g][:, WV - 1:WV], scalar1=1e-30, scalar2=None, op0=ALU.max))
                fw.op('dve', ['rZ'], ['rZ'], lambda e: e.reciprocal(out=rZ[:], in_=rZ[:]))
                gv = g_[:, hk * 9:(hk + 1) * 9].rearrange("p (g b) -> p g b", b=3)
                fw.op('dve', ['rZ', gk], ['coef'], lambda e: e.tensor_tensor(out=coef[:], in0=rZ[:], in1=gv[:, :, 0], op=ALU.mult))
                for g in range(3):
                    fw.op('dve', ['pOC%d' % g, 'coef'], [Yk], lambda e: e.tensor_scalar(out=Y_[:, g, :], in0=pOC[g][:, 0:64], scalar1=coef[:, g:g + 1], scalar2=None, op0=ALU.mult))
                    if g == 0:
                        fw.op('dve', ['pOC0', 'rZ'], ['imp'], lambda e: e.tensor_scalar(out=imp[:], in0=pOC[0][:, 64:64 + Ns], scalar1=rZ[:, 0:1], scalar2=None, op0=ALU.mult))
                    else:
                        fw.op('dve', ['pOC%d' % g, 'rZ', 'imp'], ['imp'], lambda e: e.scalar_tensor_tensor(out=imp[:], in0=pOC[g][:, 64:64 + Ns], scalar=rZ[:, g:g + 1], in1=imp[:], op0=ALU.mult, op1=ALU.add))
                if 2 * i + 2 < Ns:
                    fw.op('pool', ['imp'], ['imp'], lambda e: e.memset(imp[:, 2 * i + 2:Ns], -1.0))
                fw.op('pool', ['imp'], ['imp'], lambda e: e.memset(imp[0:64, 2 * i + 1:2 * i + 2], -1.0))
                fw.op('pool', ['imp'], ['imp'], lambda e: e.memset(imp[64:128, 2 * i + 1:2 * i + 2], 1e4))
                fw.op('pool', ['imp'], ['imp'], lambda e: e.memset(imp[:, 2 * i:2 * i + 1], 1e4))
                if i >= 1:
                    fw.op('pool', ['imp'], ['imp'], lambda e: e.memset(imp[0:64, 2 * i - 1:2 * i], 1e4))
                fw.op('pool', ['imp'], ['imp'], lambda e: e.memset(imp[:, 0:1], 1e4))
                fw.op('dve', ['imp'], ['m1'], lambda e: e.max(out=m1[:], in_=imp[:]))
                fw.op('dve', ['imp', 'm1'], ['imp2'], lambda e: e.match_replace(out=imp2[:], in_to_replace=m1[:], in_values=imp[:], imm_value=-2.0))
                fw.op('dve', ['imp2'], ['m2'], lambda e: e.max(out=m2[:], in_=imp2[:]))
                fw.op('dve', ['imp', 'm2'], ['sel'], lambda e: e.tensor_scalar(out=sel[:], in0=imp[:], scalar1=m2[:, 7:8], scalar2=None, op0=ALU.is_ge))
                fw.op('dve', ['sel'], ['nsel'], lambda e: e.tensor_scalar(out=nsel[:, 64:64 + Ns], in0=sel[:], scalar1=-NEG, scalar2=NEG, op0=ALU.mult, op1=ALU.add))
                fw.op('dve', ['nsel'], ['nselT%d' % par], lambda e: e.tensor_copy(out=nselT[par][:], in_=nsel[:]))

            def partA2(i):
                par = i % 2
                Q_, Qk = QX[par], 'QX%d' % par
                ngrp = (2 * i + 1) // 64 + 1
                for grp in range(ngrp):
                    fw.op('pe', ['nselT%d' % par, 'identb'], ['pTn'], lambda e: e.transpose(out=pTn[:, :], in_=nselT[par][:, grp * 64:grp * 64 + 128], identity=identb[:]))
                    fw.op('dve', ['pTn'], [Qk], lambda e: e.tensor_copy(out=Q_[64:128, grp, :, :], in_=pTn[64:128, :].unsqueeze(1).to_broadcast([64, 3, 128])))

            def partB(i):
                par = i % 2
                Q_, Qk = QX[par], 'QX%d' % par
                g_, gk = gts[par], 'gt%d' % par
                Y_, Yk = Yb[par], 'Yb%d' % par
                q0 = Q_[0:64, 0, :, :].rearrange("p g q -> p (g q)")
                gv = g_[:, hk * 9:(hk + 1) * 9].rearrange("p (g b) -> p g b", b=3)
                k0 = max(0, i - 4)

                def win_score(kc):
                    p_, pk = rot(pS, 'pS')
                    msk = causb if kc == i else (bandb if kc == i - 4 else None)
                    fw.op('pe', ['KwT', Qk], [pk], lambda e: e.matmul(p_[:, 0:384], lhsT=KwT[:, kc * 128:(kc + 1) * 128], rhs=q0, start=True, stop=(msk is None)))
                    if msk is not None:
                        fw.op('pe', ['identb', 'causb', 'bandb'], [pk], lambda e: e.matmul(p_[:, 0:384], lhsT=identb[:], rhs=msk[:], start=False, stop=True))
                    return p_, pk

                def win_pv(kc, e_, ek):
                    fw.op('pe', [ek, 'Vw1'], ['pOs'], lambda e: e.matmul(pOs[0:65, 0:384], lhsT=Vw1[:, kc, :], rhs=e_[:], start=(kc == k0), stop=(kc == i)))

                def sel_score(kc):
                    grp = kc // 32
                    p_, pk = rot(pS, 'pS')
                    fw.op('pe', ['KX', Qk], [pk], lambda e: e.matmul(p_[:, 0:384], lhsT=KX[:, kc * 128:(kc + 1) * 128], rhs=Q_[:, grp, :, :].rearrange("p g q -> p (g q)"), start=True, stop=(kc < i)))
                    if kc == i:
                        fw.op('pe', ['identb', 'causb'], [pk], lambda e: e.matmul(p_[:, 0:384], lhsT=identb[:], rhs=causb[:], start=False, stop=True))
                    return p_, pk

                def sel_pv(kc, e_, ek):
                    fw.op('pe', [ek, 'Vs1'], ['pOs'], lambda e: e.matmul(pOs[0:65, 0:384], lhsT=Vs1[:, kc, :], rhs=e_[:], start=(kc == 0), stop=(kc == i)))

                run_branch(list(range(k0, i + 1)), win_score, win_pv)
                fw.op('act', ['pOs'], ['OT0'], lambda e: e.copy(out=OT[0][:], in_=pOs[0:65, 0:384]))
                run_branch(list(range(i + 1)), sel_score, sel_pv)
                fw.op('act', ['pOs'], ['OT1'], lambda e: e.copy(out=OT[1][:], in_=pOs[0:65, 0:384]))
                if i + 1 < NQ:
                    partA2(i + 1)
                for bi in range(2):
                    o_, ok = OT[bi], 'OT%d' % bi
                    for g in range(3):
                        pf, pfk = pOC[g], 'pOC%d' % g
                        fw.op('pe', [ok, 'identf'], [pfk], lambda e: e.transpose(out=pf[:, 0:65], in_=o_[0:65, g * 128:(g + 1) * 128], identity=identf[0:65, 0:65]))
                        fw.op('dve', [pfk], ['rd'], lambda e: e.reciprocal(out=rd[:], in_=pf[:, 64:65]))
                        fw.op('dve', ['rd', gk], ['rd'], lambda e: e.tensor_tensor(out=rd[:], in0=rd[:], in1=gv[:, g, 2 - bi:3 - bi], op=ALU.mult))
                        fw.op('dve', [pfk, 'rd', Yk], [Yk], lambda e: e.scalar_tensor_tensor(out=Y_[:, g, :], in0=pf[:, 0:64], scalar=rd[:, 0:1], in1=Y_[:, g, :], op0=ALU.mult, op1=ALU.add))
                fw.dma('pool', yq[i][:, 384 + hk * 192:384 + (hk + 1) * 192], Y_[:].rearrange("p g d -> p (g d)"), [Yk], ['y_b'])

            partA(0)
            partA2(0)
            for i in range(NQ):
                if i + 1 < NQ:
                    partA(i + 1)
                partB(i)
        fw.barrier()


def phase_p3(K, l):
    nc, fw, T = K.nc, K.fw, K.T
    NM = T // 256
    with ExitStack() as st:
        sb = lambda name, shape, dt=F32: st.enter_context(nc.sbuf_tensor(name + "_L%d" % l, shape, dt))
        ps = lambda name, shape, dt=F32: st.enter_context(nc.psum_tensor(name + "_L%d" % l, shape, dt))
        Wo = sb("f_Wo", [128, 8, 1024], BF16)
        W1 = sb("f_W1", [128, 8, 4096], BF16)
        W2 = sb("f_W2", [128, 32, 1024], BF16)
        wst = [sb("f_wst%d" % i, [128, 1024]) for i in range(2)]
        gf = sb("f_gf", [128, 8])
        identb = sb("f_identb", [128, 128], BF16)
        yt = sb("f_yt", [128, 2, 1024])
        yb = sb("f_yb", [128, 2, 1024], BF16)
        yT = sb("f_yT", [128, 8, 256], BF16)
        xr = [sb("f_xr%d" % i, [128, 2, 1024]) for i in range(1)]
        xb = sb("f_xb", [128, 2, 1024], BF16)
        xnT = sb("f_xnT", [128, 8, 256], BF16)
        hT = sb("f_hT", [128, 32, 256], BF16)
        rl = [sb("f_rl%d" % i, [128, 256]) for i in range(2)]
        junk = sb("f_junk", [128, 1024], BF16)
        ss = sb("f_ss", [128, 2])
        rstd = sb("f_rstd", [128, 2])
        pT = [ps("f_pT%d" % i, [128, 256], BF16) for i in range(2)]
        pA = [ps("f_pA%d" % i, [128, 512]) for i in range(4)]
        pH = [ps("f_pH%d" % i, [128, 512]) for i in range(2)]
        fw.dma('sp', identb[:], K.c['identb'], [], ['identb'])
        fw.dma('sp', gf[:], K.w['norm_ffn'][l].rearrange("(k p) -> p k", p=128), [], ['gf'])
        wi = [0]

        def loadw(dst_ap, src_ap, scal=None):
            i = wi[0] % 2
            wi[0] += 1
            fw.dma('sp' if i else 'pool', wst[i][:], src_ap, [], ['wst%d' % i])
            if scal is None:
                fw.op('dve' if i else 'pool', ['wst%d' % i], ['W'], lambda e: e.tensor_copy(out=dst_ap, in_=wst[i][:]))
            else:
                fw.op('dve' if i else 'pool', ['wst%d' % i, 'gf'], ['W'], lambda e: e.tensor_scalar(out=dst_ap, in0=wst[i][:], scalar1=scal, scalar2=None, op0=ALU.mult))
        for kc in range(8):
            loadw(Wo[:, kc, :], K.w['w_out'][l, kc * 128:(kc + 1) * 128, :])
            for q4 in range(4):
                loadw(W1[:, kc, q4 * 1024:(q4 + 1) * 1024], K.w['w_ffn1'][l, kc * 128:(kc + 1) * 128, q4 * 1024:(q4 + 1) * 1024], gf[:, kc:kc + 1])
        for fc in range(32):
            loadw(W2[:, fc, :], K.w['w_ffn2'][l, fc * 128:(fc + 1) * 128, :])
        ysrc = K.y.rearrange("(m j p) d -> m p j d", p=128, j=2)
        xsrc = K.xin[l].rearrange("(m j p) d -> m p j d", p=128, j=2)
        xdst = K.xout[l].rearrange("(m j p) d -> m p j d", p=128, j=2)
        cnt = [0]
        rc = {}

        def rot(lst, key):
            rc[key] = rc.get(key, -1) + 1
            i = rc[key] % len(lst)
            return lst[i], '%s%d' % (key, i)

        def transposes(src, srck, dst, dstk):
            for kc in range(8):
                p_, pk = pT[kc % 2], 'pT%d' % (kc % 2)
                for j in range(2):
                    fw.op('pe', [srck, 'identb'], [pk], lambda e: e.transpose(out=p_[:, j * 128:(j + 1) * 128], in_=src[:, j, kc * 128:(kc + 1) * 128], identity=identb[:]))
                if kc % 2:
                    fw.op('act', [pk], [dstk], lambda e: e.copy(out=dst[:, kc, :], in_=p_[:]))
                else:
                    fw.op('dve', [pk], [dstk], lambda e: e.tensor_copy(out=dst[:, kc, :], in_=p_[:]))

        for m in range(NM):
            x_, xk = xr[0], 'xr0'
            fw.dma('sp', yt[:], ysrc[m], [], ['yt'])
            fw.dma('sp', x_[:], xsrc[m], [], [xk])
            fw.op('pool', ['yt'], ['yb'], lambda e: e.tensor_copy(out=yb[:], in_=yt[:]))
            transposes(yb, 'yb', yT, 'yT')
            for j in range(2):
                for nh in range(2):
                    p_, pk = rot(pA, 'pA')
                    for kc in range(8):
                        fw.op('pe', ['yT', 'W'], [pk], lambda e: e.matmul(p_[:], lhsT=yT[:, kc, j * 128:(j + 1) * 128], rhs=Wo[:, kc, nh * 512:(nh + 1) * 512], start=(kc == 0), stop=(kc == 7)))
                    fw.op('dve', [pk, xk], [xk], lambda e: e.tensor_tensor(out=x_[:, j, nh * 512:(nh + 1) * 512], in0=x_[:, j, nh * 512:(nh + 1) * 512], in1=p_[:], op=ALU.add))
            fw.op('dve', [], ['ss'], lambda e: e.memset(ss[:], 0.0))
            for j in range(2):
                fw.op('act', [xk, 'ss'], ['junk', 'ss'], lambda e: e.activation(out=junk[:], in_=x_[:, j, :], func=AF.Square, accum_out=ss[:, j:j + 1]))
            fw.op('dve', ['ss'], ['rstd'], lambda e: e.tensor_scalar(out=rstd[:], in0=ss[:], scalar1=1.0 / D, scalar2=EPS, op0=ALU.mult, op1=ALU.add))
            fw.op('act', ['rstd'], ['rstd'], lambda e: e.activation(out=rstd[:], in_=rstd[:], func=AF.Sqrt))
            fw.op('dve', ['rstd'], ['rstd'], lambda e: e.reciprocal(out=rstd[:], in_=rstd[:]))
            for j in range(2):
                fw.op('pool', [xk, 'rstd'], ['xb'], lambda e: e.tensor_scalar(out=xb[:, j, :], in0=x_[:, j, :], scalar1=rstd[:, j:j + 1], scalar2=None, op0=ALU.mult))
            transposes(xb, 'xb', xnT, 'xnT')
            for fc in range(32):
                p_, pk = rot(pH, 'pH')
                for kc in range(8):
                    fw.op('pe', ['xnT', 'W'], [pk], lambda e: e.matmul(p_[:, 0:256], lhsT=W1[:, kc, fc * 128:(fc + 1) * 128], rhs=xnT[:, kc, :], start=(kc == 0), stop=(kc == 7)))
                r_, rk = rot(rl, 'rl')
                fw.op('act', [pk], [rk], lambda e: e.activation(out=r_[:], in_=p_[:, 0:256], func=AF.Relu))
                fw.op('pool' if fc % 2 else 'dve', [rk], ['hT%d' % fc], lambda e: e.tensor_tensor(out=hT[:, fc, :], in0=r_[:], in1=r_[:], op=ALU.mult))
            for j in range(2):
                for nh in range(2):
                    p_, pk = rot(pA, 'pA')
                    for fc in range(32):
                        fw.op('pe', ['hT%d' % fc, 'W'], [pk], lambda e: e.matmul(p_[:], lhsT=hT[:, fc, j * 128:(j + 1) * 128], rhs=W2[:, fc, nh * 512:(nh + 1) * 512], start=(fc == 0), stop=(fc == 31)))
                    fw.op('dve', [pk, xk], [xk], lambda e: e.tensor_tensor(out=x_[:, j, nh * 512:(nh + 1) * 512], in0=x_[:, j, nh * 512:(nh + 1) * 512], in1=p_[:], op=ALU.add))
            fw.dma('pool', xdst[m], x_[:], [xk], ['xout'])
        fw.barrier()


def kernel(**inputs):
    x = np.asarray(inputs["x"], dtype=np.float32)
    B, T, _ = x.shape
    nc, K = build(T, L=2)
    base = {n: np.ascontiguousarray(np.asarray(inputs[n], dtype=np.float32)) for n in WNAMES}
    for n, v in K.consts.items():
        base['c_' + n] = v
    for n, v in K.consts_n.items():
        base['cn_' + n] = v
    in_maps = []
    for b in range(B):
        m = dict(base)
        m['x'] = np.ascontiguousarray(x[b])
        in_maps.append(m)
    res = run_bass_kernel_spmd(nc, in_maps, core_ids=list(range(B)))
    return np.stack([np.asarray(r["out"], dtype=np.float32) for r in res.results], axis=0)
```

```python
import os
import numpy as np
import ml_dtypes
from contextlib import ExitStack
import concourse.bass as bass
import concourse.mybir as mybir
from concourse.bass_utils import run_bass_kernel_spmd

F32 = mybir.dt.float32
BF16 = mybir.dt.bfloat16
AF = mybir.ActivationFunctionType
ALU = mybir.AluOpType
AX = mybir.AxisListType

D = 1024
DIN = 2974
DFF = 4096
EPS = 1e-6
NEG = -30000.0

C_QKV, C_Z, C_B, C_A, C_QB, C_KV, C_GATE, C_U = 0, 1152, 1536, 1542, 1548, 1932, 2700, 2718
C_KC, C_VC, C_KS, C_VS, C_KW, C_VW = 1932, 2060, 2188, 2316, 2444, 2572


class FW:
    def __init__(self, nc, stack):
        self.nc = nc
        self.eng = {'pe': nc.tensor, 'act': nc.scalar, 'dve': nc.vector, 'pool': nc.gpsimd, 'sp': nc.sync}
        self.semh = {}
        self.cnt = {}
        for e in ['pe', 'act', 'dve', 'pool']:
            self.semh[e] = stack.enter_context(nc.semaphore('s_' + e))
            self.cnt[e] = 0
        self.NDS = 6
        self.dq = {}
        for q in ['sp', 'pool']:
            keys = []
            for i in range(self.NDS):
                k = 'd_%s%d' % (q, i)
                self.semh[k] = stack.enter_context(nc.semaphore(k))
                keys.append(k)
            self.dq[q] = {'keys': keys, 'n': 0}
        self.known = {e: {} for e in self.eng}
        self.lastw = {}
        self.readers = {}
        self.ninst = 0

    def _deps(self, R, W):
        deps = {}

        def add(k, v):
            if deps.get(k, 0) < v:
                deps[k] = v
        for r in R:
            t = self.lastw.get(r)
            if t is not None:
                add(*t)
        for w in W:
            t = self.lastw.get(w)
            if t is not None:
                add(*t)
            rd = self.readers.get(w)
            if rd:
                for k, v in rd.items():
                    add(k, v)
        return deps

    def _wait(self, e, deps):
        kn = self.known[e]
        for k, v in deps.items():
            if k == e and e == 'pe':
                continue
            if kn.get(k, 0) >= v:
                continue
            self.eng[e].wait_ge(self.semh[k], v)
            kn[k] = v
            self.ninst += 1

    def _commit(self, tok, R, W):
        k, v = tok
        for r in R:
            rd = self.readers.setdefault(r, {})
            if rd.get(k, 0) < v:
                rd[k] = v
        for w in W:
            self.lastw[w] = tok
            self.readers[w] = {}

    def op(self, e, R, W, fn):
        self._wait(e, self._deps(R, W))
        inst = fn(self.eng[e])
        self.cnt[e] += 1
        inst.then_inc(self.semh[e], 1)
        self._commit((e, self.cnt[e]), R, W)
        self.ninst += 1
        return inst

    def dma(self, q, out, in_, R, W):
        dq = self.dq[q]
        j = dq['n']
        s = j % self.NDS
        key = dq['keys'][s]
        deps = self._deps(R, W)
        if j >= self.NDS:
            v = 16 * (j // self.NDS)
            if deps.get(key, 0) < v:
                deps[key] = v
        self._wait(q, deps)
        inst = self.eng[q].dma_start(out=out, in_=in_)
        inst.then_inc(self.semh[key], 16)
        dq['n'] += 1
        self._commit((key, 16 * (j // self.NDS + 1)), R, W)
        self.ninst += 1

    def barrier(self):
        cur = {}
        for q, dq in self.dq.items():
            for si, key in enumerate(dq['keys']):
                n = (dq['n'] - si + self.NDS - 1) // self.NDS if dq['n'] > si else 0
                if n > 0:
                    cur[key] = 16 * n
        for e in ['pe', 'act', 'dve', 'pool']:
            if self.cnt[e] > 0:
                cur[e] = self.cnt[e]
        for e in self.eng:
            self._wait(e, {k: v for k, v in cur.items() if k != e})

    def finish(self):
        deps = {}
        for q, dq in self.dq.items():
            for s, key in enumerate(dq['keys']):
                n = (dq['n'] - s + self.NDS - 1) // self.NDS if dq['n'] > s else 0
                if n > 0:
                    deps[key] = 16 * n
        for e in ['pe', 'act', 'dve', 'pool']:
            if self.cnt[e] > 0:
                deps[e] = self.cnt[e]
        self._wait('sp', deps)


class Ctx:
    pass


def _consts():
    c = {}
    c['identb'] = np.eye(128).astype(ml_dtypes.bfloat16)
    c['identf'] = np.eye(128).astype(np.float32)
    ob = np.zeros((128, 128), np.float32)
    ob[:64, :64] = 1.0
    ob[64:, 64:] = 1.0
    c['onesblk'] = ob
    i = np.arange(64)
    c['triU'] = (i[:, None] <= i[None, :]).astype(np.float32)
    c['biasL'] = np.where(i[None, :] < i[:, None], 0.0, NEG).astype(np.float32)
    c['biasU'] = np.where(i[:, None] <= i[None, :], 0.0, NEG).astype(np.float32)
    t1 = np.arange(1, 513, dtype=np.float32)
    c['invcnt'] = np.stack([1.0 / np.minimum(t1, float(w)) for w in (2, 4, 8, 16)]).astype(np.float32)
    return c


def phase_p1(K, l):
    nc, fw, T = K.nc, K.fw, K.T
    NM = T // 512
    with ExitStack() as st:
        sb = lambda name, shape, dt: st.enter_context(nc.sbuf_tensor(name + "_L%d" % l, shape, dt))
        ps = lambda name, shape, dt: st.enter_context(nc.psum_tensor(name + "_L%d" % l, shape, dt))
        Wb = sb("p1_Wb", [128, 8, DIN], BF16)
        wst = [sb("p1_wst%d" % i, [128, DIN // 2], F32) for i in range(2)]
        gmix = sb("p1_gmix", [128, 8], F32)
        identb = sb("p1_identb", [128, 128], BF16)
        onesblk = sb("p1_onesblk", [128, 128], F32)
        cw = sb("p1_cw", [128, 9, 4], F32)
        qg = sb("p1_qg", [128, 4], F32)
        dtb = sb("p1_dtb", [128, 6], F32)
        nA = sb("p1_nA", [128, 6], F32)
        pscale = sb("p1_pscale", [128, 256], F32)
        poolw = sb("p1_poolw", [64, 4, 64], F32)
        poolwb = sb("p1_poolwb", [64, 4, 64], BF16)
        invc = sb("p1_invc", [64, 4, 512], F32)
        xt = [sb("p1_xt%d" % i, [128, 4, D], F32) for i in range(1)]
        junk = sb("p1_junk", [128, D], BF16)
        ss = sb("p1_ss", [128, 4], F32)
        rstd = sb("p1_rstd", [128, 4], F32)
        xb = sb("p1_xb", [128, 4, D], BF16)
        xT = [sb("p1_xT%d" % i, [128, 8, 512], BF16) for i in range(2)]
        cst = [sb("p1_cst%d" % i, [128, 515], F32) for i in range(9)]
        acc = [sb("p1_acc%d" % i, [128, 512], F32) for i in range(3)]
        sl = [sb("p1_sl%d" % i, [128, 512], F32) for i in range(3)]
        sq = [sb("p1_sq%d" % i, [128, 512], F32) for i in range(3)]
        rn = [sb("p1_rn%d" % i, [128, 512], F32) for i in range(3)]
        of = [sb("p1_of%d" % i, [128, 512], F32) for i in range(2)]
        ob = [sb("p1_ob%d" % i, [128, 512], BF16) for i in range(2)]
        ust = [sb("p1_ust%d" % i, [64, 527], F32) for i in range(4)]
        s2 = sb("p1_s2", [64, 526], F32)
        s4 = sb("p1_s4", [64, 524], F32)
        s8 = sb("p1_s8", [64, 520], F32)
        s16 = sb("p1_s16", [64, 512], F32)
        dfb = [sb("p1_dfb%d" % i, [64, 512], BF16) for i in range(4)]
        ycs = sb("p1_ycs", [128, 4, 256], F32)
        zst = sb("p1_zst", [128, 4, 384], F32)
        bgs = sb("p1_bgs", [128, 4, 12], F32)
        tmpa = sb("p1_tmpa", [128, 6], F32)
        gts = sb("p1_gts", [128, 4, 18], F32)
        v1s = [sb("p1_v1s%d" % i, [128, 4, 130], BF16) for i in range(2)]
        v1w = [sb("p1_v1w%d" % i, [128, 4, 130], BF16) for i in range(2)]
        pT = [ps("p1_pT%d" % i, [128, 512], BF16) for i in range(2)]
        pF = [ps("p1_pF%d" % i, [128, 512], F32) for i in range(3)]
        pS = [ps("p1_pS%d" % i, [128, 512], F32) for i in range(3)]

        fw.dma('sp', identb[:], K.c['identb'], [], ['identb'])
        fw.dma('sp', onesblk[:], K.c['onesblk'], [], ['onesblk'])
        fw.dma('sp', gmix[:], K.w['norm_mix'][l].rearrange("(k p) -> p k", p=128), [], ['gmix'])
        for k in range(4):
            fw.dma('sp', cw[:, :, k], K.w['conv_w'][l, k].rearrange("(c p) -> p c", p=128), [], ['cw'])
        for hh in range(2):
            fw.dma('sp', qg[hh * 64:(hh + 1) * 64, 0:1], K.w['nsa_q_norm'][l].rearrange("(p o) -> p o", o=1), [], ['qg'])
            for j in range(3):
                fw.dma('sp', qg[hh * 64:(hh + 1) * 64, 1 + j:2 + j], K.w['nsa_k_norm'][l, j].rearrange("(p o) -> p o", o=1), [], ['qg'])
        fw.op('dve', ['qg'], ['qg'], lambda e: e.tensor_scalar(out=qg[:, 0:1], in0=qg[:, 0:1], scalar1=0.125, scalar2=None, op0=ALU.mult))
        fw.dma('sp', dtb[:], K.w['dt_bias'][l:l + 1, :].partition_broadcast(128), [], ['dtb'])
        fw.dma('sp', nA[:], K.w['a_log'][l:l + 1, :].partition_broadcast(128), [], ['nA'])
        fw.op('act', ['nA'], ['nA'], lambda e: e.activation(out=nA[:], in_=nA[:], func=AF.Exp))
        fw.op('dve', ['nA'], ['nA'], lambda e: e.tensor_scalar(out=nA[:], in0=nA[:], scalar1=-1.0, scalar2=None, op0=ALU.mult))
        fw.dma('sp', pscale[:], K.w['pool_scale'][l:l + 1, :].partition_broadcast(128), [], ['pscale'])
        fw.dma('sp', poolw[:], K.w['pool_w'][l].rearrange("g c d -> c g d"), [], ['poolw'])
        fw.op('dve', ['poolw'], ['poolwb'], lambda e: e.tensor_copy(out=poolwb[:], in_=poolw[:]))
        fw.dma('sp', invc[:], K.c['invcnt'].partition_broadcast(64), [], ['invc'])
        HW = DIN // 2
        for kc in range(8):
            for hf in range(2):
                w_ = wst[hf]
                fw.dma('sp' if hf else 'pool', w_[:], K.w['w_in'][l, kc * 128:(kc + 1) * 128, hf * HW:(hf + 1) * HW], [], ['wst%d' % hf])
                fw.op('dve' if hf else 'pool', ['wst%d' % hf, 'gmix'], ['Wb'],
                      lambda e: e.tensor_scalar(out=Wb[:, kc, hf * HW:(hf + 1) * HW], in0=w_[:], scalar1=gmix[:, kc:kc + 1], scalar2=None, op0=ALU.mult))
        for i in range(9):
            fw.op('pool', [], ['cst%d' % i], lambda e: e.memset(cst[i][:, 0:3], 0.0))
        for i in range(4):
            fw.op('pool', [], ['ust%d' % i], lambda e: e.memset(ust[i][:, 0:15], 0.0))
        for i in range(2):
            fw.op('pool', [], ['v1s%d' % i], lambda e: e.memset(v1s[i][:], 1.0))
            fw.op('pool', [], ['v1w%d' % i], lambda e: e.memset(v1w[i][:], 1.0))

        xsrc = K.xin[l].rearrange("(m j p) d -> m p j d", p=128, j=4)
        cnt = [0]
        rc = {}
        freel = {}

        def rot(lst, key):
            rc[key] = rc.get(key, -1) + 1
            i = rc[key] % len(lst)
            return lst[i], '%s%d' % (key, i)

        for m in range(NM):
            par = m % 2
            x_, xk = xt[0], 'xt0'
            xT_, xTk = xT[par], 'xT%d' % par
            fw.dma('pool', x_[:], xsrc[m], [], [xk])
            fw.op('dve', [], ['ss'], lambda e: e.memset(ss[:], 0.0))
            for j in range(4):
                fw.op('act', [xk, 'ss'], ['junk', 'ss'], lambda e: e.activation(out=junk[:], in_=x_[:, j, :], func=AF.Square, accum_out=ss[:, j:j + 1]))
            fw.op('dve', ['ss'], ['rstd'], lambda e: e.tensor_scalar(out=rstd[:], in0=ss[:], scalar1=1.0 / D, scalar2=EPS, op0=ALU.mult, op1=ALU.add))
            fw.op('act', ['rstd'], ['rstd'], lambda e: e.activation(out=rstd[:], in_=rstd[:], func=AF.Sqrt))
            fw.op('dve', ['rstd'], ['rstd'], lambda e: e.reciprocal(out=rstd[:], in_=rstd[:]))
            for j in range(4):
                fw.op('dve' if j % 2 else 'pool', [xk, 'rstd'], ['xb%d' % j],
                      lambda e: e.tensor_scalar(out=xb[:, j, :], in0=x_[:, j, :], scalar1=rstd[:, j:j + 1], scalar2=None, op0=ALU.mult))
            for kc in range(8):
                p_, pk = pT[kc % 2], 'pT%d' % (kc % 2)
                for j in range(4):
                    fw.op('pe', ['xb%d' % j, 'identb'], [pk], lambda e: e.transpose(out=p_[:, j * 128:(j + 1) * 128], in_=xb[:, j, kc * 128:(kc + 1) * 128], identity=identb[:]))
                if kc % 2:
                    fw.op('act', [pk], [xTk], lambda e: e.copy(out=xT_[:, kc, :], in_=p_[:]))
                else:
                    fw.op('dve', [pk], [xTk], lambda e: e.tensor_copy(out=xT_[:, kc, :], in_=p_[:]))

            def acq(lst, key):
                fl = freel.setdefault(key, list(range(len(lst))))
                i = fl.pop(0)
                return lst[i], '%s%d' % (key, i)

            def rel(key, k_):
                freel[key].append(int(k_[len(key):]))

            def fm(col0, ncols):
                p_, pk = acq(pF, 'pF')
                for kc in range(8):
                    fw.op('pe', [xTk, 'Wb'], [pk], lambda e: e.matmul(p_[0:ncols, :], lhsT=Wb[:, kc, col0:col0 + ncols], rhs=xT_[:, kc, :], start=(kc == 0), stop=(kc == 7)))
                return p_, pk

            def rnorm(src_ap, srck, n, mean):
                q_, qk_ = acq(sq, 'sq')
                fw.op('pool' if srck.startswith('sl') else 'act', [srck], [qk_],
                      (lambda e: e.tensor_tensor(out=q_[:], in0=src_ap, in1=src_ap, op=ALU.mult)) if srck.startswith('sl')
                      else (lambda e: e.activation(out=q_[:], in_=src_ap, func=AF.Square)))
                yield
                s_, sk_ = acq(pS, 'pS')
                fw.op('pe', [qk_, 'onesblk'], [sk_], lambda e: e.matmul(s_[:], lhsT=onesblk[:], rhs=q_[:], start=True, stop=True))
                rel('sq', qk_)
                yield
                r_, rk_ = rot(rn, 'rn')
                fw.op('dve', [sk_], [rk_], lambda e: e.tensor_scalar(out=r_[:], in0=s_[:], scalar1=(1.0 / 64 if mean else 1.0), scalar2=EPS, op0=ALU.mult, op1=ALU.add))
                rel('pS', sk_)
                fw.op('act', [rk_], [rk_], lambda e: e.activation(out=r_[:], in_=r_[:], func=AF.Sqrt))
                fw.op('dve', [rk_], [rk_], lambda e: e.reciprocal(out=r_[:], in_=r_[:]))
                return r_, rk_

            def gdn_chunk(ch):
                p_, pk = fm(C_QKV + ch * 128, 128)
                yield
                c_, ck = cst[ch], 'cst%d' % ch
                fw.op('act', [pk], [ck], lambda e: e.copy(out=c_[:, 3:515], in_=p_[:]))
                a_, ak = rot(acc, 'acc')
                fw.op('dve', [ck, 'cw'], [ak], lambda e: e.tensor_scalar(out=a_[:], in0=c_[:, 0:512], scalar1=cw[:, ch, 0:1], scalar2=None, op0=ALU.mult))
                for k in range(1, 4):
                    fw.op('dve', [ck, 'cw', ak], [ak],
                          lambda e: e.scalar_tensor_tensor(out=a_[:], in0=c_[:, k:k + 512], scalar=cw[:, ch, k:k + 1], in1=a_[:], op0=ALU.mult, op1=ALU.add))
                fw.op('pool', [ck], [ck], lambda e: e.tensor_copy(out=c_[:, 0:3], in_=c_[:, 512:515]))
                s_, sk = acq(sl, 'sl')
                fw.op('act', [ak], [sk], lambda e: e.activation(out=s_[:], in_=a_[:], func=AF.Silu))
                rel('pF', pk)
                if ch < 6:
                    r_, rk = yield from rnorm(s_[:], sk, 128, False)
                    o_, ok = rot(of, 'of')
                    fw.op('dve', [sk, rk], [ok], lambda e: e.scalar_tensor_tensor(out=o_[:], in0=s_[:], scalar=(0.125 if ch < 3 else 1.0), in1=r_[:], op0=ALU.mult, op1=ALU.mult))
                    fw.dma('sp', K.gqk[ch * 128:(ch + 1) * 128, m * 512:(m + 1) * 512], o_[:], [ok], ['gqk'])
                else:
                    fw.dma('sp', K.gqk[ch * 128:(ch + 1) * 128, m * 512:(m + 1) * 512], s_[:], [sk], ['gqk'])
                rel('sl', sk)

            def nsa_chunk(col0, ch, gi_, dst):
                p_, pk = fm(col0 + ch * 128, 128)
                yield
                r_, rk = yield from rnorm(p_[:], pk, 128, True)
                o_, ok = rot(ob, 'ob')
                fw.op('dve', [pk, rk, 'qg'], [ok], lambda e: e.scalar_tensor_tensor(out=o_[:], in0=p_[:], scalar=qg[:, gi_:gi_ + 1], in1=r_[:], op0=ALU.mult, op1=ALU.mult))
                fw.dma('sp', dst[ch * 128:(ch + 1) * 128, m * 512:(m + 1) * 512], o_[:], [ok], ['nsa_dst'])
                rel('pF', pk)

            def raw_chunk(col0, dst):
                p_, pk = fm(col0, 128)
                yield
                o_, ok = rot(ob, 'ob')
                fw.op('act', [pk], [ok], lambda e: e.copy(out=o_[:], in_=p_[:]))
                fw.dma('sp', dst[:, m * 512:(m + 1) * 512], o_[:], [ok], ['nsa_dst'])
                rel('pF', pk)

            def pool_chunk(gi, wlen):
                p_, pk = fm(C_U + gi * 64, 64)
                yield
                u_, uk = ust[gi], 'ust%d' % gi
                fw.op('act', [pk], [uk], lambda e: e.copy(out=u_[:, 15:527], in_=p_[0:64, :]))
                rel('pF', pk)
                fw.op('dve', [uk], ['s2'], lambda e: e.tensor_tensor(out=s2[:], in0=u_[:, 1:527], in1=u_[:, 0:526], op=ALU.add))
                sw = s2
                if wlen >= 4:
                    fw.op('dve', ['s2'], ['s4'], lambda e: e.tensor_tensor(out=s4[:], in0=s2[:, 2:526], in1=s2[:, 0:524], op=ALU.add))
                    sw = s4
                if wlen >= 8:
                    fw.op('dve', ['s4'], ['s8'], lambda e: e.tensor_tensor(out=s8[:], in0=s4[:, 4:524], in1=s4[:, 0:520], op=ALU.add))
                    sw = s8
                if wlen >= 16:
                    fw.op('dve', ['s8'], ['s16'], lambda e: e.tensor_tensor(out=s16[:], in0=s8[:, 8:520], in1=s8[:, 0:512], op=ALU.add))
                    sw = s16
                nsw = {2: 526, 4: 524, 8: 520, 16: 512}[wlen]
                swk = 's%d' % wlen
                d_, dk = dfb[gi], 'dfb%d' % gi
                if m == 0:
                    fw.op('dve', [swk, 'invc'], [swk], lambda e: e.tensor_tensor(out=sw[:, nsw - 512:nsw], in0=sw[:, nsw - 512:nsw], in1=invc[:, gi, :], op=ALU.mult))
                    fw.op('dve', [swk, uk], [dk], lambda e: e.tensor_tensor(out=d_[:], in0=sw[:, nsw - 512:nsw], in1=u_[:, 15:527], op=ALU.subtract))
                else:
                    fw.op('dve', [swk, uk], [dk], lambda e: e.scalar_tensor_tensor(out=d_[:], in0=sw[:, nsw - 512:nsw], scalar=1.0 / wlen, in1=u_[:, 15:527], op0=ALU.mult, op1=ALU.subtract))
                fw.op('pool', [uk], [uk], lambda e: e.tensor_copy(out=u_[:, 0:15], in_=u_[:, 512:527]))

            def pool_out(j):
                p_, pk = acq(pS, 'pS')
                for gi in range(4):
                    fw.op('pe', ['dfb%d' % gi, 'poolwb'], [pk], lambda e: e.matmul(p_[:, gi * 64:(gi + 1) * 64], lhsT=dfb[gi][:, j * 128:(j + 1) * 128], rhs=poolwb[:, gi, :], start=True, stop=True))
                yield
                fw.op('dve', [pk, 'pscale'], ['ycs'], lambda e: e.tensor_tensor(out=ycs[:, j, :], in0=p_[:, 0:256], in1=pscale[:], op=ALU.mult))
                rel('pS', pk)
                if j == 3:
                    fw.dma('sp', K.y.rearrange("(m j p) d -> m p j d", p=128, j=4)[m][:, :, 768:1024], ycs[:], ['ycs'], ['y_c'])

            v_, vk = v1s[par], 'v1s%d' % par
            vw_, vwk = v1w[par], 'v1w%d' % par

            def tok_a(j):
                p_, pk = acq(pF, 'pF')
                for kc in range(8):
                    fw.op('pe', [xTk, 'Wb'], [pk], lambda e: e.matmul(p_[:, 0:396], lhsT=xT_[:, kc, j * 128:(j + 1) * 128], rhs=Wb[:, kc, C_Z:C_Z + 396], start=(kc == 0), stop=(kc == 7)))
                yield
                fw.op('act', [pk], ['zst'], lambda e: e.activation(out=zst[:, j, :], in_=p_[:, 0:384], func=AF.Silu))
                fw.op('act', [pk], ['bgs'], lambda e: e.activation(out=bgs[:, j, 0:6], in_=p_[:, 384:390], func=AF.Sigmoid))
                fw.op('dve', [pk, 'dtb', 'zst', 'bgs'], ['tmpa'], lambda e: e.tensor_tensor(out=tmpa[:], in0=p_[:, 390:396], in1=dtb[:], op=ALU.add))
                fw.op('act', ['tmpa'], ['tmpa'], lambda e: e.activation(out=tmpa[:], in_=tmpa[:], func=AF.Exp))
                fw.op('act', ['tmpa'], ['tmpa'], lambda e: e.activation(out=tmpa[:], in_=tmpa[:], func=AF.Ln, bias=1.0))
                fw.op('dve', ['tmpa', 'nA'], ['bgs'], lambda e: e.tensor_tensor(out=bgs[:, j, 6:12], in0=tmpa[:], in1=nA[:], op=ALU.mult))
                rel('pF', pk)

            def tok_b(j):
                p_, pk = acq(pF, 'pF')
                for (c0, o0) in ((C_VS, 0), (C_VW, 128), (C_GATE, 256)):
                    nn = 18 if c0 == C_GATE else 128
                    for kc in range(8):
                        fw.op('pe', [xTk, 'Wb'], [pk], lambda e: e.matmul(p_[:, o0:o0 + nn], lhsT=xT_[:, kc, j * 128:(j + 1) * 128], rhs=Wb[:, kc, c0:c0 + nn], start=(kc == 0), stop=(kc == 7)))
                yield
                fw.op('act', [pk], ['gts'], lambda e: e.activation(out=gts[:, j, :], in_=p_[:, 256:274], func=AF.Sigmoid))
                fw.op('dve', [pk, 'gts'], [vk], lambda e: e.tensor_copy(out=v_[:, j, :].rearrange("p (h c) -> p h c", h=2)[:, :, 0:64],
                                                                     in_=p_[:, 0:128].rearrange("p (h c) -> p h c", h=2)))
                fw.op('dve', [pk, 'gts'], [vwk], lambda e: e.tensor_copy(out=vw_[:, j, :].rearrange("p (h c) -> p h c", h=2)[:, :, 0:64],
                                                                     in_=p_[:, 128:256].rearrange("p (h c) -> p h c", h=2)))
                rel('pF', pk)

            tasks = [gdn_chunk(ch) for ch in range(9)]
            tasks += [nsa_chunk(C_QB, ch, 0, K.qn) for ch in range(3)] + [nsa_chunk(C_KS, 0, 2, K.ksT), nsa_chunk(C_KW, 0, 3, K.kwT)]
            tasks += [raw_chunk(C_KC, K.kcT), raw_chunk(C_VC, K.vcT)]
            tasks += [pool_chunk(gi, wlen) for gi, wlen in enumerate((2, 4, 8, 16))]
            tasks += [None]
            tasks += [pool_out(j) for j in range(4)]
            for j in range(4):
                tasks += [tok_a(j), tok_b(j)]
            active = []
            ti = 0
            while ti < len(tasks) or active:
                while ti < len(tasks) and len(active) < 3:
                    if tasks[ti] is None:
                        if active:
                            break
                        ti += 1
                        continue
                    active.append(tasks[ti])
                    ti += 1
                for g_ in list(active):
                    try:
                        next(g_)
                    except StopIteration:
                        active.remove(g_)
            fw.dma('sp', K.zs.rearrange("(m j p) d -> m p j d", p=128, j=4)[m], zst[:], ['zst'], ['zs'])
            fw.dma('sp', K.bg.rearrange("(m j p) d -> m p j d", p=128, j=4)[m], bgs[:], ['bgs'], ['bg'])
            fw.dma('sp', K.gt.rearrange("(m j p) d -> m p j d", p=128, j=4)[m], gts[:], ['gts'], ['gt'])
            fw.dma('sp', K.vs1.rearrange("(m j p) d -> m p j d", p=128, j=4)[m], v_[:], [vk], ['vs1'])
            fw.dma('sp', K.vw1.rearrange("(m j p) d -> m p j d", p=128, j=4)[m], vw_[:], [vwk], ['vw1'])
        fw.barrier()


WNAMES = ["norm_mix", "w_in", "conv_w", "a_log", "dt_bias", "gdn_norm", "nsa_q_norm", "nsa_k_norm", "cmp_pos",
          "cmp_w1", "cmp_w2", "pool_w", "pool_scale", "w_out", "norm_ffn", "w_ffn1", "w_ffn2"]
WSHAPES = {"norm_mix": [2, 1024], "w_in": [2, 1024, 2974], "conv_w": [2, 4, 1152], "a_log": [2, 6], "dt_bias": [2, 6],
           "gdn_norm": [2, 64], "nsa_q_norm": [2, 64], "nsa_k_norm": [2, 3, 64], "cmp_pos": [2, 2, 32, 64],
           "cmp_w1": [2, 2, 2048, 256], "cmp_w2": [2, 2, 256, 64], "pool_w": [2, 4, 64, 64], "pool_scale": [2, 256],
           "w_out": [2, 1024, 1024], "norm_ffn": [2, 1024], "w_ffn1": [2, 1024, 4096], "w_ffn2": [2, 4096, 1024]}


def build(T, L=2, phases=("p1", "gdn", "cmp", "nsa", "p3"), dbg=False):
    nc = bass.Bass("TRN2", target_bir_lowering=False)
    K = Ctx()
    K.nc, K.T, K.L = nc, T, L
    K.cut = int(os.environ.get('GDN_CUT', '99'))
    K.f32r = os.environ.get('GDN_F32R', '0') == '1'
    dk = "ExternalOutput" if dbg else "Internal"
    K.w = {n: nc.dram_tensor(n, WSHAPES[n], F32, kind="ExternalInput").ap() for n in WNAMES}
    cs = _consts()
    K.c = {n: nc.dram_tensor("c_" + n, list(v.shape), BF16 if v.dtype != np.float32 else F32, kind="ExternalInput").ap() for n, v in cs.items()}
    x = nc.dram_tensor("x", [T, D], F32, kind="ExternalInput").ap()
    out = nc.dram_tensor("out", [T, D], F32, kind="ExternalOutput").ap()
    xs1 = nc.dram_tensor("xs1", [T, D], F32, kind=dk).ap()
    K.xin = [x, xs1]
    K.xout = [xs1, out] if L == 2 else [out]
    K.gqk = nc.dram_tensor("gqk", [1152, T], F32, kind=dk).ap()
    K.zs = nc.dram_tensor("zs", [T, 384], F32, kind=dk).ap()
    K.bg = nc.dram_tensor("bg", [T, 12], F32, kind=dk).ap()
    K.gt = nc.dram_tensor("gt", [T, 18], F32, kind=dk).ap()
    K.qn = nc.dram_tensor("qn", [384, T], BF16, kind=dk).ap()
    K.ksT = nc.dram_tensor("ksT", [128, T], BF16, kind=dk).ap()
    K.kwT = nc.dram_tensor("kwT", [128, T], BF16, kind=dk).ap()
    K.kcT = nc.dram_tensor("kcT", [128, T], BF16, kind=dk).ap()
    K.vcT = nc.dram_tensor("vcT", [128, T], BF16, kind=dk).ap()
    K.vs1 = nc.dram_tensor("vs1", [T, 130], BF16, kind=dk).ap()
    K.vw1 = nc.dram_tensor("vw1", [T, 130], BF16, kind=dk).ap()
    K.y = nc.dram_tensor("y", [T, D], F32, kind=dk).ap()
    Ns_, NcP_ = T // 64, T // 16
    K.KcT = nc.dram_tensor("KcT", [2, 64, NcP_], BF16, kind=dk).ap()
    K.Vc1 = nc.dram_tensor("Vc1", [2, NcP_, 64 + Ns_ + 1], BF16, kind=dk).ap()
    csn = _nsa_consts(T)
    K.cn = {n: nc.dram_tensor("cn_" + n, list(v.shape), BF16, kind="ExternalInput").ap() for n, v in csn.items()}
    K.consts_n = csn
    with ExitStack() as st:
        st.enter_context(nc.allow_non_contiguous_dma(reason="small parameter loads"))
        K.fw = FW(nc, st)
        for l in range(L):
            if "p1" in phases:
                phase_p1(K, l)
            if "gdn" in phases:
                phase_gdn(K, l)
            if "cmp" in phases:
                phase_cmp(K, l)
            if "nsa" in phases:
                phase_nsa(K, l)
            if "p3" in phases:
                phase_p3(K, l)
        K.fw.finish()
    K.consts = cs
    return nc, K


def phase_gdn(K, l):
    nc, fw, T = K.nc, K.fw, K.T
    NCH = T // 64
    F32R = mybir.dt.float32r
    R_ = (lambda ap: ap.bitcast(F32R)) if K.f32r else (lambda ap: ap)
    with ExitStack() as st:
        sb = lambda name, shape, dt=F32: st.enter_context(nc.sbuf_tensor(name + "_L%d" % l, shape, dt))
        identf = sb("g_identf", [64, 64])
        ones64 = sb("g_ones", [64, 64])
        triU = sb("g_triU", [64, 64])
        biasL = sb("g_biasL", [64, 64])
        biasU = sb("g_biasU", [64, 64])
        gnorm = sb("g_gnorm", [64, 64])
        Fq = [sb("g_F%d" % i, [64, 18, 512]) for i in range(2)]
        Zs = [sb("g_Z%d" % i, [64, 8, 384]) for i in range(1)]
        BG = [sb("g_BG%d" % i, [64, 8, 12]) for i in range(2)]
        Yst = [sb("g_Y%d" % i, [64, 8, 384]) for i in range(1)]
        S = [sb("g_S%d" % i, [64, 6, 64]) for i in range(2)]
        Sb = [sb("g_Sb%d" % i, [64, 6, 64], BF16) for i in range(2)]
        names = ["KVt", "rhsG", "gcol", "D1", "D2", "E1", "E2", "egrow", "egcol", "kdsc", "nbeta", "bege", "N", "qkT", "vb", "kbg", "kd", "qdT",
                 "Mp0", "Mp1", "Np0", "Np1", "Tt0", "Tt1", "U", "WT", "vnew", "tmpS", "O", "sqo", "ssum", "Yt", "Gs", "Nb", "Fb"]
        BFN = ("Mp0", "Mp1", "Np0", "Np1", "Tt0", "Tt1", "vb", "kbg", "kd", "qdT", "qkT", "WT", "vnew", "Nb", "Fb")
        W2 = {}
        SCAN_IN = ("U", "WT", "kd", "qdT", "qkT", "egrow")
        for nm in names:
            shape = [64, 12, 64] if nm in ("KVt", "Fb") else [64, 390] if nm == "Gs" else ([64, 6] if nm in ("gcol", "egcol", "kdsc", "nbeta", "bege", "ssum") else [64, 6, 64])
            W2[nm] = [sb("g_%s%d" % (nm, i), shape, BF16 if nm in BFN else F32) for i in range(4 if nm in SCAN_IN else 2)]
        banks = [st.enter_context(nc.psum_tensor("g_ps%d_L%d" % (i, l), [64, 512], F32)) for i in range(8)]
        bc = [0]

        def bank():
            bc[0] += 1
            i = bc[0] % 8
            return banks[i], 'gps%d' % i

        fw.dma('sp', identf[:], K.c['identf'][0:64, 0:64], [], ['identf'])
        fw.dma('sp', triU[:], K.c['triU'], [], ['triU'])
        fw.dma('sp', biasL[:], K.c['biasL'], [], ['biasL'])
        fw.dma('sp', biasU[:], K.c['biasU'], [], ['biasU'])
        fw.dma('sp', gnorm[:], K.w['gdn_norm'][l:l + 1, :].partition_broadcast(64), [], ['gnorm'])
        fw.op('pool', [], ['ones64'], lambda e: e.memset(ones64[:], 1.0))
        fw.op('pool', [], ['S0'], lambda e: e.memset(S[0][:], 0.0))
        fw.op('pool', [], ['Sb0'], lambda e: e.memset(Sb[0][:], 0.0))

        gq = K.gqk.rearrange("(s p) t -> p s t", p=64)
        zsr = K.zs.rearrange("(n c) f -> c n f", c=64)
        bgr = K.bg.rearrange("(n c) f -> c n f", c=64)
        yr = K.y.rearrange("(n c) f -> c n f", c=64)

        def b3(ap2, n=64):
            return ap2.unsqueeze(2).to_broadcast([64, 6, n])

        def m3(ap2):
            return ap2.unsqueeze(1).to_broadcast([64, 6, 64])

        state = {}

        def load_group(gi):
            par = gi % 2
            fw.dma('sp', Fq[par][:], gq[:, :, gi * 512:(gi + 1) * 512], [], ['F%d' % par])
            fw.dma('sp', BG[par][:], bgr[:, gi * 8:(gi + 1) * 8, :], [], ['BG%d' % par])

        def prep(n):
            gi, j = n // 8, n % 8
            gp = gi % 2
            p = n % 2
            F_, Fk = Fq[gp], 'F%d' % gp
            BG_, BGk = BG[gp], 'BG%d' % gp
            ix = lambda nm: (n % 4) if nm in SCAN_IN else p
            t = {nm: W2[nm][ix(nm)] for nm in names}
            k_ = lambda nm: '%s%d' % (nm, ix(nm))
            cs = slice(j * 64, (j + 1) * 64)
            qT = F_[:, 0:6, cs]
            kT = F_[:, 6:12, cs]
            beta = BG_[:, j, 0:6]
            g = BG_[:, j, 6:12]
            fw.op('pool', [Fk], [k_('Fb')], lambda e: e.tensor_copy(out=t['Fb'][:], in_=F_[:, 0:12, cs]))
            pk1, pk1k = bank()
            pv1, pv1k = bank()
            for h in range(6):
                fw.op('pe', [Fk, 'identf'], [pk1k], lambda e: e.transpose(out=pk1[:, h * 64:(h + 1) * 64], in_=F_[:, 6 + h, cs], identity=identf[:]))
            for h in range(6):
                fw.op('pe', [Fk, 'identf'], [pv1k], lambda e: e.transpose(out=pv1[:, h * 64:(h + 1) * 64], in_=F_[:, 12 + h, cs], identity=identf[:]))
            fw.op('act', [pk1k], [k_('KVt')], lambda e: e.copy(out=t['KVt'][:, 0:6, :], in_=pk1[:, 0:384].rearrange("p (h c) -> p h c", h=6)))
            fw.op('act', [pv1k], [k_('KVt')], lambda e: e.copy(out=t['KVt'][:, 6:12, :], in_=pv1[:, 0:384].rearrange("p (h c) -> p h c", h=6)))
            yield
            pKK, pKKk = bank()
            pQK, pQKk = bank()
            for h in range(6):
                fw.op('pe', [k_('Fb')], [pKKk], lambda e: e.matmul(pKK[:, h * 64:(h + 1) * 64], lhsT=t['Fb'][:, 6 + h, :], rhs=t['Fb'][:, 6 + h, :], start=True, stop=True))
            for h in range(6):
                fw.op('pe', [k_('Fb')], [pQKk], lambda e: e.matmul(pQK[:, h * 64:(h + 1) * 64], lhsT=t['Fb'][:, 6 + h, :], rhs=t['Fb'][:, h, :], start=True, stop=True))
            yield
            fw.op('dve', [BGk, 'triU'], [k_('rhsG')], lambda e: e.tensor_tensor(out=t['rhsG'][:], in0=b3(g), in1=m3(triU[:]), op=ALU.mult))
            pG, pGk = bank()
            fw.op('pe', [k_('rhsG'), 'ones64'], [pGk], lambda e: e.matmul(pG[:, 0:384], lhsT=R_(ones64[:]), rhs=R_(t['rhsG'][:].rearrange("p h c -> p (h c)")), start=True, stop=True))
            fw.op('pe', [BGk, 'triU'], [pGk], lambda e: e.matmul(pG[:, 384:390], lhsT=R_(triU[:]), rhs=R_(g), start=True, stop=True))
            fw.op('act', [pGk], [k_('Gs')], lambda e: e.copy(out=t['Gs'][:], in_=pG[:, 0:390]))
            pGk = k_('Gs')
            pG3 = t['Gs'][:, 0:384].rearrange("p (h c) -> p h c", h=6)
            fw.op('pool', [pGk], [k_('gcol')], lambda e: e.tensor_copy(out=t['gcol'][:], in_=t['Gs'][:, 384:390]))
            fw.op('dve', [pGk, k_('gcol')], [k_('D1')], lambda e: e.tensor_tensor(out=t['D1'][:], in0=b3(t['gcol'][:]), in1=pG3, op=ALU.subtract))
            fw.op('pool', [k_('D1'), 'biasU'], [k_('D2')], lambda e: e.tensor_tensor(out=t['D2'][:], in0=m3(biasU[:]), in1=t['D1'][:], op=ALU.subtract))
            fw.op('dve', [k_('D1'), 'biasL'], [k_('D1')], lambda e: e.tensor_tensor(out=t['D1'][:], in0=t['D1'][:], in1=m3(biasL[:]), op=ALU.add))
            fw.op('act', [k_('D1')], [k_('E1')], lambda e: e.activation(out=t['E1'][:], in_=t['D1'][:], func=AF.Exp))
            fw.op('act', [k_('D2')], [k_('E2')], lambda e: e.activation(out=t['E2'][:], in_=t['D2'][:], func=AF.Exp))
            fw.op('act', [pGk], [k_('egrow')], lambda e: e.activation(out=t['egrow'][:], in_=pG3, func=AF.Exp))
            fw.op('act', [k_('gcol')], [k_('egcol')], lambda e: e.activation(out=t['egcol'][:], in_=t['gcol'][:], func=AF.Exp))
            fw.op('dve', [pGk, k_('gcol')], [k_('kdsc')], lambda e: e.tensor_tensor(out=t['kdsc'][:], in0=pG3[:, :, 63], in1=t['gcol'][:], op=ALU.subtract))
            fw.op('act', [k_('kdsc')], [k_('kdsc')], lambda e: e.activation(out=t['kdsc'][:], in_=t['kdsc'][:], func=AF.Exp))
            fw.op('pool', [BGk], [k_('nbeta')], lambda e: e.tensor_scalar(out=t['nbeta'][:], in0=beta, scalar1=-1.0, scalar2=None, op0=ALU.mult))
            fw.op('pool', [BGk, k_('egcol')], [k_('bege')], lambda e: e.tensor_tensor(out=t['bege'][:], in0=beta, in1=t['egcol'][:], op=ALU.mult))
            yield
            fw.op('dve', [pKKk, k_('E1')], [k_('N')], lambda e: e.tensor_tensor(out=t['N'][:], in0=pKK[:, 0:384].rearrange("p (h c) -> p h c", h=6), in1=t['E1'][:], op=ALU.mult))
            fw.op('dve', [k_('N'), k_('nbeta')], [k_('N')], lambda e: e.tensor_tensor(out=t['N'][:], in0=t['N'][:], in1=b3(t['nbeta'][:]), op=ALU.mult))
            fw.op('pool', [k_('N')], [k_('Nb')], lambda e: e.tensor_copy(out=t['Nb'][:], in_=t['N'][:]))
            fw.op('dve', [pQKk, k_('E2')], [k_('qkT')], lambda e: e.tensor_tensor(out=t['qkT'][:], in0=pQK[:, 0:384].rearrange("p (h c) -> p h c", h=6), in1=t['E2'][:], op=ALU.mult))
            fw.op('pool', [k_('KVt'), BGk], [k_('vb')], lambda e: e.tensor_tensor(out=t['vb'][:], in0=t['KVt'][:, 6:12, :], in1=b3(beta), op=ALU.mult))
            fw.op('pool', [k_('KVt'), k_('bege')], [k_('kbg')], lambda e: e.tensor_tensor(out=t['kbg'][:], in0=t['KVt'][:, 0:6, :], in1=b3(t['bege'][:]), op=ALU.mult))
            fw.op('pool', [k_('KVt'), k_('kdsc')], [k_('kd')], lambda e: e.tensor_tensor(out=t['kd'][:], in0=t['KVt'][:, 0:6, :], in1=b3(t['kdsc'][:]), op=ALU.mult))
            fw.op('pool', [Fk, k_('egrow')], [k_('qdT')], lambda e: e.tensor_tensor(out=t['qdT'][:], in0=qT, in1=t['egrow'][:], op=ALU.mult))
            yield
            pM, pMk = bank()
            for h in range(6):
                fw.op('pe', [k_('N'), 'identf'], [pMk], lambda e: e.transpose(out=pM[:, h * 64:(h + 1) * 64], in_=t['N'][:, h, :], identity=identf[:]))
            pM3 = pM[:, 0:384].rearrange("p (h c) -> p h c", h=6)
            fw.op('act', [pMk], [k_('Mp0')], lambda e: e.copy(out=t['Mp0'][:], in_=pM3))
            fw.op('pool', [k_('Mp0'), 'identf'], [k_('Tt0')], lambda e: e.tensor_tensor(out=t['Tt0'][:], in0=t['Mp0'][:], in1=m3(identf[:]), op=ALU.add))
            yield
            Np, Npk = t['Nb'], k_('Nb')
            Mp, Mpk = t['Mp0'], k_('Mp0')
            Tt, Ttk = t['Tt0'], k_('Tt0')
            for s_ in range(5):
                Nn, Nnk = t['Np%d' % (s_ % 2)], k_('Np%d' % (s_ % 2))
                Mn, Mnk = t['Mp%d' % ((s_ + 1) % 2)], k_('Mp%d' % ((s_ + 1) % 2))
                Tn, Tnk = t['Tt%d' % ((s_ + 1) % 2)], k_('Tt%d' % ((s_ + 1) % 2))
                pN2, pN2k = bank()
                for h in range(6):
                    fw.op('pe', [Mpk, Npk], [pN2k], lambda e: e.matmul(pN2[:, h * 64:(h + 1) * 64], lhsT=R_(Mp[:, h, :]), rhs=R_(Np[:, h, :]), start=True, stop=True))
                if s_ < 4:
                    pM2, pM2k = bank()
                    for h in range(6):
                        fw.op('pe', [Mpk, Npk], [pM2k], lambda e: e.matmul(pM2[:, h * 64:(h + 1) * 64], lhsT=R_(Np[:, h, :]), rhs=R_(Mp[:, h, :]), start=True, stop=True))
                yield
                fw.op('act', [pN2k], [Nnk], lambda e: e.copy(out=Nn[:], in_=pN2[:, 0:384].rearrange("p (h c) -> p h c", h=6)))
                if s_ < 4:
                    fw.op('dve', [pM2k], [Mnk], lambda e: e.tensor_copy(out=Mn[:], in_=pM2[:, 0:384].rearrange("p (h c) -> p h c", h=6)))
                pT_, pTk = bank()
                for h in range(6):
                    fw.op('pe', [Nnk, Ttk], [pTk], lambda e: e.matmul(pT_[:, h * 64:(h + 1) * 64], lhsT=R_(Nn[:, h, :]), rhs=R_(Tt[:, h, :]), start=True, stop=True))
                yield
                fw.op('dve', [pTk, Ttk], [Tnk], lambda e: e.tensor_tensor(out=Tn[:], in0=pT_[:, 0:384].rearrange("p (h c) -> p h c", h=6), in1=Tt[:], op=ALU.add))
                Np, Npk, Mp, Mpk, Tt, Ttk = Nn, Nnk, Mn, Mnk, Tn, Tnk
            yield
            pU, pUk = bank()
            pW, pWk = bank()
            for h in range(6):
                fw.op('pe', [Ttk, k_('vb')], [pUk], lambda e: e.matmul(pU[:, h * 64:(h + 1) * 64], lhsT=R_(Tt[:, h, :]), rhs=R_(t['vb'][:, h, :]), start=True, stop=True))
            for h in range(6):
                fw.op('pe', [Ttk, k_('kbg')], [pWk], lambda e: e.matmul(pW[:, h * 64:(h + 1) * 64], lhsT=R_(t['kbg'][:, h, :]), rhs=R_(Tt[:, h, :]), start=True, stop=True))
            yield
            fw.op('act', [pUk], [k_('U')], lambda e: e.copy(out=t['U'][:], in_=pU[:, 0:384].rearrange("p (h c) -> p h c", h=6)))
            fw.op('dve', [pWk], [k_('WT')], lambda e: e.tensor_copy(out=t['WT'][:], in_=pW[:, 0:384].rearrange("p (h c) -> p h c", h=6)))

        def scan(n):
            gi, j = n // 8, n % 8
            gp = gi % 2
            p = n % 2
            ix = lambda nm: (n % 4) if nm in SCAN_IN else p
            t = {nm: W2[nm][ix(nm)] for nm in names}
            k_ = lambda nm: '%s%d' % (nm, ix(nm))
            if j == 0:
                fw.dma('pool', Zs[0][:], zsr[:, gi * 8:(gi + 1) * 8, :], [], ['Z0'])
            So, Sok = S[n % 2], 'S%d' % (n % 2)
            Sn, Snk = S[(n + 1) % 2], 'S%d' % ((n + 1) % 2)
            Sob, Sobk = Sb[n % 2], 'Sb%d' % (n % 2)
            Snb, Snbk = Sb[(n + 1) % 2], 'Sb%d' % ((n + 1) % 2)
            pWS, pWSk = bank()
            for h in range(6):
                fw.op('pe', [k_('WT'), Sobk], [pWSk], lambda e: e.matmul(pWS[:, h * 64:(h + 1) * 64], lhsT=t['WT'][:, h, :], rhs=Sob[:, h, :], start=True, stop=True))
            yield
            fw.op('dve', [pWSk, k_('U')], [k_('vnew')], lambda e: e.tensor_tensor(out=t['vnew'][:], in0=t['U'][:], in1=pWS[:, 0:384].rearrange("p (h c) -> p h c", h=6), op=ALU.subtract))
            pdS, pdSk = bank()
            for h in range(6):
                fw.op('pe', [k_('kd'), k_('vnew')], [pdSk], lambda e: e.matmul(pdS[:, h * 64:(h + 1) * 64], lhsT=R_(t['kd'][:, h, :]), rhs=R_(t['vnew'][:, h, :]), start=True, stop=True))
            pO, pOk = bank()
            for h in range(6):
                fw.op('pe', [k_('qdT'), Sobk], [pOk], lambda e: e.matmul(pO[:, h * 64:(h + 1) * 64], lhsT=t['qdT'][:, h, :], rhs=Sob[:, h, :], start=True, stop=False))
                fw.op('pe', [k_('qkT'), k_('vnew')], [pOk], lambda e: e.matmul(pO[:, h * 64:(h + 1) * 64], lhsT=R_(t['qkT'][:, h, :]), rhs=R_(t['vnew'][:, h, :]), start=False, stop=True))
            yield
            fw.op('pool', [Sok, k_('egrow')], [k_('tmpS')], lambda e: e.tensor_tensor(out=t['tmpS'][:], in0=So[:], in1=t['egrow'][:, :, 63:64].to_broadcast([64, 6, 64]), op=ALU.mult))
            fw.op('dve', [k_('tmpS'), pdSk], [Snk], lambda e: e.tensor_tensor(out=Sn[:], in0=t['tmpS'][:], in1=pdS[:, 0:384].rearrange("p (h c) -> p h c", h=6), op=ALU.add))
            fw.op('pool', [Snk], [Snbk], lambda e: e.tensor_copy(out=Snb[:], in_=Sn[:]))
            yield
            fw.op('act', [pOk], [k_('O')], lambda e: e.copy(out=t['O'][:], in_=pO[:, 0:384].rearrange("p (h c) -> p h c", h=6)))
            fw.op('pool', [k_('O')], [k_('sqo')], lambda e: e.tensor_tensor(out=t['sqo'][:], in0=t['O'][:], in1=t['O'][:], op=ALU.mult))
            fw.op('dve', [k_('sqo')], [k_('ssum')], lambda e: e.tensor_reduce(out=t['ssum'][:], in_=t['sqo'][:], axis=AX.X, op=ALU.add))
            fw.op('dve', [k_('ssum')], [k_('ssum')], lambda e: e.tensor_scalar(out=t['ssum'][:], in0=t['ssum'][:], scalar1=1.0 / 64, scalar2=EPS, op0=ALU.mult, op1=ALU.add))
            fw.op('act', [k_('ssum')], [k_('ssum')], lambda e: e.activation(out=t['ssum'][:], in_=t['ssum'][:], func=AF.Sqrt))
            fw.op('dve', [k_('ssum')], [k_('ssum')], lambda e: e.reciprocal(out=t['ssum'][:], in_=t['ssum'][:]))
            fw.op('pool', [k_('O'), k_('ssum')], [k_('Yt')], lambda e: e.tensor_tensor(out=t['Yt'][:], in0=t['O'][:], in1=b3(t['ssum'][:]), op=ALU.mult))
            fw.op('pool', [k_('Yt'), 'gnorm'], [k_('Yt')], lambda e: e.tensor_tensor(out=t['Yt'][:], in0=t['Yt'][:], in1=m3(gnorm[:]), op=ALU.mult))
            fw.op('dve', [k_('Yt'), 'Z0'], ['Y0'], lambda e: e.tensor_tensor(out=Yst[0][:, j, :].rearrange("p (h c) -> p h c", h=6), in0=t['Yt'][:], in1=Zs[0][:, j, :].rearrange("p (h c) -> p h c", h=6), op=ALU.mult))
            if j == 7:
                fw.dma('pool', yr[:, gi * 8:(gi + 1) * 8, 0:384], Yst[0][:], ['Y0'], ['y_a'])

        def run_il(gens):
            gens = list(gens)
            while gens:
                for g_ in list(gens):
                    try:
                        next(g_)
                    except StopIteration:
                        gens.remove(g_)

        def seq(*gens):
            for g_ in gens:
                yield from g_

        load_group(0)
        run_il([prep(0), prep(1)])
        for n in range(0, NCH, 2):
            tasks = [seq(scan(n), scan(n + 1))]
            if n + 2 < NCH:
                if (n + 2) % 8 == 0:
                    load_group((n + 2) // 8)
                tasks += [prep(n + 2), prep(n + 3)]
            run_il(tasks)
        fw.barrier()


def _nsa_consts(T):
    c = {}
    Ns = T // 64
    NcP = T // 16
    Nc = NcP - 1
    WV = 64 + Ns + 1
    cc = np.arange(NcP)
    nn = np.arange(Ns)
    mat = ((cc[:, None] >= 4 * nn[None, :] - 1) & (cc[:, None] <= 4 * nn[None, :] + 3) & (cc[:, None] < Nc)).astype(np.float32)
    v1 = np.zeros((NcP, WV), np.float32)
    v1[:, 64:64 + Ns] = mat
    v1[:, WV - 1] = 1.0
    c['vc1init'] = v1.astype(ml_dtypes.bfloat16)
    cl = np.arange(128)
    q = np.arange(128)
    cm = np.zeros((128, 17, 3, 128), np.float32)
    for idx in range(17):
        ok = (16 * cl[:, None] + 31) <= (128 * idx + q[None, :])
        cm[:, idx, :, :] = np.where(ok, 0.0, NEG)[:, None, :]
    c['cmask'] = cm.reshape(128, 17, 384).astype(ml_dtypes.bfloat16)
    c['causb'] = np.tile(np.where(cl[:, None] <= q[None, :], 0.0, NEG), (1, 3)).astype(ml_dtypes.bfloat16)
    c['bandb'] = np.tile(np.where(cl[:, None] > q[None, :], 0.0, NEG), (1, 3)).astype(ml_dtypes.bfloat16)
    TE = min(T, 4096)
    t = np.arange(TE)
    c['exT'] = (((t[None, :] // 64) % 64) == np.arange(64)[:, None]).astype(np.float32).astype(ml_dtypes.bfloat16)
    return c


def phase_cmp(K, l):
    nc, fw, T = K.nc, K.fw, K.T
    NcP = T // 16
    Ns = T // 64
    WV = 64 + Ns + 1
    with ExitStack() as st:
        sb = lambda name, shape, dt=F32: st.enter_context(nc.sbuf_tensor(name + "_L%d" % l, shape, dt))
        ps = lambda name, shape, dt=F32: st.enter_context(nc.psum_tensor(name + "_L%d" % l, shape, dt))
        w1f = sb("c_w1f", [64, 32, 256])
        w1b = sb("c_w1b", [64, 32, 256], BF16)
        w2f = sb("c_w2f", [128, 2, 64])
        w2b = sb("c_w2b", [128, 2, 64], BF16)
        posf = sb("c_posf", [64, 32])
        posb = sb("c_posb", [64, 32], BF16)
        bias1 = sb("c_bias1", [128, 2])
        XT = sb("c_XT", [64, T + 16], BF16)
        h1T = sb("c_h1T", [128, 2, 512], BF16)
        ones64 = sb("c_ones", [64, 64])
        kg = sb("c_kg", [64, 1])
        sq = sb("c_sq", [64, 512])
        rn = sb("c_rn", [64, 512])
        ko = sb("c_ko", [64, 512], BF16)
        vst = sb("c_vst", [128, WV], BF16)
        pH = [ps("c_pH%d" % i, [128, 512]) for i in range(2)]
        pB = ps("c_pB", [128, 512])
        pK = ps("c_pK", [128, 512])
        pS = ps("c_pS", [128, 512])
        fw.op('pool', [], ['ones64'], lambda e: e.memset(ones64[:], 1.0))
        fw.dma('sp', kg[:], K.w['nsa_k_norm'][l, 0].rearrange("(p o) -> p o", o=1), [], ['kg'])
        for kv in range(2):
            fw.dma('sp', w1f[:], K.w['cmp_w1'][l, kv].rearrange("(l d) j -> d l j", d=64), [], ['w1f'])
            fw.op('dve', ['w1f'], ['w1b'], lambda e: e.tensor_copy(out=w1b[:], in_=w1f[:]))
            fw.dma('sp', w2f[:], K.w['cmp_w2'][l, kv].rearrange("(c p) d -> p c d", p=128), [], ['w2f'])
            fw.op('dve', ['w2f'], ['w2b'], lambda e: e.tensor_copy(out=w2b[:], in_=w2f[:]))
            fw.dma('sp', posf[:], K.w['cmp_pos'][l, kv].rearrange("l d -> d l"), [], ['posf'])
            fw.op('dve', ['posf'], ['posb'], lambda e: e.tensor_copy(out=posb[:], in_=posf[:]))
            for jc in range(2):
                for ll in range(32):
                    fw.op('pe', ['w1b', 'posb'], ['pB'], lambda e: e.matmul(pB[:, jc:jc + 1], lhsT=w1b[:, ll, jc * 128:(jc + 1) * 128], rhs=posb[:, ll:ll + 1], start=(ll == 0), stop=(ll == 31)))
            fw.op('act', ['pB'], ['bias1'], lambda e: e.copy(out=bias1[:], in_=pB[:, 0:2]))
            src = K.kcT if kv == 0 else K.vcT
            for hk in range(2):
                fw.dma('sp', XT[:, 0:T], src[hk * 64:(hk + 1) * 64, :], [], ['XT'])
                fw.op('pool', [], ['XT'], lambda e: e.memset(XT[:, T:T + 16], 0.0))
                for n0 in range(0, NcP, 512):
                    nn = min(512, NcP - n0)
                    for jc in range(2):
                        p_, pk = pH[jc], 'pH%d' % jc
                        for ll in range(32):
                            fw.op('pe', ['w1b', 'XT'], [pk], lambda e: e.matmul(p_[:, 0:nn], lhsT=w1b[:, ll, jc * 128:(jc + 1) * 128],
                                                                                rhs=XT[:, ll + 16 * n0:ll + 16 * (n0 + nn - 1) + 1:16], start=(ll == 0), stop=(ll == 31)))
                        fw.op('act', [pk, 'bias1'], ['h1T'], lambda e: e.activation(out=h1T[:, jc, 0:nn], in_=p_[:, 0:nn], func=AF.Silu, bias=bias1[:, jc:jc + 1]))
                    if kv == 0:
                        for jc in range(2):
                            fw.op('pe', ['h1T', 'w2b'], ['pK'], lambda e: e.matmul(pK[0:64, 0:nn], lhsT=w2b[:, jc, :], rhs=h1T[:, jc, 0:nn], start=(jc == 0), stop=(jc == 1)))
                        fw.op('act', ['pK'], ['sq'], lambda e: e.activation(out=sq[:, 0:nn], in_=pK[0:64, 0:nn], func=AF.Square))
                        fw.op('pe', ['sq', 'ones64'], ['pS'], lambda e: e.matmul(pS[0:64, 0:nn], lhsT=ones64[:], rhs=sq[:, 0:nn], start=True, stop=True))
                        fw.op('dve', ['pS'], ['rn'], lambda e: e.tensor_scalar(out=rn[:, 0:nn], in0=pS[0:64, 0:nn], scalar1=1.0 / 64, scalar2=EPS, op0=ALU.mult, op1=ALU.add))
                        fw.op('act', ['rn'], ['rn'], lambda e: e.activation(out=rn[:, 0:nn], in_=rn[:, 0:nn], func=AF.Sqrt))
                        fw.op('dve', ['rn'], ['rn'], lambda e: e.reciprocal(out=rn[:, 0:nn], in_=rn[:, 0:nn]))
                        fw.op('dve', ['pK', 'rn', 'kg'], ['ko'], lambda e: e.scalar_tensor_tensor(out=ko[:, 0:nn], in0=pK[0:64, 0:nn], scalar=kg[:, 0:1], in1=rn[:, 0:nn], op0=ALU.mult, op1=ALU.mult))
                        fw.dma('pool', K.KcT[hk, :, n0:n0 + nn], ko[:, 0:nn], ['ko'], ['KcT'])
                    else:
                        for c0 in range(0, nn, 128):
                            cn = min(128, nn - c0)
                            for jc in range(2):
                                fw.op('pe', ['h1T', 'w2b'], ['pK'], lambda e: e.matmul(pK[0:cn, 0:64], lhsT=h1T[:, jc, c0:c0 + cn], rhs=w2b[:, jc, :], start=(jc == 0), stop=(jc == 1)))
                            fw.dma('sp', vst[0:cn, :], K.cn['vc1init'][n0 + c0:n0 + c0 + cn, :], [], ['vst'])
                            fw.op('act', ['pK'], ['vst'], lambda e: e.copy(out=vst[0:cn, 0:64], in_=pK[0:cn, 0:64]))
                            fw.dma('pool', K.Vc1[hk, n0 + c0:n0 + c0 + cn, :], vst[0:cn, :], ['vst'], ['Vc1'])
        fw.barrier()


def phase_nsa(K, l):
    nc, fw, T = K.nc, K.fw, K.T
    NcP = T // 16
    Ns = T // 64
    WV = 64 + Ns + 1
    NQ = T // 128
    NCC = (NcP + 127) // 128
    NG = (Ns + 63) // 64
    SW = max(64 + Ns, 128 * 1 if Ns < 64 else 64 + Ns)
    TE = min(T, 4096)
    with ExitStack() as st:
        sb = lambda name, shape, dt=F32: st.enter_context(nc.sbuf_tensor(name + "_L%d" % l, shape, dt))
        ps = lambda name, shape, dt=F32: st.enter_context(nc.psum_tensor(name + "_L%d" % l, shape, dt))
        identb = sb("n_identb", [128, 128], BF16)
        identf = sb("n_identf", [128, 128])
        cmask = sb("n_cmask", [128, 17, 384], BF16)
        causb = sb("n_causb", [128, 384], BF16)
        bandb = sb("n_bandb", [128, 384], BF16)
        KX = sb("n_KX", [128, T], BF16)
        KwT = sb("n_KwT", [64, T], BF16)
        Vs1 = sb("n_Vs1", [128, NQ, 65], BF16)
        Vw1 = sb("n_Vw1", [128, NQ, 65], BF16)
        KcT = sb("n_KcT", [64, NCC * 128], BF16)
        Vc1 = sb("n_Vc1", [128, NCC, WV], BF16)
        QX = [sb("n_QX%d" % i, [128, NG, 3, 128], BF16) for i in range(2)]
        gts = [sb("n_gt%d" % i, [128, 18]) for i in range(2)]
        E = [sb("n_E%d" % i, [128, 384], BF16) for i in range(4)]
        rZ = sb("n_rZ", [128, 3])
        coef = sb("n_coef", [128, 3])
        imp = sb("n_imp", [128, Ns])
        imp2 = sb("n_imp2", [128, Ns])
        m1 = sb("n_m1", [128, 8])
        m2 = sb("n_m2", [128, 8])
        sel = sb("n_sel", [128, Ns])
        nsel = sb("n_nsel", [128, SW], BF16)
        nselT = [sb("n_nselT%d" % i, [128, SW], BF16) for i in range(2)]
        OT = [sb("n_OT%d" % i, [65, 384]) for i in range(2)]
        rd = sb("n_rd", [128, 1])
        Yb = [sb("n_Yb%d" % i, [128, 3, 64]) for i in range(2)]
        pS = [ps("n_pS%d" % i, [128, 512]) for i in range(3)]
        pOC = [ps("n_pOC%d" % i, [128, 512]) for i in range(3)]
        pOs = ps("n_pOs", [128, 512])
        pOw = pOs
        pTn = ps("n_pTn", [128, 128], BF16)

        fw.dma('sp', identb[:], K.c['identb'], [], ['identb'])
        fw.dma('sp', identf[:], K.c['identf'], [], ['identf'])
        fw.dma('sp', cmask[:], K.cn['cmask'], [], ['cmask'])
        fw.dma('sp', causb[:], K.cn['causb'], [], ['causb'])
        fw.dma('sp', bandb[:], K.cn['bandb'], [], ['bandb'])
        fw.op('pool', [], ['nsel'], lambda e: e.memset(nsel[:], 0.0))
        fw.op('pool', [], ['KcT'], lambda e: e.memset(KcT[:], 0.0))
        cnt = [0]
        rc = {}

        def rot(lst, key):
            rc[key] = rc.get(key, -1) + 1
            i = rc[key] % len(lst)
            return lst[i], '%s%d' % (key, i)

        yq = K.y.rearrange("(i p) d -> i p d", p=128)
        gtq = K.gt.rearrange("(i p) d -> i p d", p=128)
        for hk in range(2):
            fw.dma('sp', KX[0:64, :], K.ksT[hk * 64:(hk + 1) * 64, :], [], ['KX'])
            for t0 in range(0, T, TE):
                fw.dma('sp', KX[64:128, t0:t0 + TE], K.cn['exT'], [], ['KX'])
            fw.dma('sp', KwT[:], K.kwT[hk * 64:(hk + 1) * 64, :], [], ['KwT'])
            fw.dma('sp', Vs1[:], K.vs1.rearrange("(c p) w -> p c w", p=128)[:, :, hk * 65:(hk + 1) * 65], [], ['Vs1'])
            fw.dma('sp', Vw1[:], K.vw1.rearrange("(c p) w -> p c w", p=128)[:, :, hk * 65:(hk + 1) * 65], [], ['Vw1'])
            fw.dma('sp', KcT[:, 0:NcP], K.KcT[hk], [], ['KcT'])
            if NcP >= 128:
                fw.dma('sp', Vc1[:], K.Vc1[hk].rearrange("(c p) w -> p c w", p=128), [], ['Vc1'])
            else:
                fw.op('pool', [], ['Vc1'], lambda e: e.memset(Vc1[:], 0.0))
                fw.dma('sp', Vc1[0:NcP, 0, :], K.Vc1[hk], [], ['Vc1'])
            def run_branch(tiles, score, pv, look=2):
                pend = [score(tk) for tk in tiles[:look]]
                for ti, tk in enumerate(tiles):
                    p_, pk = pend.pop(0)
                    if ti + look < len(tiles):
                        pend.append(score(tiles[ti + look]))
                    e_, ek = rot(E, 'E')
                    fw.op('act', [pk], [ek], lambda e: e.activation(out=e_[:], in_=p_[:, 0:384], func=AF.Exp))
                    pv(tk, e_, ek)

            def partA(i):
                par = i % 2
                Q_, Qk = QX[par], 'QX%d' % par
                g_, gk = gts[par], 'gt%d' % par
                Y_, Yk = Yb[par], 'Yb%d' % par
                ngrp = (2 * i + 1) // 64 + 1
                for grp in range(ngrp):
                    fw.dma('sp', Q_[0:64, grp, :, :], K.qn[hk * 192:(hk + 1) * 192, i * 128:(i + 1) * 128].rearrange("(g d) q -> d g q", d=64), [], [Qk])
                fw.dma('sp', g_[:], gtq[i], [], [gk])
                q0 = Q_[0:64, 0, :, :].rearrange("p g q -> p (g q)")
                jmax = (8 * i + 6) // 128

                def cmp_score(jc):
                    p_, pk = rot(pS, 'pS')
                    full = (16 * (128 * jc + 127) + 31) <= 128 * i
                    fw.op('pe', ['KcT', Qk], [pk], lambda e: e.matmul(p_[:, 0:384], lhsT=KcT[:, jc * 128:(jc + 1) * 128], rhs=q0, start=True, stop=full))
                    if not full:
                        idx = (128 * i - 2048 * jc) // 128
                        fw.op('pe', ['identb', 'cmask'], [pk], lambda e: e.matmul(p_[:, 0:384], lhsT=identb[:], rhs=cmask[:, idx, :], start=False, stop=True))
                    return p_, pk

                def cmp_pv(jc, e_, ek):
                    for g in range(3):
                        fw.op('pe', [ek, 'Vc1'], ['pOC%d' % g], lambda e: e.matmul(pOC[g][:, 0:WV], lhsT=e_[:, g * 128:(g + 1) * 128], rhs=Vc1[:, jc, :], start=(jc == 0), stop=(jc == jmax)))

                run_branch(list(range(jmax + 1)), cmp_score, cmp_pv)
                for g in range(3):
                    fw.op('dve', ['pOC%d' % g], ['rZ'], lambda e: e.tensor_scalar(out=rZ[:, g:g + 1], in0=pOC[g][:, WV - 1:WV], scalar1=1e-30, scalar2=None, op0=ALU.max))
                fw.op('dve', ['rZ'], ['rZ'], lambda e: e.reciprocal(out=rZ[:], in_=rZ[:]))
                gv = g_[:, hk * 9:(hk + 1) * 9].rearrange("p (g b) -> p g b", b=3)
                fw.op('dve', ['rZ', gk], ['coef'], lambda e: e.tensor_tensor(out=coef[:], in0=rZ[:], in1=gv[:, :, 0], op=ALU.mult))
                for g in range(3):
                    fw.op('dve', ['pOC%d' % g, 'coef'], [Yk], lambda e: e.tensor_scalar(out=Y_[:, g, :], in0=pOC[g][:, 0:64], scalar1=coef[:, g:g + 1], scalar2=None, op0=ALU.mult))
                    if g == 0:
                        fw.op('dve', ['pOC0', 'rZ'], ['imp'], lambda e: e.tensor_scalar(out=imp[:], in0=pOC[0][:, 64:64 + Ns], scalar1=rZ[:, 0:1], scalar2=None, op0=ALU.mult))
                    else:
                        fw.op('dve', ['pOC%d' % g, 'rZ', 'imp'], ['imp'], lambda e: e.scalar_tensor_tensor(out=imp[:], in0=pOC[g][:, 64:64 + Ns], scalar=rZ[:, g:g + 1], in1=imp[:], op0=ALU.mult, op1=ALU.add))
                if 2 * i + 2 < Ns:
                    fw.op('pool', ['imp'], ['imp'], lambda e: e.memset(imp[:, 2 * i + 2:Ns], -1.0))
                fw.op('pool', ['imp'], ['imp'], lambda e: e.memset(imp[0:64, 2 * i + 1:2 * i + 2], -1.0))
                fw.op('pool', ['imp'], ['imp'], lambda e: e.memset(imp[64:128, 2 * i + 1:2 * i + 2], 1e4))
                fw.op('pool', ['imp'], ['imp'], lambda e: e.memset(imp[:, 2 * i:2 * i + 1], 1e4))
                if i >= 1:
                    fw.op('pool', ['imp'], ['imp'], lambda e: e.memset(imp[0:64, 2 * i - 1:2 * i], 1e4))
                fw.op('pool', ['imp'], ['imp'], lambda e: e.memset(imp[:, 0:1], 1e4))
                fw.op('dve', ['imp'], ['m1'], lambda e: e.max(out=m1[:], in_=imp[:]))
                fw.op('dve', ['imp', 'm1'], ['imp2'], lambda e: e.match_replace(out=imp2[:], in_to_replace=m1[:], in_values=imp[:], imm_value=-2.0))
                fw.op('dve', ['imp2'], ['m2'], lambda e: e.max(out=m2[:], in_=imp2[:]))
                fw.op('dve', ['imp', 'm2'], ['sel'], lambda e: e.tensor_scalar(out=sel[:], in0=imp[:], scalar1=m2[:, 7:8], scalar2=None, op0=ALU.is_ge))
                fw.op('dve', ['sel'], ['nsel'], lambda e: e.tensor_scalar(out=nsel[:, 64:64 + Ns], in0=sel[:], scalar1=-NEG, scalar2=NEG, op0=ALU.mult, op1=ALU.add))
                fw.op('dve', ['nsel'], ['nselT%d' % par], lambda e: e.tensor_copy(out=nselT[par][:], in_=nsel[:]))

            def partA2(i):
                par = i % 2
                Q_, Qk = QX[par], 'QX%d' % par
                ngrp = (2 * i + 1) // 64 + 1
                for grp in range(ngrp):
                    fw.op('pe', ['nselT%d' % par, 'identb'], ['pTn'], lambda e: e.transpose(out=pTn[:, :], in_=nselT[par][:, grp * 64:grp * 64 + 128], identity=identb[:]))
                    fw.op('dve', ['pTn'], [Qk], lambda e: e.tensor_copy(out=Q_[64:128, grp, :, :], in_=pTn[64:128, :].unsqueeze(1).to_broadcast([64, 3, 128])))

            def partB(i):
                par = i % 2
                Q_, Qk = QX[par], 'QX%d' % par
                g_, gk = gts[par], 'gt%d' % par
                Y_, Yk = Yb[par], 'Yb%d' % par
                q0 = Q_[0:64, 0, :, :].rearrange("p g q -> p (g q)")
                gv = g_[:, hk * 9:(hk + 1) * 9].rearrange("p (g b) -> p g b", b=3)
                k0 = max(0, i - 4)

                def win_score(kc):
                    p_, pk = rot(pS, 'pS')
                    msk = causb if kc == i else (bandb if kc == i - 4 else None)
                    fw.op('pe', ['KwT', Qk], [pk], lambda e: e.matmul(p_[:, 0:384], lhsT=KwT[:, kc * 128:(kc + 1) * 128], rhs=q0, start=True, stop=(msk is None)))
                    if msk is not None:
                        fw.op('pe', ['identb', 'causb', 'bandb'], [pk], lambda e: e.matmul(p_[:, 0:384], lhsT=identb[:], rhs=msk[:], start=False, stop=True))
                    return p_, pk

                def win_pv(kc, e_, ek):
                    fw.op('pe', [ek, 'Vw1'], ['pOs'], lambda e: e.matmul(pOs[0:65, 0:384], lhsT=Vw1[:, kc, :], rhs=e_[:], start=(kc == k0), stop=(kc == i)))

                def sel_score(kc):
                    grp = kc // 32
                    p_, pk = rot(pS, 'pS')
                    fw.op('pe', ['KX', Qk], [pk], lambda e: e.matmul(p_[:, 0:384], lhsT=KX[:, kc * 128:(kc + 1) * 128], rhs=Q_[:, grp, :, :].rearrange("p g q -> p (g q)"), start=True, stop=(kc < i)))
                    if kc == i:
                        fw.op('pe', ['identb', 'causb'], [pk], lambda e: e.matmul(p_[:, 0:384], lhsT=identb[:], rhs=causb[:], start=False, stop=True))
                    return p_, pk

                def sel_pv(kc, e_, ek):
                    fw.op('pe', [ek, 'Vs1'], ['pOs'], lambda e: e.matmul(pOs[0:65, 0:384], lhsT=Vs1[:, kc, :], rhs=e_[:], start=(kc == 0), stop=(kc == i)))

                run_branch(list(range(k0, i + 1)), win_score, win_pv)
                fw.op('act', ['pOs'], ['OT0'], lambda e: e.copy(out=OT[0][:], in_=pOs[0:65, 0:384]))
                run_branch(list(range(i + 1)), sel_score, sel_pv)
                fw.op('act', ['pOs'], ['OT1'], lambda e: e.copy(out=OT[1][:], in_=pOs[0:65, 0:384]))
                if i + 1 < NQ:
                    partA2(i + 1)
                for bi in range(2):
                    o_, ok = OT[bi], 'OT%d' % bi
                    for g in range(3):
                        pf, pfk = pOC[g], 'pOC%d' % g
                        fw.op('pe', [ok, 'identf'], [pfk], lambda e: e.transpose(out=pf[:, 0:65], in_=o_[0:65, g * 128:(g + 1) * 128], identity=identf[0:65, 0:65]))
                        fw.op('dve', [pfk], ['rd'], lambda e: e.reciprocal(out=rd[:], in_=pf[:, 64:65]))
                        fw.op('dve', ['rd', gk], ['rd'], lambda e: e.tensor_tensor(out=rd[:], in0=rd[:], in1=gv[:, g, 2 - bi:3 - bi], op=ALU.mult))
                        fw.op('dve', [pfk, 'rd', Yk], [Yk], lambda e: e.scalar_tensor_tensor(out=Y_[:, g, :], in0=pf[:, 0:64], scalar=rd[:, 0:1], in1=Y_[:, g, :], op0=ALU.mult, op1=ALU.add))
                fw.dma('pool', yq[i][:, 384 + hk * 192:384 + (hk + 1) * 192], Y_[:].rearrange("p g d -> p (g d)"), [Yk], ['y_b'])

            partA(0)
            partA2(0)
            for i in range(NQ):
                if i + 1 < NQ:
                    partA(i + 1)
                partB(i)
        fw.barrier()


def phase_p3(K, l):
    nc, fw, T = K.nc, K.fw, K.T
    NM = T // 256
    with ExitStack() as st:
        sb = lambda name, shape, dt=F32: st.enter_context(nc.sbuf_tensor(name + "_L%d" % l, shape, dt))
        ps = lambda name, shape, dt=F32: st.enter_context(nc.psum_tensor(name + "_L%d" % l, shape, dt))
        Wo = sb("f_Wo", [128, 8, 1024], BF16)
        W1 = sb("f_W1", [128, 8, 4096], BF16)
        W2 = sb("f_W2", [128, 32, 1024], BF16)
        wst = [sb("f_wst%d" % i, [128, 1024]) for i in range(2)]
        gf = sb("f_gf", [128, 8])
        identb = sb("f_identb", [128, 128], BF16)
        yt = sb("f_yt", [128, 2, 1024])
        yb = sb("f_yb", [128, 2, 1024], BF16)
        yT = sb("f_yT", [128, 8, 256], BF16)
        xr = [sb("f_xr%d" % i, [128, 2, 1024]) for i in range(1)]
        xb = sb("f_xb", [128, 2, 1024], BF16)
        xnT = sb("f_xnT", [128, 8, 256], BF16)
        hT = sb("f_hT", [128, 32, 256], BF16)
        rl = [sb("f_rl%d" % i, [128, 256]) for i in range(2)]
        junk = sb("f_junk", [128, 1024], BF16)
        ss = sb("f_ss", [128, 2])
        rstd = sb("f_rstd", [128, 2])
        pT = [ps("f_pT%d" % i, [128, 256], BF16) for i in range(2)]
        pA = [ps("f_pA%d" % i, [128, 512]) for i in range(4)]
        pH = [ps("f_pH%d" % i, [128, 512]) for i in range(2)]
        fw.dma('sp', identb[:], K.c['identb'], [], ['identb'])
        fw.dma('sp', gf[:], K.w['norm_ffn'][l].rearrange("(k p) -> p k", p=128), [], ['gf'])
        wi = [0]

        def loadw(dst_ap, src_ap, scal=None):
            i = wi[0] % 2
            wi[0] += 1
            fw.dma('sp' if i else 'pool', wst[i][:], src_ap, [], ['wst%d' % i])
            if scal is None:
                fw.op('dve' if i else 'pool', ['wst%d' % i], ['W'], lambda e: e.tensor_copy(out=dst_ap, in_=wst[i][:]))
            else:
                fw.op('dve' if i else 'pool', ['wst%d' % i, 'gf'], ['W'], lambda e: e.tensor_scalar(out=dst_ap, in0=wst[i][:], scalar1=scal, scalar2=None, op0=ALU.mult))
        for kc in range(8):
            loadw(Wo[:, kc, :], K.w['w_out'][l, kc * 128:(kc + 1) * 128, :])
            for q4 in range(4):
                loadw(W1[:, kc, q4 * 1024:(q4 + 1) * 1024], K.w['w_ffn1'][l, kc * 128:(kc + 1) * 128, q4 * 1024:(q4 + 1) * 1024], gf[:, kc:kc + 1])
        for fc in range(32):
            loadw(W2[:, fc, :], K.w['w_ffn2'][l, fc * 128:(fc + 1) * 128, :])
        ysrc = K.y.rearrange("(m j p) d -> m p j d", p=128, j=2)
        xsrc = K.xin[l].rearrange("(m j p) d -> m p j d", p=128, j=2)
        xdst = K.xout[l].rearrange("(m j p) d -> m p j d", p=128, j=2)
        cnt = [0]
        rc = {}

        def rot(lst, key):
            rc[key] = rc.get(key, -1) + 1
            i = rc[key] % len(lst)
            return lst[i], '%s%d' % (key, i)

        def transposes(src, srck, dst, dstk):
            for kc in range(8):
                p_, pk = pT[kc % 2], 'pT%d' % (kc % 2)
                for j in range(2):
                    fw.op('pe', [srck, 'identb'], [pk], lambda e: e.transpose(out=p_[:, j * 128:(j + 1) * 128], in_=src[:, j, kc * 128:(kc + 1) * 128], identity=identb[:]))
                if kc % 2:
                    fw.op('act', [pk], [dstk], lambda e: e.copy(out=dst[:, kc, :], in_=p_[:]))
                else:
                    fw.op('dve', [pk], [dstk], lambda e: e.tensor_copy(out=dst[:, kc, :], in_=p_[:]))

        for m in range(NM):
            x_, xk = xr[0], 'xr0'
            fw.dma('sp', yt[:], ysrc[m], [], ['yt'])
            fw.dma('sp', x_[:], xsrc[m], [], [xk])
            fw.op('pool', ['yt'], ['yb'], lambda e: e.tensor_copy(out=yb[:], in_=yt[:]))
            transposes(yb, 'yb', yT, 'yT')
            for j in range(2):
                for nh in range(2):
                    p_, pk = rot(pA, 'pA')
                    for kc in range(8):
                        fw.op('pe', ['yT', 'W'], [pk], lambda e: e.matmul(p_[:], lhsT=yT[:, kc, j * 128:(j + 1) * 128], rhs=Wo[:, kc, nh * 512:(nh + 1) * 512], start=(kc == 0), stop=(kc == 7)))
                    fw.op('dve', [pk, xk], [xk], lambda e: e.tensor_tensor(out=x_[:, j, nh * 512:(nh + 1) * 512], in0=x_[:, j, nh * 512:(nh + 1) * 512], in1=p_[:], op=ALU.add))
            fw.op('dve', [], ['ss'], lambda e: e.memset(ss[:], 0.0))
            for j in range(2):
                fw.op('act', [xk, 'ss'], ['junk', 'ss'], lambda e: e.activation(out=junk[:], in_=x_[:, j, :], func=AF.Square, accum_out=ss[:, j:j + 1]))
            fw.op('dve', ['ss'], ['rstd'], lambda e: e.tensor_scalar(out=rstd[:], in0=ss[:], scalar1=1.0 / D, scalar2=EPS, op0=ALU.mult, op1=ALU.add))
            fw.op('act', ['rstd'], ['rstd'], lambda e: e.activation(out=rstd[:], in_=rstd[:], func=AF.Sqrt))
            fw.op('dve', ['rstd'], ['rstd'], lambda e: e.reciprocal(out=rstd[:], in_=rstd[:]))
            for j in range(2):
                fw.op('pool', [xk, 'rstd'], ['xb'], lambda e: e.tensor_scalar(out=xb[:, j, :], in0=x_[:, j, :], scalar1=rstd[:, j:j + 1], scalar2=None, op0=ALU.mult))
            transposes(xb, 'xb', xnT, 'xnT')
            for fc in range(32):
                p_, pk = rot(pH, 'pH')
                for kc in range(8):
                    fw.op('pe', ['xnT', 'W'], [pk], lambda e: e.matmul(p_[:, 0:256], lhsT=W1[:, kc, fc * 128:(fc + 1) * 128], rhs=xnT[:, kc, :], start=(kc == 0), stop=(kc == 7)))
                r_, rk = rot(rl, 'rl')
                fw.op('act', [pk], [rk], lambda e: e.activation(out=r_[:], in_=p_[:, 0:256], func=AF.Relu))
                fw.op('pool' if fc % 2 else 'dve', [rk], ['hT%d' % fc], lambda e: e.tensor_tensor(out=hT[:, fc, :], in0=r_[:], in1=r_[:], op=ALU.mult))
            for j in range(2):
                for nh in range(2):
                    p_, pk = rot(pA, 'pA')
                    for fc in range(32):
                        fw.op('pe', ['hT%d' % fc, 'W'], [pk], lambda e: e.matmul(p_[:], lhsT=hT[:, fc, j * 128:(j + 1) * 128], rhs=W2[:, fc, nh * 512:(nh + 1) * 512], start=(fc == 0), stop=(fc == 31)))
                    fw.op('dve', [pk, xk], [xk], lambda e: e.tensor_tensor(out=x_[:, j, nh * 512:(nh + 1) * 512], in0=x_[:, j, nh * 512:(nh + 1) * 512], in1=p_[:], op=ALU.add))
            fw.dma('pool', xdst[m], x_[:], [xk], ['xout'])
        fw.barrier()


def kernel(**inputs):
    x = np.asarray(inputs["x"], dtype=np.float32)
    B, T, _ = x.shape
    nc, K = build(T, L=2)
    base = {n: np.ascontiguousarray(np.asarray(inputs[n], dtype=np.float32)) for n in WNAMES}
    for n, v in K.consts.items():
        base['c_' + n] = v
    for n, v in K.consts_n.items():
        base['cn_' + n] = v
    in_maps = []
    for b in range(B):
        m = dict(base)
        m['x'] = np.ascontiguousarray(x[b])
        in_maps.append(m)
    res = run_bass_kernel_spmd(nc, in_maps, core_ids=list(range(B)))
    return np.stack([np.asarray(r["out"], dtype=np.float32) for r in res.results], axis=0)
```

```python
import os
import numpy as np
import ml_dtypes
from contextlib import ExitStack
import concourse.bass as bass
import concourse.mybir as mybir
from concourse.bass_utils import run_bass_kernel_spmd

F32 = mybir.dt.float32
BF16 = mybir.dt.bfloat16
AF = mybir.ActivationFunctionType
ALU = mybir.AluOpType
AX = mybir.AxisListType

D = 1024
DIN = 2974
DFF = 4096
EPS = 1e-6
NEG = -30000.0

C_QKV, C_Z, C_B, C_A, C_QB, C_KV, C_GATE, C_U = 0, 1152, 1536, 1542, 1548, 1932, 2700, 2718
C_KC, C_VC, C_KS, C_VS, C_KW, C_VW = 1932, 2060, 2188, 2316, 2444, 2572


class FW:
    def __init__(self, nc, stack):
        self.nc = nc
        self.eng = {'pe': nc.tensor, 'act': nc.scalar, 'dve': nc.vector, 'pool': nc.gpsimd, 'sp': nc.sync}
        self.semh = {}
        self.cnt = {}
        for e in ['pe', 'act', 'dve', 'pool']:
            self.semh[e] = stack.enter_context(nc.semaphore('s_' + e))
            self.cnt[e] = 0
        self.NDS = 6
        self.dq = {}
        for q in ['sp', 'pool']:
            keys = []
            for i in range(self.NDS):
                k = 'd_%s%d' % (q, i)
                self.semh[k] = stack.enter_context(nc.semaphore(k))
                keys.append(k)
            self.dq[q] = {'keys': keys, 'n': 0}
        self.known = {e: {} for e in self.eng}
        self.lastw = {}
        self.readers = {}
        self.ninst = 0

    def _deps(self, R, W):
        deps = {}

        def add(k, v):
            if deps.get(k, 0) < v:
                deps[k] = v
        for r in R:
            t = self.lastw.get(r)
            if t is not None:
                add(*t)
        for w in W:
            t = self.lastw.get(w)
            if t is not None:
                add(*t)
            rd = self.readers.get(w)
            if rd:
                for k, v in rd.items():
                    add(k, v)
        return deps

    def _wait(self, e, deps):
        kn = self.known[e]
        for k, v in deps.items():
            if k == e and e == 'pe':
                continue
            if kn.get(k, 0) >= v:
                continue
            self.eng[e].wait_ge(self.semh[k], v)
            kn[k] = v
            self.ninst += 1

    def _commit(self, tok, R, W):
        k, v = tok
        for r in R:
            rd = self.readers.setdefault(r, {})
            if rd.get(k, 0) < v:
                rd[k] = v
        for w in W:
            self.lastw[w] = tok
            self.readers[w] = {}

    def op(self, e, R, W, fn):
        self._wait(e, self._deps(R, W))
        inst = fn(self.eng[e])
        self.cnt[e] += 1
        inst.then_inc(self.semh[e], 1)
        self._commit((e, self.cnt[e]), R, W)
        self.ninst += 1
        return inst

    def dma(self, q, out, in_, R, W):
        dq = self.dq[q]
        j = dq['n']
        s = j % self.NDS
        key = dq['keys'][s]
        deps = self._deps(R, W)
        if j >= self.NDS:
            v = 16 * (j // self.NDS)
            if deps.get(key, 0) < v:
                deps[key] = v
        self._wait(q, deps)
        inst = self.eng[q].dma_start(out=out, in_=in_)
        inst.then_inc(self.semh[key], 16)
        dq['n'] += 1
        self._commit((key, 16 * (j // self.NDS + 1)), R, W)
        self.ninst += 1

    def barrier(self):
        cur = {}
        for q, dq in self.dq.items():
            for si, key in enumerate(dq['keys']):
                n = (dq['n'] - si + self.NDS - 1) // self.NDS if dq['n'] > si else 0
                if n > 0:
                    cur[key] = 16 * n
        for e in ['pe', 'act', 'dve', 'pool']:
            if self.cnt[e] > 0:
                cur[e] = self.cnt[e]
        for e in self.eng:
            self._wait(e, {k: v for k, v in cur.items() if k != e})

    def finish(self):
        deps = {}
        for q, dq in self.dq.items():
            for s, key in enumerate(dq['keys']):
                n = (dq['n'] - s + self.NDS - 1) // self.NDS if dq['n'] > s else 0
                if n > 0:
                    deps[key] = 16 * n
        for e in ['pe', 'act', 'dve', 'pool']:
            if self.cnt[e] > 0:
                deps[e] = self.cnt[e]
        self._wait('sp', deps)


class Ctx:
    pass


def _consts():
    c = {}
    c['identb'] = np.eye(128).astype(ml_dtypes.bfloat16)
    c['identf'] = np.eye(128).astype(np.float32)
    ob = np.zeros((128, 128), np.float32)
    ob[:64, :64] = 1.0
    ob[64:, 64:] = 1.0
    c['onesblk'] = ob
    i = np.arange(64)
    c['triU'] = (i[:, None] <= i[None, :]).astype(np.float32)
    c['biasL'] = np.where(i[None, :] < i[:, None], 0.0, NEG).astype(np.float32)
    c['biasU'] = np.where(i[:, None] <= i[None, :], 0.0, NEG).astype(np.float32)
    t1 = np.arange(1, 513, dtype=np.float32)
    c['invcnt'] = np.stack([1.0 / np.minimum(t1, float(w)) for w in (2, 4, 8, 16)]).astype(np.float32)
    return c


def phase_p1(K, l):
    nc, fw, T = K.nc, K.fw, K.T
    NM = T // 512
    with ExitStack() as st:
        sb = lambda name, shape, dt: st.enter_context(nc.sbuf_tensor(name + "_L%d" % l, shape, dt))
        ps = lambda name, shape, dt: st.enter_context(nc.psum_tensor(name + "_L%d" % l, shape, dt))
        Wb = sb("p1_Wb", [128, 8, DIN], BF16)
        wst = [sb("p1_wst%d" % i, [128, DIN // 2], F32) for i in range(2)]
        gmix = sb("p1_gmix", [128, 8], F32)
        identb = sb("p1_identb", [128, 128], BF16)
        onesblk = sb("p1_onesblk", [128, 128], F32)
        cw = sb("p1_cw", [128, 9, 4], F32)
        qg = sb("p1_qg", [128, 4], F32)
        dtb = sb("p1_dtb", [128, 6], F32)
        nA = sb("p1_nA", [128, 6], F32)
        pscale = sb("p1_pscale", [128, 256], F32)
        poolw = sb("p1_poolw", [64, 4, 64], F32)
        poolwb = sb("p1_poolwb", [64, 4, 64], BF16)
        invc = sb("p1_invc", [64, 4, 512], F32)
        xt = [sb("p1_xt%d" % i, [128, 4, D], F32) for i in range(1)]
        junk = sb("p1_junk", [128, D], BF16)
        ss = sb("p1_ss", [128, 4], F32)
        rstd = sb("p1_rstd", [128, 4], F32)
        xb = sb("p1_xb", [128, 4, D], BF16)
        xT = [sb("p1_xT%d" % i, [128, 8, 512], BF16) for i in range(2)]
        cst = [sb("p1_cst%d" % i, [128, 515], F32) for i in range(9)]
        acc = [sb("p1_acc%d" % i, [128, 512], F32) for i in range(3)]
        sl = [sb("p1_sl%d" % i, [128, 512], F32) for i in range(3)]
        sq = [sb("p1_sq%d" % i, [128, 512], F32) for i in range(3)]
        rn = [sb("p1_rn%d" % i, [128, 512], F32) for i in range(3)]
        of = [sb("p1_of%d" % i, [128, 512], F32) for i in range(2)]
        ob = [sb("p1_ob%d" % i, [128, 512], BF16) for i in range(2)]
        ust = [sb("p1_ust%d" % i, [64, 527], F32) for i in range(4)]
        s2 = sb("p1_s2", [64, 526], F32)
        s4 = sb("p1_s4", [64, 524], F32)
        s8 = sb("p1_s8", [64, 520], F32)
        s16 = sb("p1_s16", [64, 512], F32)
        dfb = [sb("p1_dfb%d" % i, [64, 512], BF16) for i in range(4)]
        ycs = sb("p1_ycs", [128, 4, 256], F32)
        zst = sb("p1_zst", [128, 4, 384], F32)
        bgs = sb("p1_bgs", [128, 4, 12], F32)
        tmpa = sb("p1_tmpa", [128, 6], F32)
        gts = sb("p1_gts", [128, 4, 18], F32)
        v1s = [sb("p1_v1s%d" % i, [128, 4, 130], BF16) for i in range(2)]
        v1w = [sb("p1_v1w%d" % i, [128, 4, 130], BF16) for i in range(2)]
        pT = [ps("p1_pT%d" % i, [128, 512], BF16) for i in range(2)]
        pF = [ps("p1_pF%d" % i, [128, 512], F32) for i in range(3)]
        pS = [ps("p1_pS%d" % i, [128, 512], F32) for i in range(3)]

        fw.dma('sp', identb[:], K.c['identb'], [], ['identb'])
        fw.dma('sp', onesblk[:], K.c['onesblk'], [], ['onesblk'])
        fw.dma('sp', gmix[:], K.w['norm_mix'][l].rearrange("(k p) -> p k", p=128), [], ['gmix'])
        for k in range(4):
            fw.dma('sp', cw[:, :, k], K.w['conv_w'][l, k].rearrange("(c p) -> p c", p=128), [], ['cw'])
        for hh in range(2):
            fw.dma('sp', qg[hh * 64:(hh + 1) * 64, 0:1], K.w['nsa_q_norm'][l].rearrange("(p o) -> p o", o=1), [], ['qg'])
            for j in range(3):
                fw.dma('sp', qg[hh * 64:(hh + 1) * 64, 1 + j:2 + j], K.w['nsa_k_norm'][l, j].rearrange("(p o) -> p o", o=1), [], ['qg'])
        fw.op('dve', ['qg'], ['qg'], lambda e: e.tensor_scalar(out=qg[:, 0:1], in0=qg[:, 0:1], scalar1=0.125, scalar2=None, op0=ALU.mult))
        fw.dma('sp', dtb[:], K.w['dt_bias'][l:l + 1, :].partition_broadcast(128), [], ['dtb'])
        fw.dma('sp', nA[:], K.w['a_log'][l:l + 1, :].partition_broadcast(128), [], ['nA'])
        fw.op('act', ['nA'], ['nA'], lambda e: e.activation(out=nA[:], in_=nA[:], func=AF.Exp))
        fw.op('dve', ['nA'], ['nA'], lambda e: e.tensor_scalar(out=nA[:], in0=nA[:], scalar1=-1.0, scalar2=None, op0=ALU.mult))
        fw.dma('sp', pscale[:], K.w['pool_scale'][l:l + 1, :].partition_broadcast(128), [], ['pscale'])
        fw.dma('sp', poolw[:], K.w['pool_w'][l].rearrange("g c d -> c g d"), [], ['poolw'])
        fw.op('dve', ['poolw'], ['poolwb'], lambda e: e.tensor_copy(out=poolwb[:], in_=poolw[:]))
        fw.dma('sp', invc[:], K.c['invcnt'].partition_broadcast(64), [], ['invc'])
        HW = DIN // 2
        for kc in range(8):
            for hf in range(2):
                w_ = wst[hf]
                fw.dma('sp' if hf else 'pool', w_[:], K.w['w_in'][l, kc * 128:(kc + 1) * 128, hf * HW:(hf + 1) * HW], [], ['wst%d' % hf])
                fw.op('dve' if hf else 'pool', ['wst%d' % hf, 'gmix'], ['Wb'],
                      lambda e: e.tensor_scalar(out=Wb[:, kc, hf * HW:(hf + 1) * HW], in0=w_[:], scalar1=gmix[:, kc:kc + 1], scalar2=None, op0=ALU.mult))
        for i in range(9):
            fw.op('pool', [], ['cst%d' % i], lambda e: e.memset(cst[i][:, 0:3], 0.0))
        for i in range(4):
            fw.op('pool', [], ['ust%d' % i], lambda e: e.memset(ust[i][:, 0:15], 0.0))
        for i in range(2):
            fw.op('pool', [], ['v1s%d' % i], lambda e: e.memset(v1s[i][:], 1.0))
            fw.op('pool', [], ['v1w%d' % i], lambda e: e.memset(v1w[i][:], 1.0))

        xsrc = K.xin[l].rearrange("(m j p) d -> m p j d", p=128, j=4)
        cnt = [0]
        rc = {}
        freel = {}

        def rot(lst, key):
            rc[key] = rc.get(key, -1) + 1
            i = rc[key] % len(lst)
            return lst[i], '%s%d' % (key, i)

        def front(m):
            par = m % 2
            x_, xk = xt[0], 'xt0'
            xT_, xTk = xT[par], 'xT%d' % par
            fw.dma('pool', x_[:], xsrc[m], [], [xk])
            fw.op('dve', [], ['ss'], lambda e: e.memset(ss[:], 0.0))
            for j in range(4):
                fw.op('act', [xk, 'ss'], ['junk', 'ss'], lambda e: e.activation(out=junk[:], in_=x_[:, j, :], func=AF.Square, accum_out=ss[:, j:j + 1]))
            fw.op('dve', ['ss'], ['rstd'], lambda e: e.tensor_scalar(out=rstd[:], in0=ss[:], scalar1=1.0 / D, scalar2=EPS, op0=ALU.mult, op1=ALU.add))
            fw.op('act', ['rstd'], ['rstd'], lambda e: e.activation(out=rstd[:], in_=rstd[:], func=AF.Sqrt))
            fw.op('dve', ['rstd'], ['rstd'], lambda e: e.reciprocal(out=rstd[:], in_=rstd[:]))
            for j in range(4):
                fw.op('dve' if j % 2 else 'pool', [xk, 'rstd'], ['xb%d' % j],
                      lambda e: e.tensor_scalar(out=xb[:, j, :], in0=x_[:, j, :], scalar1=rstd[:, j:j + 1], scalar2=None, op0=ALU.mult))
            for kc in range(8):
                p_, pk = pT[kc % 2], 'pT%d' % (kc % 2)
                for j in range(4):
                    fw.op('pe', ['xb%d' % j, 'identb'], [pk], lambda e: e.transpose(out=p_[:, j * 128:(j + 1) * 128], in_=xb[:, j, kc * 128:(kc + 1) * 128], identity=identb[:]))
                if kc % 2:
                    fw.op('act', [pk], [xTk], lambda e: e.copy(out=xT_[:, kc, :], in_=p_[:]))
                else:
                    fw.op('dve', [pk], [xTk], lambda e: e.tensor_copy(out=xT_[:, kc, :], in_=p_[:]))


        def chunks(m):
            par = m % 2
            x_, xk = xt[0], 'xt0'
            xT_, xTk = xT[par], 'xT%d' % par

            def acq(lst, key):
                fl = freel.setdefault(key, list(range(len(lst))))
                i = fl.pop(0)
                return lst[i], '%s%d' % (key, i)

            def rel(key, k_):
                freel[key].append(int(k_[len(key):]))

            def fm(col0, ncols):
                p_, pk = acq(pF, 'pF')
                for kc in range(8):
                    fw.op('pe', [xTk, 'Wb'], [pk], lambda e: e.matmul(p_[0:ncols, :], lhsT=Wb[:, kc, col0:col0 + ncols], rhs=xT_[:, kc, :], start=(kc == 0), stop=(kc == 7)))
                return p_, pk

            def rnorm(src_ap, srck, n, mean):
                q_, qk_ = acq(sq, 'sq')
                fw.op('pool' if srck.startswith('sl') else 'act', [srck], [qk_],
                      (lambda e: e.tensor_tensor(out=q_[:], in0=src_ap, in1=src_ap, op=ALU.mult)) if srck.startswith('sl')
                      else (lambda e: e.activation(out=q_[:], in_=src_ap, func=AF.Square)))
                yield
                s_, sk_ = acq(pS, 'pS')
                fw.op('pe', [qk_, 'onesblk'], [sk_], lambda e: e.matmul(s_[:], lhsT=onesblk[:], rhs=q_[:], start=True, stop=True))
                rel('sq', qk_)
                yield
                r_, rk_ = rot(rn, 'rn')
                fw.op('dve', [sk_], [rk_], lambda e: e.tensor_scalar(out=r_[:], in0=s_[:], scalar1=(1.0 / 64 if mean else 1.0), scalar2=EPS, op0=ALU.mult, op1=ALU.add))
                rel('pS', sk_)
                fw.op('act', [rk_], [rk_], lambda e: e.activation(out=r_[:], in_=r_[:], func=AF.Sqrt))
                fw.op('dve', [rk_], [rk_], lambda e: e.reciprocal(out=r_[:], in_=r_[:]))
                return r_, rk_

            def gdn_chunk(ch):
                p_, pk = fm(C_QKV + ch * 128, 128)
                yield
                c_, ck = cst[ch], 'cst%d' % ch
                fw.op('act', [pk], [ck], lambda e: e.copy(out=c_[:, 3:515], in_=p_[:]))
                a_, ak = rot(acc, 'acc')
                fw.op('dve', [ck, 'cw'], [ak], lambda e: e.tensor_scalar(out=a_[:], in0=c_[:, 0:512], scalar1=cw[:, ch, 0:1], scalar2=None, op0=ALU.mult))
                for k in range(1, 4):
                    fw.op('dve', [ck, 'cw', ak], [ak],
                          lambda e: e.scalar_tensor_tensor(out=a_[:], in0=c_[:, k:k + 512], scalar=cw[:, ch, k:k + 1], in1=a_[:], op0=ALU.mult, op1=ALU.add))
                fw.op('pool', [ck], [ck], lambda e: e.tensor_copy(out=c_[:, 0:3], in_=c_[:, 512:515]))
                s_, sk = acq(sl, 'sl')
                fw.op('act', [ak], [sk], lambda e: e.activation(out=s_[:], in_=a_[:], func=AF.Silu))
                rel('pF', pk)
                if ch < 6:
                    r_, rk = yield from rnorm(s_[:], sk, 128, False)
                    o_, ok = rot(of, 'of')
                    fw.op('dve', [sk, rk], [ok], lambda e: e.scalar_tensor_tensor(out=o_[:], in0=s_[:], scalar=(0.125 if ch < 3 else 1.0), in1=r_[:], op0=ALU.mult, op1=ALU.mult))
                    fw.dma('sp', K.gqk[ch * 128:(ch + 1) * 128, m * 512:(m + 1) * 512], o_[:], [ok], ['gqk'])
                else:
                    fw.dma('sp', K.gqk[ch * 128:(ch + 1) * 128, m * 512:(m + 1) * 512], s_[:], [sk], ['gqk'])
                rel('sl', sk)

            def nsa_chunk(col0, ch, gi_, dst):
                p_, pk = fm(col0 + ch * 128, 128)
                yield
                r_, rk = yield from rnorm(p_[:], pk, 128, True)
                o_, ok = rot(ob, 'ob')
                fw.op('dve', [pk, rk, 'qg'], [ok], lambda e: e.scalar_tensor_tensor(out=o_[:], in0=p_[:], scalar=qg[:, gi_:gi_ + 1], in1=r_[:], op0=ALU.mult, op1=ALU.mult))
                fw.dma('sp', dst[ch * 128:(ch + 1) * 128, m * 512:(m + 1) * 512], o_[:], [ok], ['nsa_dst'])
                rel('pF', pk)

            def raw_chunk(col0, dst):
                p_, pk = fm(col0, 128)
                yield
                o_, ok = rot(ob, 'ob')
                fw.op('act', [pk], [ok], lambda e: e.copy(out=o_[:], in_=p_[:]))
                fw.dma('sp', dst[:, m * 512:(m + 1) * 512], o_[:], [ok], ['nsa_dst'])
                rel('pF', pk)

            def pool_chunk(gi, wlen):
                p_, pk = fm(C_U + gi * 64, 64)
                yield
                u_, uk = ust[gi], 'ust%d' % gi
                fw.op('act', [pk], [uk], lambda e: e.copy(out=u_[:, 15:527], in_=p_[0:64, :]))
                rel('pF', pk)
                fw.op('dve', [uk], ['s2'], lambda e: e.tensor_tensor(out=s2[:], in0=u_[:, 1:527], in1=u_[:, 0:526], op=ALU.add))
                sw = s2
                if wlen >= 4:
                    fw.op('dve', ['s2'], ['s4'], lambda e: e.tensor_tensor(out=s4[:], in0=s2[:, 2:526], in1=s2[:, 0:524], op=ALU.add))
                    sw = s4
                if wlen >= 8:
                    fw.op('dve', ['s4'], ['s8'], lambda e: e.tensor_tensor(out=s8[:], in0=s4[:, 4:524], in1=s4[:, 0:520], op=ALU.add))
                    sw = s8
                if wlen >= 16:
                    fw.op('dve', ['s8'], ['s16'], lambda e: e.tensor_tensor(out=s16[:], in0=s8[:, 8:520], in1=s8[:, 0:512], op=ALU.add))
                    sw = s16
                nsw = {2: 526, 4: 524, 8: 520, 16: 512}[wlen]
                swk = 's%d' % wlen
                d_, dk = dfb[gi], 'dfb%d' % gi
                if m == 0:
                    fw.op('dve', [swk, 'invc'], [swk], lambda e: e.tensor_tensor(out=sw[:, nsw - 512:nsw], in0=sw[:, nsw - 512:nsw], in1=invc[:, gi, :], op=ALU.mult))
                    fw.op('dve', [swk, uk], [dk], lambda e: e.tensor_tensor(out=d_[:], in0=sw[:, nsw - 512:nsw], in1=u_[:, 15:527], op=ALU.subtract))
                else:
                    fw.op('dve', [swk, uk], [dk], lambda e: e.scalar_tensor_tensor(out=d_[:], in0=sw[:, nsw - 512:nsw], scalar=1.0 / wlen, in1=u_[:, 15:527], op0=ALU.mult, op1=ALU.subtract))
                fw.op('pool', [uk], [uk], lambda e: e.tensor_copy(out=u_[:, 0:15], in_=u_[:, 512:527]))

            def pool_out(j):
                p_, pk = acq(pS, 'pS')
                for gi in range(4):
                    fw.op('pe', ['dfb%d' % gi, 'poolwb'], [pk], lambda e: e.matmul(p_[:, gi * 64:(gi + 1) * 64], lhsT=dfb[gi][:, j * 128:(j + 1) * 128], rhs=poolwb[:, gi, :], start=True, stop=True))
                yield
                fw.op('dve', [pk, 'pscale'], ['ycs'], lambda e: e.tensor_tensor(out=ycs[:, j, :], in0=p_[:, 0:256], in1=pscale[:], op=ALU.mult))
                rel('pS', pk)
                if j == 3:
                    fw.dma('sp', K.y.rearrange("(m j p) d -> m p j d", p=128, j=4)[m][:, :, 768:1024], ycs[:], ['ycs'], ['y_c'])

            v_, vk = v1s[par], 'v1s%d' % par
            vw_, vwk = v1w[par], 'v1w%d' % par

            def tok_a(j):
                p_, pk = acq(pF, 'pF')
                for kc in range(8):
                    fw.op('pe', [xTk, 'Wb'], [pk], lambda e: e.matmul(p_[:, 0:396], lhsT=xT_[:, kc, j * 128:(j + 1) * 128], rhs=Wb[:, kc, C_Z:C_Z + 396], start=(kc == 0), stop=(kc == 7)))
                yield
                fw.op('act', [pk], ['zst'], lambda e: e.activation(out=zst[:, j, :], in_=p_[:, 0:384], func=AF.Silu))
                fw.op('act', [pk], ['bgs'], lambda e: e.activation(out=bgs[:, j, 0:6], in_=p_[:, 384:390], func=AF.Sigmoid))
                fw.op('dve', [pk, 'dtb', 'zst', 'bgs'], ['tmpa'], lambda e: e.tensor_tensor(out=tmpa[:], in0=p_[:, 390:396], in1=dtb[:], op=ALU.add))
                fw.op('act', ['tmpa'], ['tmpa'], lambda e: e.activation(out=tmpa[:], in_=tmpa[:], func=AF.Exp))
                fw.op('act', ['tmpa'], ['tmpa'], lambda e: e.activation(out=tmpa[:], in_=tmpa[:], func=AF.Ln, bias=1.0))
                fw.op('dve', ['tmpa', 'nA'], ['bgs'], lambda e: e.tensor_tensor(out=bgs[:, j, 6:12], in0=tmpa[:], in1=nA[:], op=ALU.mult))
                rel('pF', pk)

            def tok_b(j):
                p_, pk = acq(pF, 'pF')
                for (c0, o0) in ((C_VS, 0), (C_VW, 128), (C_GATE, 256)):
                    nn = 18 if c0 == C_GATE else 128
                    for kc in range(8):
                        fw.op('pe', [xTk, 'Wb'], [pk], lambda e: e.matmul(p_[:, o0:o0 + nn], lhsT=xT_[:, kc, j * 128:(j + 1) * 128], rhs=Wb[:, kc, c0:c0 + nn], start=(kc == 0), stop=(kc == 7)))
                yield
                fw.op('act', [pk], ['gts'], lambda e: e.activation(out=gts[:, j, :], in_=p_[:, 256:274], func=AF.Sigmoid))
                fw.op('dve', [pk, 'gts'], [vk], lambda e: e.tensor_copy(out=v_[:, j, :].rearrange("p (h c) -> p h c", h=2)[:, :, 0:64],
                                                                     in_=p_[:, 0:128].rearrange("p (h c) -> p h c", h=2)))
                fw.op('dve', [pk, 'gts'], [vwk], lambda e: e.tensor_copy(out=vw_[:, j, :].rearrange("p (h c) -> p h c", h=2)[:, :, 0:64],
                                                                     in_=p_[:, 128:256].rearrange("p (h c) -> p h c", h=2)))
                rel('pF', pk)

            tasks = [gdn_chunk(ch) for ch in range(9)]
            tasks += [nsa_chunk(C_QB, ch, 0, K.qn) for ch in range(3)] + [nsa_chunk(C_KS, 0, 2, K.ksT), nsa_chunk(C_KW, 0, 3, K.kwT)]
            tasks += [raw_chunk(C_KC, K.kcT), raw_chunk(C_VC, K.vcT)]
            tasks += [pool_chunk(gi, wlen) for gi, wlen in enumerate((2, 4, 8, 16))]
            tasks += [None]
            tasks += [pool_out(j) for j in range(4)]
            for j in range(4):
                tasks += [tok_a(j), tok_b(j)]
            active = []
            ti = 0
            while ti < len(tasks) or active:
                while ti < len(tasks) and len(active) < 3:
                    if tasks[ti] is None:
                        if active:
                            break
                        ti += 1
                        continue
                    active.append(tasks[ti])
                    ti += 1
                for g_ in list(active):
                    try:
                        next(g_)
                    except StopIteration:
                        active.remove(g_)
            fw.dma('sp', K.zs.rearrange("(m j p) d -> m p j d", p=128, j=4)[m], zst[:], ['zst'], ['zs'])
            fw.dma('sp', K.bg.rearrange("(m j p) d -> m p j d", p=128, j=4)[m], bgs[:], ['bgs'], ['bg'])
            fw.dma('sp', K.gt.rearrange("(m j p) d -> m p j d", p=128, j=4)[m], gts[:], ['gts'], ['gt'])
            fw.dma('sp', K.vs1.rearrange("(m j p) d -> m p j d", p=128, j=4)[m], v_[:], [vk], ['vs1'])
            fw.dma('sp', K.vw1.rearrange("(m j p) d -> m p j d", p=128, j=4)[m], vw_[:], [vwk], ['vw1'])
        front(0)
        for m in range(NM):
            if m + 1 < NM:
                front(m + 1)
            chunks(m)
        fw.barrier()


WNAMES = ["norm_mix", "w_in", "conv_w", "a_log", "dt_bias", "gdn_norm", "nsa_q_norm", "nsa_k_norm", "cmp_pos",
          "cmp_w1", "cmp_w2", "pool_w", "pool_scale", "w_out", "norm_ffn", "w_ffn1", "w_ffn2"]
WSHAPES = {"norm_mix": [2, 1024], "w_in": [2, 1024, 2974], "conv_w": [2, 4, 1152], "a_log": [2, 6], "dt_bias": [2, 6],
           "gdn_norm": [2, 64], "nsa_q_norm": [2, 64], "nsa_k_norm": [2, 3, 64], "cmp_pos": [2, 2, 32, 64],
           "cmp_w1": [2, 2, 2048, 256], "cmp_w2": [2, 2, 256, 64], "pool_w": [2, 4, 64, 64], "pool_scale": [2, 256],
           "w_out": [2, 1024, 1024], "norm_ffn": [2, 1024], "w_ffn1": [2, 1024, 4096], "w_ffn2": [2, 4096, 1024]}


def build(T, L=2, phases=("p1", "gdn", "cmp", "nsa", "p3"), dbg=False):
    nc = bass.Bass("TRN2", target_bir_lowering=False)
    K = Ctx()
    K.nc, K.T, K.L = nc, T, L
    K.cut = int(os.environ.get('GDN_CUT', '99'))
    K.f32r = os.environ.get('GDN_F32R', '0') == '1'
    dk = "ExternalOutput" if dbg else "Internal"
    K.w = {n: nc.dram_tensor(n, WSHAPES[n], F32, kind="ExternalInput").ap() for n in WNAMES}
    cs = _consts()
    K.c = {n: nc.dram_tensor("c_" + n, list(v.shape), BF16 if v.dtype != np.float32 else F32, kind="ExternalInput").ap() for n, v in cs.items()}
    x = nc.dram_tensor("x", [T, D], F32, kind="ExternalInput").ap()
    out = nc.dram_tensor("out", [T, D], F32, kind="ExternalOutput").ap()
    xs1 = nc.dram_tensor("xs1", [T, D], F32, kind=dk).ap()
    K.xin = [x, xs1]
    K.xout = [xs1, out] if L == 2 else [out]
    K.gqk = nc.dram_tensor("gqk", [1152, T], F32, kind=dk).ap()
    K.zs = nc.dram_tensor("zs", [T, 384], F32, kind=dk).ap()
    K.bg = nc.dram_tensor("bg", [T, 12], F32, kind=dk).ap()
    K.gt = nc.dram_tensor("gt", [T, 18], F32, kind=dk).ap()
    K.qn = nc.dram_tensor("qn", [384, T], BF16, kind=dk).ap()
    K.ksT = nc.dram_tensor("ksT", [128, T], BF16, kind=dk).ap()
    K.kwT = nc.dram_tensor("kwT", [128, T], BF16, kind=dk).ap()
    K.kcT = nc.dram_tensor("kcT", [128, T], BF16, kind=dk).ap()
    K.vcT = nc.dram_tensor("vcT", [128, T], BF16, kind=dk).ap()
    K.vs1 = nc.dram_tensor("vs1", [T, 130], BF16, kind=dk).ap()
    K.vw1 = nc.dram_tensor("vw1", [T, 130], BF16, kind=dk).ap()
    K.y = nc.dram_tensor("y", [T, D], F32, kind=dk).ap()
    Ns_, NcP_ = T // 64, T // 16
    K.KcT = nc.dram_tensor("KcT", [2, 64, NcP_], BF16, kind=dk).ap()
    K.Vc1 = nc.dram_tensor("Vc1", [2, NcP_, 64 + Ns_ + 1], BF16, kind=dk).ap()
    csn = _nsa_consts(T)
    K.cn = {n: nc.dram_tensor("cn_" + n, list(v.shape), BF16, kind="ExternalInput").ap() for n, v in csn.items()}
    K.consts_n = csn
    with ExitStack() as st:
        st.enter_context(nc.allow_non_contiguous_dma(reason="small parameter loads"))
        K.fw = FW(nc, st)
        for l in range(L):
            if "p1" in phases:
                phase_p1(K, l)
            if "gdn" in phases:
                phase_gdn(K, l)
            if "cmp" in phases:
                phase_cmp(K, l)
            if "nsa" in phases:
                phase_nsa(K, l)
            if "p3" in phases:
                phase_p3(K, l)
        K.fw.finish()
    K.consts = cs
    return nc, K


def phase_gdn(K, l):
    nc, fw, T = K.nc, K.fw, K.T
    NCH = T // 64
    F32R = mybir.dt.float32r
    R_ = (lambda ap: ap.bitcast(F32R)) if K.f32r else (lambda ap: ap)
    with ExitStack() as st:
        sb = lambda name, shape, dt=F32: st.enter_context(nc.sbuf_tensor(name + "_L%d" % l, shape, dt))
        identf = sb("g_identf", [64, 64])
        ones64 = sb("g_ones", [64, 64])
        triU = sb("g_triU", [64, 64])
        biasL = sb("g_biasL", [64, 64])
        biasU = sb("g_biasU", [64, 64])
        gnorm = sb("g_gnorm", [64, 64])
        Fq = [sb("g_F%d" % i, [64, 18, 512]) for i in range(2)]
        Zs = [sb("g_Z%d" % i, [64, 8, 384]) for i in range(1)]
        BG = [sb("g_BG%d" % i, [64, 8, 12]) for i in range(2)]
        Yst = [sb("g_Y%d" % i, [64, 8, 384]) for i in range(1)]
        S = [sb("g_S%d" % i, [64, 6, 64]) for i in range(2)]
        Sb = [sb("g_Sb%d" % i, [64, 6, 64], BF16) for i in range(2)]
        names = ["KVt", "rhsG", "gcol", "D1", "D2", "E1", "E2", "egrow", "egcol", "kdsc", "nbeta", "bege", "N", "qkT", "vb", "kbg", "kd", "qdT",
                 "Mp0", "Mp1", "Np0", "Np1", "Tt0", "Tt1", "U", "WT", "vnew", "tmpS", "O", "sqo", "ssum", "Yt", "Gs", "Nb", "Fb"]
        BFN = ("Mp0", "Mp1", "Np0", "Np1", "Tt0", "Tt1", "vb", "kbg", "kd", "qdT", "qkT", "WT", "vnew", "Nb", "Fb")
        W2 = {}
        SCAN_IN = ("U", "WT", "kd", "qdT", "qkT", "egrow")
        for nm in names:
            shape = [64, 12, 64] if nm in ("KVt", "Fb") else [64, 390] if nm == "Gs" else ([64, 6] if nm in ("gcol", "egcol", "kdsc", "nbeta", "bege", "ssum") else [64, 6, 64])
            W2[nm] = [sb("g_%s%d" % (nm, i), shape, BF16 if nm in BFN else F32) for i in range(4 if nm in SCAN_IN else 2)]
        banks = [st.enter_context(nc.psum_tensor("g_ps%d_L%d" % (i, l), [64, 512], F32)) for i in range(8)]
        bc = [0]

        def bank():
            bc[0] += 1
            i = bc[0] % 8
            return banks[i], 'gps%d' % i

        fw.dma('sp', identf[:], K.c['identf'][0:64, 0:64], [], ['identf'])
        fw.dma('sp', triU[:], K.c['triU'], [], ['triU'])
        fw.dma('sp', biasL[:], K.c['biasL'], [], ['biasL'])
        fw.dma('sp', biasU[:], K.c['biasU'], [], ['biasU'])
        fw.dma('sp', gnorm[:], K.w['gdn_norm'][l:l + 1, :].partition_broadcast(64), [], ['gnorm'])
        fw.op('pool', [], ['ones64'], lambda e: e.memset(ones64[:], 1.0))
        fw.op('pool', [], ['S0'], lambda e: e.memset(S[0][:], 0.0))
        fw.op('pool', [], ['Sb0'], lambda e: e.memset(Sb[0][:], 0.0))

        gq = K.gqk.rearrange("(s p) t -> p s t", p=64)
        zsr = K.zs.rearrange("(n c) f -> c n f", c=64)
        bgr = K.bg.rearrange("(n c) f -> c n f", c=64)
        yr = K.y.rearrange("(n c) f -> c n f", c=64)

        def b3(ap2, n=64):
            return ap2.unsqueeze(2).to_broadcast([64, 6, n])

        def m3(ap2):
            return ap2.unsqueeze(1).to_broadcast([64, 6, 64])

        state = {}

        def load_group(gi):
            par = gi % 2
            fw.dma('sp', Fq[par][:], gq[:, :, gi * 512:(gi + 1) * 512], [], ['F%d' % par])
            fw.dma('sp', BG[par][:], bgr[:, gi * 8:(gi + 1) * 8, :], [], ['BG%d' % par])

        def prep(n):
            gi, j = n // 8, n % 8
            gp = gi % 2
            p = n % 2
            F_, Fk = Fq[gp], 'F%d' % gp
            BG_, BGk = BG[gp], 'BG%d' % gp
            ix = lambda nm: (n % 4) if nm in SCAN_IN else p
            t = {nm: W2[nm][ix(nm)] for nm in names}
            k_ = lambda nm: '%s%d' % (nm, ix(nm))
            cs = slice(j * 64, (j + 1) * 64)
            qT = F_[:, 0:6, cs]
            kT = F_[:, 6:12, cs]
            beta = BG_[:, j, 0:6]
            g = BG_[:, j, 6:12]
            fw.op('pool', [Fk], [k_('Fb')], lambda e: e.tensor_copy(out=t['Fb'][:], in_=F_[:, 0:12, cs]))
            pk1, pk1k = bank()
            pv1, pv1k = bank()
            for h in range(6):
                fw.op('pe', [Fk, 'identf'], [pk1k], lambda e: e.transpose(out=pk1[:, h * 64:(h + 1) * 64], in_=F_[:, 6 + h, cs], identity=identf[:]))
            for h in range(6):
                fw.op('pe', [Fk, 'identf'], [pv1k], lambda e: e.transpose(out=pv1[:, h * 64:(h + 1) * 64], in_=F_[:, 12 + h, cs], identity=identf[:]))
            fw.op('act', [pk1k], [k_('KVt')], lambda e: e.copy(out=t['KVt'][:, 0:6, :], in_=pk1[:, 0:384].rearrange("p (h c) -> p h c", h=6)))
            fw.op('act', [pv1k], [k_('KVt')], lambda e: e.copy(out=t['KVt'][:, 6:12, :], in_=pv1[:, 0:384].rearrange("p (h c) -> p h c", h=6)))
            yield
            pKK, pKKk = bank()
            pQK, pQKk = bank()
            for h in range(6):
                fw.op('pe', [k_('Fb')], [pKKk], lambda e: e.matmul(pKK[:, h * 64:(h + 1) * 64], lhsT=t['Fb'][:, 6 + h, :], rhs=t['Fb'][:, 6 + h, :], start=True, stop=True))
            for h in range(6):
                fw.op('pe', [k_('Fb')], [pQKk], lambda e: e.matmul(pQK[:, h * 64:(h + 1) * 64], lhsT=t['Fb'][:, 6 + h, :], rhs=t['Fb'][:, h, :], start=True, stop=True))
            yield
            fw.op('dve', [BGk, 'triU'], [k_('rhsG')], lambda e: e.tensor_tensor(out=t['rhsG'][:], in0=b3(g), in1=m3(triU[:]), op=ALU.mult))
            pG, pGk = bank()
            fw.op('pe', [k_('rhsG'), 'ones64'], [pGk], lambda e: e.matmul(pG[:, 0:384], lhsT=R_(ones64[:]), rhs=R_(t['rhsG'][:].rearrange("p h c -> p (h c)")), start=True, stop=True))
            fw.op('pe', [BGk, 'triU'], [pGk], lambda e: e.matmul(pG[:, 384:390], lhsT=R_(triU[:]), rhs=R_(g), start=True, stop=True))
            fw.op('act', [pGk], [k_('Gs')], lambda e: e.copy(out=t['Gs'][:], in_=pG[:, 0:390]))
            pGk = k_('Gs')
            pG3 = t['Gs'][:, 0:384].rearrange("p (h c) -> p h c", h=6)
            fw.op('pool', [pGk], [k_('gcol')], lambda e: e.tensor_copy(out=t['gcol'][:], in_=t['Gs'][:, 384:390]))
            fw.op('dve', [pGk, k_('gcol')], [k_('D1')], lambda e: e.tensor_tensor(out=t['D1'][:], in0=b3(t['gcol'][:]), in1=pG3, op=ALU.subtract))
            fw.op('pool', [k_('D1'), 'biasU'], [k_('D2')], lambda e: e.tensor_tensor(out=t['D2'][:], in0=m3(biasU[:]), in1=t['D1'][:], op=ALU.subtract))
            fw.op('dve', [k_('D1'), 'biasL'], [k_('D1')], lambda e: e.tensor_tensor(out=t['D1'][:], in0=t['D1'][:], in1=m3(biasL[:]), op=ALU.add))
            fw.op('act', [k_('D1')], [k_('E1')], lambda e: e.activation(out=t['E1'][:], in_=t['D1'][:], func=AF.Exp))
            fw.op('act', [k_('D2')], [k_('E2')], lambda e: e.activation(out=t['E2'][:], in_=t['D2'][:], func=AF.Exp))
            fw.op('act', [pGk], [k_('egrow')], lambda e: e.activation(out=t['egrow'][:], in_=pG3, func=AF.Exp))
            fw.op('act', [k_('gcol')], [k_('egcol')], lambda e: e.activation(out=t['egcol'][:], in_=t['gcol'][:], func=AF.Exp))
            fw.op('dve', [pGk, k_('gcol')], [k_('kdsc')], lambda e: e.tensor_tensor(out=t['kdsc'][:], in0=pG3[:, :, 63], in1=t['gcol'][:], op=ALU.subtract))
            fw.op('act', [k_('kdsc')], [k_('kdsc')], lambda e: e.activation(out=t['kdsc'][:], in_=t['kdsc'][:], func=AF.Exp))
            fw.op('pool', [BGk], [k_('nbeta')], lambda e: e.tensor_scalar(out=t['nbeta'][:], in0=beta, scalar1=-1.0, scalar2=None, op0=ALU.mult))
            fw.op('pool', [BGk, k_('egcol')], [k_('bege')], lambda e: e.tensor_tensor(out=t['bege'][:], in0=beta, in1=t['egcol'][:], op=ALU.mult))
            yield
            fw.op('dve', [pKKk, k_('E1')], [k_('N')], lambda e: e.tensor_tensor(out=t['N'][:], in0=pKK[:, 0:384].rearrange("p (h c) -> p h c", h=6), in1=t['E1'][:], op=ALU.mult))
            fw.op('dve', [k_('N'), k_('nbeta')], [k_('N')], lambda e: e.tensor_tensor(out=t['N'][:], in0=t['N'][:], in1=b3(t['nbeta'][:]), op=ALU.mult))
            fw.op('pool', [k_('N')], [k_('Nb')], lambda e: e.tensor_copy(out=t['Nb'][:], in_=t['N'][:]))
            fw.op('dve', [pQKk, k_('E2')], [k_('qkT')], lambda e: e.tensor_tensor(out=t['qkT'][:], in0=pQK[:, 0:384].rearrange("p (h c) -> p h c", h=6), in1=t['E2'][:], op=ALU.mult))
            fw.op('pool', [k_('KVt'), BGk], [k_('vb')], lambda e: e.tensor_tensor(out=t['vb'][:], in0=t['KVt'][:, 6:12, :], in1=b3(beta), op=ALU.mult))
            fw.op('pool', [k_('KVt'), k_('bege')], [k_('kbg')], lambda e: e.tensor_tensor(out=t['kbg'][:], in0=t['KVt'][:, 0:6, :], in1=b3(t['bege'][:]), op=ALU.mult))
            fw.op('pool', [k_('KVt'), k_('kdsc')], [k_('kd')], lambda e: e.tensor_tensor(out=t['kd'][:], in0=t['KVt'][:, 0:6, :], in1=b3(t['kdsc'][:]), op=ALU.mult))
            fw.op('pool', [Fk, k_('egrow')], [k_('qdT')], lambda e: e.tensor_tensor(out=t['qdT'][:], in0=qT, in1=t['egrow'][:], op=ALU.mult))
            yield
            pM, pMk = bank()
            for h in range(6):
                fw.op('pe', [k_('N'), 'identf'], [pMk], lambda e: e.transpose(out=pM[:, h * 64:(h + 1) * 64], in_=t['N'][:, h, :], identity=identf[:]))
            pM3 = pM[:, 0:384].rearrange("p (h c) -> p h c", h=6)
            fw.op('act', [pMk], [k_('Mp0')], lambda e: e.copy(out=t['Mp0'][:], in_=pM3))
            fw.op('pool', [k_('Mp0'), 'identf'], [k_('Tt0')], lambda e: e.tensor_tensor(out=t['Tt0'][:], in0=t['Mp0'][:], in1=m3(identf[:]), op=ALU.add))
            yield
            Np, Npk = t['Nb'], k_('Nb')
            Mp, Mpk = t['Mp0'], k_('Mp0')
            Tt, Ttk = t['Tt0'], k_('Tt0')
            for s_ in range(5):
                Nn, Nnk = t['Np%d' % (s_ % 2)], k_('Np%d' % (s_ % 2))
                Mn, Mnk = t['Mp%d' % ((s_ + 1) % 2)], k_('Mp%d' % ((s_ + 1) % 2))
                Tn, Tnk = t['Tt%d' % ((s_ + 1) % 2)], k_('Tt%d' % ((s_ + 1) % 2))
                pN2, pN2k = bank()
                for h in range(6):
                    fw.op('pe', [Mpk, Npk], [pN2k], lambda e: e.matmul(pN2[:, h * 64:(h + 1) * 64], lhsT=R_(Mp[:, h, :]), rhs=R_(Np[:, h, :]), start=True, stop=True))
                if s_ < 4:
                    pM2, pM2k = bank()
                    for h in range(6):
                        fw.op('pe', [Mpk, Npk], [pM2k], lambda e: e.matmul(pM2[:, h * 64:(h + 1) * 64], lhsT=R_(Np[:, h, :]), rhs=R_(Mp[:, h, :]), start=True, stop=True))
                yield
                fw.op('act', [pN2k], [Nnk], lambda e: e.copy(out=Nn[:], in_=pN2[:, 0:384].rearrange("p (h c) -> p h c", h=6)))
                if s_ < 4:
                    fw.op('dve', [pM2k], [Mnk], lambda e: e.tensor_copy(out=Mn[:], in_=pM2[:, 0:384].rearrange("p (h c) -> p h c", h=6)))
                pT_, pTk = bank()
                for h in range(6):
                    fw.op('pe', [Nnk, Ttk], [pTk], lambda e: e.matmul(pT_[:, h * 64:(h + 1) * 64], lhsT=R_(Nn[:, h, :]), rhs=R_(Tt[:, h, :]), start=True, stop=True))
                yield
                fw.op('dve', [pTk, Ttk], [Tnk], lambda e: e.tensor_tensor(out=Tn[:], in0=pT_[:, 0:384].rearrange("p (h c) -> p h c", h=6), in1=Tt[:], op=ALU.add))
                Np, Npk, Mp, Mpk, Tt, Ttk = Nn, Nnk, Mn, Mnk, Tn, Tnk
            yield
            pU, pUk = bank()
            pW, pWk = bank()
            for h in range(6):
                fw.op('pe', [Ttk, k_('vb')], [pUk], lambda e: e.matmul(pU[:, h * 64:(h + 1) * 64], lhsT=R_(Tt[:, h, :]), rhs=R_(t['vb'][:, h, :]), start=True, stop=True))
            for h in range(6):
                fw.op('pe', [Ttk, k_('kbg')], [pWk], lambda e: e.matmul(pW[:, h * 64:(h + 1) * 64], lhsT=R_(t['kbg'][:, h, :]), rhs=R_(Tt[:, h, :]), start=True, stop=True))
            yield
            fw.op('act', [pUk], [k_('U')], lambda e: e.copy(out=t['U'][:], in_=pU[:, 0:384].rearrange("p (h c) -> p h c", h=6)))
            fw.op('dve', [pWk], [k_('WT')], lambda e: e.tensor_copy(out=t['WT'][:], in_=pW[:, 0:384].rearrange("p (h c) -> p h c", h=6)))

        def scan(n):
            gi, j = n // 8, n % 8
            gp = gi % 2
            p = n % 2
            ix = lambda nm: (n % 4) if nm in SCAN_IN else p
            t = {nm: W2[nm][ix(nm)] for nm in names}
            k_ = lambda nm: '%s%d' % (nm, ix(nm))
            if j == 0:
                fw.dma('pool', Zs[0][:], zsr[:, gi * 8:(gi + 1) * 8, :], [], ['Z0'])
            So, Sok = S[n % 2], 'S%d' % (n % 2)
            Sn, Snk = S[(n + 1) % 2], 'S%d' % ((n + 1) % 2)
            Sob, Sobk = Sb[n % 2], 'Sb%d' % (n % 2)
            Snb, Snbk = Sb[(n + 1) % 2], 'Sb%d' % ((n + 1) % 2)
            pWS, pWSk = bank()
            for h in range(6):
                fw.op('pe', [k_('WT'), Sobk], [pWSk], lambda e: e.matmul(pWS[:, h * 64:(h + 1) * 64], lhsT=t['WT'][:, h, :], rhs=Sob[:, h, :], start=True, stop=True))
            yield
            fw.op('dve', [pWSk, k_('U')], [k_('vnew')], lambda e: e.tensor_tensor(out=t['vnew'][:], in0=t['U'][:], in1=pWS[:, 0:384].rearrange("p (h c) -> p h c", h=6), op=ALU.subtract))
            pdS, pdSk = bank()
            for h in range(6):
                fw.op('pe', [k_('kd'), k_('vnew')], [pdSk], lambda e: e.matmul(pdS[:, h * 64:(h + 1) * 64], lhsT=R_(t['kd'][:, h, :]), rhs=R_(t['vnew'][:, h, :]), start=True, stop=True))
            pO, pOk = bank()
            for h in range(6):
                fw.op('pe', [k_('qdT'), Sobk], [pOk], lambda e: e.matmul(pO[:, h * 64:(h + 1) * 64], lhsT=t['qdT'][:, h, :], rhs=Sob[:, h, :], start=True, stop=False))
                fw.op('pe', [k_('qkT'), k_('vnew')], [pOk], lambda e: e.matmul(pO[:, h * 64:(h + 1) * 64], lhsT=R_(t['qkT'][:, h, :]), rhs=R_(t['vnew'][:, h, :]), start=False, stop=True))
            yield
            fw.op('pool', [Sok, k_('egrow')], [k_('tmpS')], lambda e: e.tensor_tensor(out=t['tmpS'][:], in0=So[:], in1=t['egrow'][:, :, 63:64].to_broadcast([64, 6, 64]), op=ALU.mult))
            fw.op('dve', [k_('tmpS'), pdSk], [Snk], lambda e: e.tensor_tensor(out=Sn[:], in0=t['tmpS'][:], in1=pdS[:, 0:384].rearrange("p (h c) -> p h c", h=6), op=ALU.add))
            fw.op('pool', [Snk], [Snbk], lambda e: e.tensor_copy(out=Snb[:], in_=Sn[:]))
            yield
            fw.op('act', [pOk], [k_('O')], lambda e: e.copy(out=t['O'][:], in_=pO[:, 0:384].rearrange("p (h c) -> p h c", h=6)))
            fw.op('pool', [k_('O')], [k_('sqo')], lambda e: e.tensor_tensor(out=t['sqo'][:], in0=t['O'][:], in1=t['O'][:], op=ALU.mult))
            fw.op('dve', [k_('sqo')], [k_('ssum')], lambda e: e.tensor_reduce(out=t['ssum'][:], in_=t['sqo'][:], axis=AX.X, op=ALU.add))
            fw.op('dve', [k_('ssum')], [k_('ssum')], lambda e: e.tensor_scalar(out=t['ssum'][:], in0=t['ssum'][:], scalar1=1.0 / 64, scalar2=EPS, op0=ALU.mult, op1=ALU.add))
            fw.op('act', [k_('ssum')], [k_('ssum')], lambda e: e.activation(out=t['ssum'][:], in_=t['ssum'][:], func=AF.Sqrt))
            fw.op('dve', [k_('ssum')], [k_('ssum')], lambda e: e.reciprocal(out=t['ssum'][:], in_=t['ssum'][:]))
            fw.op('pool', [k_('O'), k_('ssum')], [k_('Yt')], lambda e: e.tensor_tensor(out=t['Yt'][:], in0=t['O'][:], in1=b3(t['ssum'][:]), op=ALU.mult))
            fw.op('pool', [k_('Yt'), 'gnorm'], [k_('Yt')], lambda e: e.tensor_tensor(out=t['Yt'][:], in0=t['Yt'][:], in1=m3(gnorm[:]), op=ALU.mult))
            fw.op('dve', [k_('Yt'), 'Z0'], ['Y0'], lambda e: e.tensor_tensor(out=Yst[0][:, j, :].rearrange("p (h c) -> p h c", h=6), in0=t['Yt'][:], in1=Zs[0][:, j, :].rearrange("p (h c) -> p h c", h=6), op=ALU.mult))
            if j == 7:
                fw.dma('pool', yr[:, gi * 8:(gi + 1) * 8, 0:384], Yst[0][:], ['Y0'], ['y_a'])

        def run_il(gens):
            gens = list(gens)
            while gens:
                for g_ in list(gens):
                    try:
                        next(g_)
                    except StopIteration:
                        gens.remove(g_)

        def seq(*gens):
            for g_ in gens:
                yield from g_

        load_group(0)
        run_il([prep(0), prep(1)])
        for n in range(0, NCH, 2):
            tasks = [seq(scan(n), scan(n + 1))]
            if n + 2 < NCH:
                if (n + 2) % 8 == 0:
                    load_group((n + 2) // 8)
                tasks += [prep(n + 2), prep(n + 3)]
            run_il(tasks)
        fw.barrier()


def _nsa_consts(T):
    c = {}
    Ns = T // 64
    NcP = T // 16
    Nc = NcP - 1
    WV = 64 + Ns + 1
    cc = np.arange(NcP)
    nn = np.arange(Ns)
    mat = ((cc[:, None] >= 4 * nn[None, :] - 1) & (cc[:, None] <= 4 * nn[None, :] + 3) & (cc[:, None] < Nc)).astype(np.float32)
    v1 = np.zeros((NcP, WV), np.float32)
    v1[:, 64:64 + Ns] = mat
    v1[:, WV - 1] = 1.0
    c['vc1init'] = v1.astype(ml_dtypes.bfloat16)
    cl = np.arange(128)
    q = np.arange(128)
    cm = np.zeros((128, 17, 3, 128), np.float32)
    for idx in range(17):
        ok = (16 * cl[:, None] + 31) <= (128 * idx + q[None, :])
        cm[:, idx, :, :] = np.where(ok, 0.0, NEG)[:, None, :]
    c['cmask'] = cm.reshape(128, 17, 384).astype(ml_dtypes.bfloat16)
    c['causb'] = np.tile(np.where(cl[:, None] <= q[None, :], 0.0, NEG), (1, 3)).astype(ml_dtypes.bfloat16)
    c['bandb'] = np.tile(np.where(cl[:, None] > q[None, :], 0.0, NEG), (1, 3)).astype(ml_dtypes.bfloat16)
    TE = min(T, 4096)
    t = np.arange(TE)
    c['exT'] = (((t[None, :] // 64) % 64) == np.arange(64)[:, None]).astype(np.float32).astype(ml_dtypes.bfloat16)
    return c


def phase_cmp(K, l):
    nc, fw, T = K.nc, K.fw, K.T
    NcP = T // 16
    Ns = T // 64
    WV = 64 + Ns + 1
    with ExitStack() as st:
        sb = lambda name, shape, dt=F32: st.enter_context(nc.sbuf_tensor(name + "_L%d" % l, shape, dt))
        ps = lambda name, shape, dt=F32: st.enter_context(nc.psum_tensor(name + "_L%d" % l, shape, dt))
        w1f = sb("c_w1f", [64, 32, 256])
        w1b = sb("c_w1b", [64, 32, 256], BF16)
        w2f = sb("c_w2f", [128, 2, 64])
        w2b = sb("c_w2b", [128, 2, 64], BF16)
        posf = sb("c_posf", [64, 32])
        posb = sb("c_posb", [64, 32], BF16)
        bias1 = sb("c_bias1", [128, 2])
        XT = sb("c_XT", [64, T + 16], BF16)
        h1T = sb("c_h1T", [128, 2, 512], BF16)
        ones64 = sb("c_ones", [64, 64])
        kg = sb("c_kg", [64, 1])
        sq = sb("c_sq", [64, 512])
        rn = sb("c_rn", [64, 512])
        ko = sb("c_ko", [64, 512], BF16)
        vst = sb("c_vst", [128, WV], BF16)
        pH = [ps("c_pH%d" % i, [128, 512]) for i in range(2)]
        pB = ps("c_pB", [128, 512])
        pK = ps("c_pK", [128, 512])
        pS = ps("c_pS", [128, 512])
        fw.op('pool', [], ['ones64'], lambda e: e.memset(ones64[:], 1.0))
        fw.dma('sp', kg[:], K.w['nsa_k_norm'][l, 0].rearrange("(p o) -> p o", o=1), [], ['kg'])
        for kv in range(2):
            fw.dma('sp', w1f[:], K.w['cmp_w1'][l, kv].rearrange("(l d) j -> d l j", d=64), [], ['w1f'])
            fw.op('dve', ['w1f'], ['w1b'], lambda e: e.tensor_copy(out=w1b[:], in_=w1f[:]))
            fw.dma('sp', w2f[:], K.w['cmp_w2'][l, kv].rearrange("(c p) d -> p c d", p=128), [], ['w2f'])
            fw.op('dve', ['w2f'], ['w2b'], lambda e: e.tensor_copy(out=w2b[:], in_=w2f[:]))
            fw.dma('sp', posf[:], K.w['cmp_pos'][l, kv].rearrange("l d -> d l"), [], ['posf'])
            fw.op('dve', ['posf'], ['posb'], lambda e: e.tensor_copy(out=posb[:], in_=posf[:]))
            for jc in range(2):
                for ll in range(32):
                    fw.op('pe', ['w1b', 'posb'], ['pB'], lambda e: e.matmul(pB[:, jc:jc + 1], lhsT=w1b[:, ll, jc * 128:(jc + 1) * 128], rhs=posb[:, ll:ll + 1], start=(ll == 0), stop=(ll == 31)))
            fw.op('act', ['pB'], ['bias1'], lambda e: e.copy(out=bias1[:], in_=pB[:, 0:2]))
            src = K.kcT if kv == 0 else K.vcT
            for hk in range(2):
                fw.dma('sp', XT[:, 0:T], src[hk * 64:(hk + 1) * 64, :], [], ['XT'])
                fw.op('pool', [], ['XT'], lambda e: e.memset(XT[:, T:T + 16], 0.0))
                for n0 in range(0, NcP, 512):
                    nn = min(512, NcP - n0)
                    for jc in range(2):
                        p_, pk = pH[jc], 'pH%d' % jc
                        for ll in range(32):
                            fw.op('pe', ['w1b', 'XT'], [pk], lambda e: e.matmul(p_[:, 0:nn], lhsT=w1b[:, ll, jc * 128:(jc + 1) * 128],
                                                                                rhs=XT[:, ll + 16 * n0:ll + 16 * (n0 + nn - 1) + 1:16], start=(ll == 0), stop=(ll == 31)))
                        fw.op('act', [pk, 'bias1'], ['h1T'], lambda e: e.activation(out=h1T[:, jc, 0:nn], in_=p_[:, 0:nn], func=AF.Silu, bias=bias1[:, jc:jc + 1]))
                    if kv == 0:
                        for jc in range(2):
                            fw.op('pe', ['h1T', 'w2b'], ['pK'], lambda e: e.matmul(pK[0:64, 0:nn], lhsT=w2b[:, jc, :], rhs=h1T[:, jc, 0:nn], start=(jc == 0), stop=(jc == 1)))
                        fw.op('act', ['pK'], ['sq'], lambda e: e.activation(out=sq[:, 0:nn], in_=pK[0:64, 0:nn], func=AF.Square))
                        fw.op('pe', ['sq', 'ones64'], ['pS'], lambda e: e.matmul(pS[0:64, 0:nn], lhsT=ones64[:], rhs=sq[:, 0:nn], start=True, stop=True))
                        fw.op('dve', ['pS'], ['rn'], lambda e: e.tensor_scalar(out=rn[:, 0:nn], in0=pS[0:64, 0:nn], scalar1=1.0 / 64, scalar2=EPS, op0=ALU.mult, op1=ALU.add))
                        fw.op('act', ['rn'], ['rn'], lambda e: e.activation(out=rn[:, 0:nn], in_=rn[:, 0:nn], func=AF.Sqrt))
                        fw.op('dve', ['rn'], ['rn'], lambda e: e.reciprocal(out=rn[:, 0:nn], in_=rn[:, 0:nn]))
                        fw.op('dve', ['pK', 'rn', 'kg'], ['ko'], lambda e: e.scalar_tensor_tensor(out=ko[:, 0:nn], in0=pK[0:64, 0:nn], scalar=kg[:, 0:1], in1=rn[:, 0:nn], op0=ALU.mult, op1=ALU.mult))
                        fw.dma('pool', K.KcT[hk, :, n0:n0 + nn], ko[:, 0:nn], ['ko'], ['KcT'])
                    else:
                        for c0 in range(0, nn, 128):
                            cn = min(128, nn - c0)
                            for jc in range(2):
                                fw.op('pe', ['h1T', 'w2b'], ['pK'], lambda e: e.matmul(pK[0:cn, 0:64], lhsT=h1T[:, jc, c0:c0 + cn], rhs=w2b[:, jc, :], start=(jc == 0), stop=(jc == 1)))
                            fw.dma('sp', vst[0:cn, :], K.cn['vc1init'][n0 + c0:n0 + c0 + cn, :], [], ['vst'])
                            fw.op('act', ['pK'], ['vst'], lambda e: e.copy(out=vst[0:cn, 0:64], in_=pK[0:cn, 0:64]))
                            fw.dma('pool', K.Vc1[hk, n0 + c0:n0 + c0 + cn, :], vst[0:cn, :], ['vst'], ['Vc1'])
        fw.barrier()


def phase_nsa(K, l):
    nc, fw, T = K.nc, K.fw, K.T
    NcP = T // 16
    Ns = T // 64
    WV = 64 + Ns + 1
    NQ = T // 128
    NCC = (NcP + 127) // 128
    NG = (Ns + 63) // 64
    SW = max(64 + Ns, 128 * 1 if Ns < 64 else 64 + Ns)
    TE = min(T, 4096)
    with ExitStack() as st:
        sb = lambda name, shape, dt=F32: st.enter_context(nc.sbuf_tensor(name + "_L%d" % l, shape, dt))
        ps = lambda name, shape, dt=F32: st.enter_context(nc.psum_tensor(name + "_L%d" % l, shape, dt))
        identb = sb("n_identb", [128, 128], BF16)
        identf = sb("n_identf", [128, 128])
        cmask = sb("n_cmask", [128, 17, 384], BF16)
        causb = sb("n_causb", [128, 384], BF16)
        bandb = sb("n_bandb", [128, 384], BF16)
        KX = sb("n_KX", [128, T], BF16)
        KwT = sb("n_KwT", [64, T], BF16)
        Vs1 = sb("n_Vs1", [128, NQ, 65], BF16)
        Vw1 = sb("n_Vw1", [128, NQ, 65], BF16)
        KcT = sb("n_KcT", [64, NCC * 128], BF16)
        Vc1 = sb("n_Vc1", [128, NCC, WV], BF16)
        QX = [sb("n_QX%d" % i, [128, NG, 3, 128], BF16) for i in range(2)]
        gts = [sb("n_gt%d" % i, [128, 18]) for i in range(2)]
        E = [sb("n_E%d" % i, [128, 384], BF16) for i in range(4)]
        rZ = sb("n_rZ", [128, 3])
        coef = sb("n_coef", [128, 3])
        imp = sb("n_imp", [128, Ns])
        imp2 = sb("n_imp2", [128, Ns])
        m1 = sb("n_m1", [128, 8])
        m2 = sb("n_m2", [128, 8])
        sel = sb("n_sel", [128, Ns])
        nsel = sb("n_nsel", [128, SW], BF16)
        nselT = [sb("n_nselT%d" % i, [128, SW], BF16) for i in range(2)]
        OT = [sb("n_OT%d" % i, [65, 384]) for i in range(2)]
        rd = sb("n_rd", [128, 1])
        Yb = [sb("n_Yb%d" % i, [128, 3, 64]) for i in range(2)]
        pS = [ps("n_pS%d" % i, [128, 512]) for i in range(3)]
        pOC = [ps("n_pOC%d" % i, [128, 512]) for i in range(3)]
        pOs = ps("n_pOs", [128, 512])
        pOw = pOs
        pTn = ps("n_pTn", [128, 128], BF16)

        fw.dma('sp', identb[:], K.c['identb'], [], ['identb'])
        fw.dma('sp', identf[:], K.c['identf'], [], ['identf'])
        fw.dma('sp', cmask[:], K.cn['cmask'], [], ['cmask'])
        fw.dma('sp', causb[:], K.cn['causb'], [], ['causb'])
        fw.dma('sp', bandb[:], K.cn['bandb'], [], ['bandb'])
        fw.op('pool', [], ['nsel'], lambda e: e.memset(nsel[:], 0.0))
        fw.op('pool', [], ['KcT'], lambda e: e.memset(KcT[:], 0.0))
        cnt = [0]
        rc = {}

        def rot(lst, key):
            rc[key] = rc.get(key, -1) + 1
            i = rc[key] % len(lst)
            return lst[i], '%s%d' % (key, i)

        yq = K.y.rearrange("(i p) d -> i p d", p=128)
        gtq = K.gt.rearrange("(i p) d -> i p d", p=128)
        for hk in range(2):
            fw.dma('sp', KX[0:64, :], K.ksT[hk * 64:(hk + 1) * 64, :], [], ['KX'])
            for t0 in range(0, T, TE):
                fw.dma('sp', KX[64:128, t0:t0 + TE], K.cn['exT'], [], ['KX'])
            fw.dma('sp', KwT[:], K.kwT[hk * 64:(hk + 1) * 64, :], [], ['KwT'])
            fw.dma('sp', Vs1[:], K.vs1.rearrange("(c p) w -> p c w", p=128)[:, :, hk * 65:(hk + 1) * 65], [], ['Vs1'])
            fw.dma('sp', Vw1[:], K.vw1.rearrange("(c p) w -> p c w", p=128)[:, :, hk * 65:(hk + 1) * 65], [], ['Vw1'])
            fw.dma('sp', KcT[:, 0:NcP], K.KcT[hk], [], ['KcT'])
            if NcP >= 128:
                fw.dma('sp', Vc1[:], K.Vc1[hk].rearrange("(c p) w -> p c w", p=128), [], ['Vc1'])
            else:
                fw.op('pool', [], ['Vc1'], lambda e: e.memset(Vc1[:], 0.0))
                fw.dma('sp', Vc1[0:NcP, 0, :], K.Vc1[hk], [], ['Vc1'])
            def run_branch(tiles, score, pv, look=2):
                pend = [score(tk) for tk in tiles[:look]]
                for ti, tk in enumerate(tiles):
                    p_, pk = pend.pop(0)
                    if ti + look < len(tiles):
                        pend.append(score(tiles[ti + look]))
                    e_, ek = rot(E, 'E')
                    fw.op('act', [pk], [ek], lambda e: e.activation(out=e_[:], in_=p_[:, 0:384], func=AF.Exp))
                    pv(tk, e_, ek)

            def partA(i):
                par = i % 2
                Q_, Qk = QX[par], 'QX%d' % par
                g_, gk = gts[par], 'gt%d' % par
                Y_, Yk = Yb[par], 'Yb%d' % par
                ngrp = (2 * i + 1) // 64 + 1
                for grp in range(ngrp):
                    fw.dma('sp', Q_[0:64, grp, :, :], K.qn[hk * 192:(hk + 1) * 192, i * 128:(i + 1) * 128].rearrange("(g d) q -> d g q", d=64), [], [Qk])
                fw.dma('sp', g_[:], gtq[i], [], [gk])
                q0 = Q_[0:64, 0, :, :].rearrange("p g q -> p (g q)")
                jmax = (8 * i + 6) // 128

                def cmp_score(jc):
                    p_, pk = rot(pS, 'pS')
                    full = (16 * (128 * jc + 127) + 31) <= 128 * i
                    fw.op('pe', ['KcT', Qk], [pk], lambda e: e.matmul(p_[:, 0:384], lhsT=KcT[:, jc * 128:(jc + 1) * 128], rhs=q0, start=True, stop=full))
                    if not full:
                        idx = (128 * i - 2048 * jc) // 128
                        fw.op('pe', ['identb', 'cmask'], [pk], lambda e: e.matmul(p_[:, 0:384], lhsT=identb[:], rhs=cmask[:, idx, :], start=False, stop=True))
                    return p_, pk

                def cmp_pv(jc, e_, ek):
                    for g in range(3):
                        fw.op('pe', [ek, 'Vc1'], ['pOC%d' % g], lambda e: e.matmul(pOC[g][:, 0:WV], lhsT=e_[:, g * 128:(g + 1) * 128], rhs=Vc1[:, jc, :], start=(jc == 0), stop=(jc == jmax)))

                run_branch(list(range(jmax + 1)), cmp_score, cmp_pv)
                for g in range(3):
                    fw.op('dve', ['pOC%d' % g], ['rZ'], lambda e: e.tensor_scalar(out=rZ[:, g:g + 1], in0=pOC[g][:, WV - 1:WV], scalar1=1e-30, scalar2=None, op0=ALU.max))
                fw.op('dve', ['rZ'], ['rZ'], lambda e: e.reciprocal(out=rZ[:], in_=rZ[:]))
                gv = g_[:, hk * 9:(hk + 1) * 9].rearrange("p (g b) -> p g b", b=3)
                fw.op('dve', ['rZ', gk], ['coef'], lambda e: e.tensor_tensor(out=coef[:], in0=rZ[:], in1=gv[:, :, 0], op=ALU.mult))
                for g in range(3):
                    fw.op('dve', ['pOC%d' % g, 'coef'], [Yk], lambda e: e.tensor_scalar(out=Y_[:, g, :], in0=pOC[g][:, 0:64], scalar1=coef[:, g:g + 1], scalar2=None, op0=ALU.mult))
                    if g == 0:
                        fw.op('dve', ['pOC0', 'rZ'], ['imp'], lambda e: e.tensor_scalar(out=imp[:], in0=pOC[0][:, 64:64 + Ns], scalar1=rZ[:, 0:1], scalar2=None, op0=ALU.mult))
                    else:
                        fw.op('dve', ['pOC%d' % g, 'rZ', 'imp'], ['imp'], lambda e: e.scalar_tensor_tensor(out=imp[:], in0=pOC[g][:, 64:64 + Ns], scalar=rZ[:, g:g + 1], in1=imp[:], op0=ALU.mult, op1=ALU.add))
                if 2 * i + 2 < Ns:
                    fw.op('pool', ['imp'], ['imp'], lambda e: e.memset(imp[:, 2 * i + 2:Ns], -1.0))
                fw.op('pool', ['imp'], ['imp'], lambda e: e.memset(imp[0:64, 2 * i + 1:2 * i + 2], -1.0))
                fw.op('pool', ['imp'], ['imp'], lambda e: e.memset(imp[64:128, 2 * i + 1:2 * i + 2], 1e4))
                fw.op('pool', ['imp'], ['imp'], lambda e: e.memset(imp[:, 2 * i:2 * i + 1], 1e4))
                if i >= 1:
                    fw.op('pool', ['imp'], ['imp'], lambda e: e.memset(imp[0:64, 2 * i - 1:2 * i], 1e4))
                fw.op('pool', ['imp'], ['imp'], lambda e: e.memset(imp[:, 0:1], 1e4))
                fw.op('dve', ['imp'], ['m1'], lambda e: e.max(out=m1[:], in_=imp[:]))
                fw.op('dve', ['imp', 'm1'], ['imp2'], lambda e: e.match_replace(out=imp2[:], in_to_replace=m1[:], in_values=imp[:], imm_value=-2.0))
                fw.op('dve', ['imp2'], ['m2'], lambda e: e.max(out=m2[:], in_=imp2[:]))
                fw.op('dve', ['imp', 'm2'], ['sel'], lambda e: e.tensor_scalar(out=sel[:], in0=imp[:], scalar1=m2[:, 7:8], scalar2=None, op0=ALU.is_ge))
                fw.op('dve', ['sel'], ['nsel'], lambda e: e.tensor_scalar(out=nsel[:, 64:64 + Ns], in0=sel[:], scalar1=-NEG, scalar2=NEG, op0=ALU.mult, op1=ALU.add))
                fw.op('dve', ['nsel'], ['nselT%d' % par], lambda e: e.tensor_copy(out=nselT[par][:], in_=nsel[:]))

            def partA2(i):
                par = i % 2
                Q_, Qk = QX[par], 'QX%d' % par
                ngrp = (2 * i + 1) // 64 + 1
                for grp in range(ngrp):
                    fw.op('pe', ['nselT%d' % par, 'identb'], ['pTn'], lambda e: e.transpose(out=pTn[:, :], in_=nselT[par][:, grp * 64:grp * 64 + 128], identity=identb[:]))
                    fw.op('dve', ['pTn'], [Qk], lambda e: e.tensor_copy(out=Q_[64:128, grp, :, :], in_=pTn[64:128, :].unsqueeze(1).to_broadcast([64, 3, 128])))

            def partB(i):
                par = i % 2
                Q_, Qk = QX[par], 'QX%d' % par
                g_, gk = gts[par], 'gt%d' % par
                Y_, Yk = Yb[par], 'Yb%d' % par
                q0 = Q_[0:64, 0, :, :].rearrange("p g q -> p (g q)")
                gv = g_[:, hk * 9:(hk + 1) * 9].rearrange("p (g b) -> p g b", b=3)
                k0 = max(0, i - 4)

                def win_score(kc):
                    p_, pk = rot(pS, 'pS')
                    msk = causb if kc == i else (bandb if kc == i - 4 else None)
                    fw.op('pe', ['KwT', Qk], [pk], lambda e: e.matmul(p_[:, 0:384], lhsT=KwT[:, kc * 128:(kc + 1) * 128], rhs=q0, start=True, stop=(msk is None)))
                    if msk is not None:
                        fw.op('pe', ['identb', 'causb', 'bandb'], [pk], lambda e: e.matmul(p_[:, 0:384], lhsT=identb[:], rhs=msk[:], start=False, stop=True))
                    return p_, pk

                def win_pv(kc, e_, ek):
                    fw.op('pe', [ek, 'Vw1'], ['pOs'], lambda e: e.matmul(pOs[0:65, 0:384], lhsT=Vw1[:, kc, :], rhs=e_[:], start=(kc == k0), stop=(kc == i)))

                def sel_score(kc):
                    grp = kc // 32
                    p_, pk = rot(pS, 'pS')
                    fw.op('pe', ['KX', Qk], [pk], lambda e: e.matmul(p_[:, 0:384], lhsT=KX[:, kc * 128:(kc + 1) * 128], rhs=Q_[:, grp, :, :].rearrange("p g q -> p (g q)"), start=True, stop=(kc < i)))
                    if kc == i:
                        fw.op('pe', ['identb', 'causb'], [pk], lambda e: e.matmul(p_[:, 0:384], lhsT=identb[:], rhs=causb[:], start=False, stop=True))
                    return p_, pk

                def sel_pv(kc, e_, ek):
                    fw.op('pe', [ek, 'Vs1'], ['pOs'], lambda e: e.matmul(pOs[0:65, 0:384], lhsT=Vs1[:, kc, :], rhs=e_[:], start=(kc == 0), stop=(kc == i)))

                run_branch(list(range(k0, i + 1)), win_score, win_pv)
                fw.op('act', ['pOs'], ['OT0'], lambda e: e.copy(out=OT[0][:], in_=pOs[0:65, 0:384]))
                run_branch(list(range(i + 1)), sel_score, sel_pv)
                fw.op('act', ['pOs'], ['OT1'], lambda e: e.copy(out=OT[1][:], in_=pOs[0:65, 0:384]))
                if i + 1 < NQ:
                    partA2(i + 1)
                for bi in range(2):
                    o_, ok = OT[bi], 'OT%d' % bi
                    for g in range(3):
                        pf, pfk = pOC[g], 'pOC%d' % g
                        fw.op('pe', [ok, 'identf'], [pfk], lambda e: e.transpose(out=pf[:, 0:65], in_=o_[0:65, g * 128:(g + 1) * 128], identity=identf[0:65, 0:65]))
                        fw.op('dve', [pfk], ['rd'], lambda e: e.reciprocal(out=rd[:], in_=pf[:, 64:65]))
                        fw.op('dve', ['rd', gk], ['rd'], lambda e: e.tensor_tensor(out=rd[:], in0=rd[:], in1=gv[:, g, 2 - bi:3 - bi], op=ALU.mult))
                        fw.op('dve', [pfk, 'rd', Yk], [Yk], lambda e: e.scalar_tensor_tensor(out=Y_[:, g, :], in0=pf[:, 0:64], scalar=rd[:, 0:1], in1=Y_[:, g, :], op0=ALU.mult, op1=ALU.add))
                fw.dma('pool', yq[i][:, 384 + hk * 192:384 + (hk + 1) * 192], Y_[:].rearrange("p g d -> p (g d)"), [Yk], ['y_b'])

            partA(0)
            partA2(0)
            for i in range(NQ):
                if i + 1 < NQ:
                    partA(i + 1)
                partB(i)
        fw.barrier()


def phase_p3(K, l):
    nc, fw, T = K.nc, K.fw, K.T
    NM = T // 256
    with ExitStack() as st:
        sb = lambda name, shape, dt=F32: st.enter_context(nc.sbuf_tensor(name + "_L%d" % l, shape, dt))
        ps = lambda name, shape, dt=F32: st.enter_context(nc.psum_tensor(name + "_L%d" % l, shape, dt))
        Wo = sb("f_Wo", [128, 8, 1024], BF16)
        W1 = sb("f_W1", [128, 8, 4096], BF16)
        W2 = sb("f_W2", [128, 32, 1024], BF16)
        wst = [sb("f_wst%d" % i, [128, 1024]) for i in range(2)]
        gf = sb("f_gf", [128, 8])
        identb = sb("f_identb", [128, 128], BF16)
        yt = sb("f_yt", [128, 2, 1024])
        yb = sb("f_yb", [128, 2, 1024], BF16)
        yT = sb("f_yT", [128, 8, 256], BF16)
        xr = [sb("f_xr%d" % i, [128, 2, 1024]) for i in range(1)]
        xb = sb("f_xb", [128, 2, 1024], BF16)
        xnT = sb("f_xnT", [128, 8, 256], BF16)
        hT = sb("f_hT", [128, 32, 256], BF16)
        rl = [sb("f_rl%d" % i, [128, 256]) for i in range(4)]
        junk = sb("f_junk", [128, 1024], BF16)
        ss = sb("f_ss", [128, 2])
        rstd = sb("f_rstd", [128, 2])
        pT = [ps("f_pT%d" % i, [128, 256], BF16) for i in range(2)]
        pA = [ps("f_pA%d" % i, [128, 512]) for i in range(4)]
        pH = [ps("f_pH%d" % i, [128, 512]) for i in range(2)]
        fw.dma('sp', identb[:], K.c['identb'], [], ['identb'])
        fw.dma('sp', gf[:], K.w['norm_ffn'][l].rearrange("(k p) -> p k", p=128), [], ['gf'])
        wi = [0]

        def loadw(dst_ap, src_ap, scal=None):
            i = wi[0] % 2
            wi[0] += 1
            fw.dma('sp' if i else 'pool', wst[i][:], src_ap, [], ['wst%d' % i])
            if scal is None:
                fw.op('dve' if i else 'pool', ['wst%d' % i], ['W'], lambda e: e.tensor_copy(out=dst_ap, in_=wst[i][:]))
            else:
                fw.op('dve' if i else 'pool', ['wst%d' % i, 'gf'], ['W'], lambda e: e.tensor_scalar(out=dst_ap, in0=wst[i][:], scalar1=scal, scalar2=None, op0=ALU.mult))
        for kc in range(8):
            loadw(Wo[:, kc, :], K.w['w_out'][l, kc * 128:(kc + 1) * 128, :])
            for q4 in range(4):
                loadw(W1[:, kc, q4 * 1024:(q4 + 1) * 1024], K.w['w_ffn1'][l, kc * 128:(kc + 1) * 128, q4 * 1024:(q4 + 1) * 1024], gf[:, kc:kc + 1])
        for fc in range(32):
            loadw(W2[:, fc, :], K.w['w_ffn2'][l, fc * 128:(fc + 1) * 128, :])
        ysrc = K.y.rearrange("(m j p) d -> m p j d", p=128, j=2)
        xsrc = K.xin[l].rearrange("(m j p) d -> m p j d", p=128, j=2)
        xdst = K.xout[l].rearrange("(m j p) d -> m p j d", p=128, j=2)
        cnt = [0]
        rc = {}

        def rot(lst, key):
            rc[key] = rc.get(key, -1) + 1
            i = rc[key] % len(lst)
            return lst[i], '%s%d' % (key, i)

        def transposes(src, srck, dst, dstk):
            for kc in range(8):
                p_, pk = pT[kc % 2], 'pT%d' % (kc % 2)
                for j in range(2):
                    fw.op('pe', [srck, 'identb'], [pk], lambda e: e.transpose(out=p_[:, j * 128:(j + 1) * 128], in_=src[:, j, kc * 128:(kc + 1) * 128], identity=identb[:]))
                if kc % 2:
                    fw.op('act', [pk], [dstk], lambda e: e.copy(out=dst[:, kc, :], in_=p_[:]))
                else:
                    fw.op('dve', [pk], [dstk], lambda e: e.tensor_copy(out=dst[:, kc, :], in_=p_[:]))

        for m in range(NM):
            x_, xk = xr[0], 'xr0'
            fw.dma('sp', yt[:], ysrc[m], [], ['yt'])
            fw.dma('sp', x_[:], xsrc[m], [], [xk])
            fw.op('pool', ['yt'], ['yb'], lambda e: e.tensor_copy(out=yb[:], in_=yt[:]))
            transposes(yb, 'yb', yT, 'yT')
            for j in range(2):
                for nh in range(2):
                    p_, pk = rot(pA, 'pA')
                    for kc in range(8):
                        fw.op('pe', ['yT', 'W'], [pk], lambda e: e.matmul(p_[:], lhsT=yT[:, kc, j * 128:(j + 1) * 128], rhs=Wo[:, kc, nh * 512:(nh + 1) * 512], start=(kc == 0), stop=(kc == 7)))
                    fw.op('dve', [pk, xk], [xk], lambda e: e.tensor_tensor(out=x_[:, j, nh * 512:(nh + 1) * 512], in0=x_[:, j, nh * 512:(nh + 1) * 512], in1=p_[:], op=ALU.add))
            fw.op('dve', [], ['ss'], lambda e: e.memset(ss[:], 0.0))
            for j in range(2):
                fw.op('act', [xk, 'ss'], ['junk', 'ss'], lambda e: e.activation(out=junk[:], in_=x_[:, j, :], func=AF.Square, accum_out=ss[:, j:j + 1]))
            fw.op('dve', ['ss'], ['rstd'], lambda e: e.tensor_scalar(out=rstd[:], in0=ss[:], scalar1=1.0 / D, scalar2=EPS, op0=ALU.mult, op1=ALU.add))
            fw.op('act', ['rstd'], ['rstd'], lambda e: e.activation(out=rstd[:], in_=rstd[:], func=AF.Sqrt))
            fw.op('dve', ['rstd'], ['rstd'], lambda e: e.reciprocal(out=rstd[:], in_=rstd[:]))
            for j in range(2):
                fw.op('pool', [xk, 'rstd'], ['xb'], lambda e: e.tensor_scalar(out=xb[:, j, :], in0=x_[:, j, :], scalar1=rstd[:, j:j + 1], scalar2=None, op0=ALU.mult))
            transposes(xb, 'xb', xnT, 'xnT')
            for fc in range(32):
                p_, pk = rot(pH, 'pH')
                for kc in range(8):
                    fw.op('pe', ['xnT', 'W'], [pk], lambda e: e.matmul(p_[:, 0:256], lhsT=W1[:, kc, fc * 128:(fc + 1) * 128], rhs=xnT[:, kc, :], start=(kc == 0), stop=(kc == 7)))
                r_, rk = rot(rl, 'rl')
                fw.op('act', [pk], [rk], lambda e: e.activation(out=r_[:], in_=p_[:, 0:256], func=AF.Relu))
                fw.op('dve', [rk], ['hT%d' % fc], lambda e: e.tensor_tensor(out=hT[:, fc, :], in0=r_[:], in1=r_[:], op=ALU.mult))
            for j in range(2):
                for nh in range(2):
                    p_, pk = rot(pA, 'pA')
                    for fc in range(32):
                        fw.op('pe', ['hT%d' % fc, 'W'], [pk], lambda e: e.matmul(p_[:], lhsT=hT[:, fc, j * 128:(j + 1) * 128], rhs=W2[:, fc, nh * 512:(nh + 1) * 512], start=(fc == 0), stop=(fc == 31)))
                    fw.op('dve', [pk, xk], [xk], lambda e: e.tensor_tensor(out=x_[:, j, nh * 512:(nh + 1) * 512], in0=x_[:, j, nh * 512:(nh + 1) * 512], in1=p_[:], op=ALU.add))
            fw.dma('pool', xdst[m], x_[:], [xk], ['xout'])
        fw.barrier()


def kernel(**inputs):
    x = np.asarray(inputs["x"], dtype=np.float32)
    B, T, _ = x.shape
    nc, K = build(T, L=2)
    base = {n: np.ascontiguousarray(np.asarray(inputs[n], dtype=np.float32)) for n in WNAMES}
    for n, v in K.consts.items():
        base['c_' + n] = v
    for n, v in K.consts_n.items():
        base['cn_' + n] = v
    in_maps = []
    for b in range(B):
        m = dict(base)
        m['x'] = np.ascontiguousarray(x[b])
        in_maps.append(m)
    res = run_bass_kernel_spmd(nc, in_maps, core_ids=list(range(B)))
    return np.stack([np.asarray(r["out"], dtype=np.float32) for r in res.results], axis=0)
```
